# Optimizing a Trainium2 kernel written in Bass

```python
import math
import jax, jax.numpy as jnp
from jax import lax
import numpy as np

D_MODEL = 2048
BATCH = 4
SEQ = 4096
DEPTH = 2

MIX_WIDTH = D_MODEL
DA_HEADS = 8
DA_HEAD_DIM = 64
DA_WIDTH = DA_HEADS * 2 * DA_HEAD_DIM
GDN_HEADS = 8
GDN_HEAD_DIM = 128
GDN_WIDTH = GDN_HEADS * GDN_HEAD_DIM
CONV_K = 4
GDN_CHUNK = 64
Q_BLOCK = 128
REL_BUCKETS = 32
REL_MAX_DIST = 128
RMS_EPS = 1e-6
NEG_INF = -1e30
IN_COLS = 4 * DA_WIDTH + 4 * GDN_WIDTH + 2 * GDN_HEADS

kernel_name = "hymba_diffattn_gdn_hybrid"


def rms_norm(x, g):
    xf = x.astype(jnp.float32)
    y = xf * lax.rsqrt(jnp.mean(xf * xf, axis=-1, keepdims=True) + RMS_EPS)
    return (y * g.astype(jnp.float32)).astype(x.dtype)


def l2_normalize(x):
    return x * lax.rsqrt(jnp.sum(x * x, axis=-1, keepdims=True) + RMS_EPS)


def t5_causal_bucket(dist):
    n = jnp.maximum(dist, 0)
    max_exact = REL_BUCKETS // 2
    large = max_exact + (jnp.log(jnp.maximum(n, max_exact).astype(jnp.float32) / max_exact)
                         / math.log(REL_MAX_DIST / max_exact)
                         * (REL_BUCKETS - max_exact)).astype(jnp.int32)
    large = jnp.minimum(large, REL_BUCKETS - 1)
    return jnp.where(n < max_exact, n, large)


def differential_attention(q1, q2, k1, k2, v, lam, rel_bias):
    B, H, T, _ = q1.shape
    n_blk = T // Q_BLOCK
    scale = DA_HEAD_DIM ** -0.5
    kpos = jnp.arange(T)

    def block(i):
        qs = i * Q_BLOCK
        q1b = lax.dynamic_slice_in_dim(q1, qs, Q_BLOCK, axis=2)
        q2b = lax.dynamic_slice_in_dim(q2, qs, Q_BLOCK, axis=2)
        dist = (qs + jnp.arange(Q_BLOCK))[:, None] - kpos[None, :]
        bias = jnp.transpose(rel_bias.astype(jnp.float32)[t5_causal_bucket(dist)], (2, 0, 1))
        causal = dist >= 0
        s1 = jnp.einsum('bhqd,bhkd->bhqk', q1b, k1).astype(jnp.float32) * scale + bias
        s2 = jnp.einsum('bhqd,bhkd->bhqk', q2b, k2).astype(jnp.float32) * scale + bias
        p1 = jax.nn.softmax(jnp.where(causal, s1, NEG_INF), axis=-1)
        p2 = jax.nn.softmax(jnp.where(causal, s2, NEG_INF), axis=-1)
        attn = p1 - lam * p2
        return jnp.einsum('bhqk,bhkv->bhqv', attn.astype(v.dtype), v)

    out = lax.map(block, jnp.arange(n_blk))
    return jnp.transpose(out, (1, 2, 0, 3, 4)).reshape(B, H, T, v.shape[-1])


def causal_depthwise_conv(x, w):
    T = x.shape[1]
    xp = jnp.pad(x, ((0, 0), (CONV_K - 1, 0), (0, 0)))
    y = xp[:, 0:T] * w[0]
    for j in range(1, CONV_K):
        y = y + xp[:, j:j + T] * w[j]
    return y


def gated_delta_rule_chunked(q, k, v, g, beta):
    B, H, T, dk = q.shape
    dv = v.shape[-1]
    C = GDN_CHUNK
    N = T // C
    q = q.reshape(B, H, N, C, dk)
    k = k.reshape(B, H, N, C, dk)
    v = v.reshape(B, H, N, C, dv)
    beta = beta.reshape(B, H, N, C)
    g = jnp.cumsum(g.reshape(B, H, N, C), axis=-1)
    tril = jnp.tril(jnp.ones((C, C), dtype=bool))
    strict = jnp.tril(jnp.ones((C, C), dtype=bool), -1)
    diff = g[..., :, None] - g[..., None, :]
    decay = jnp.where(tril, jnp.exp(jnp.where(tril, diff, 0.0)), 0.0)
    kb = k * beta[..., None]
    vb = v * beta[..., None]
    L = jnp.where(strict, jnp.einsum('bhnid,bhnjd->bhnij', kb, k) * decay, 0.0)
    A = L + jnp.eye(C, dtype=jnp.float32)
    rhs = jnp.concatenate([vb, kb * jnp.exp(g)[..., None]], axis=-1)
    sol = lax.linalg.triangular_solve(A, rhs, left_side=True, lower=True)
    u, w = sol[..., :dv], sol[..., dv:]
    qk = jnp.where(tril, jnp.einsum('bhnid,bhnjd->bhnij', q, k) * decay, 0.0)
    g_last = g[..., -1]

    def step(S, inp):
        qc, kc, uc, wc, qkc, gc, glc = inp
        v_new = uc - jnp.einsum('bhcd,bhdv->bhcv', wc, S)
        o = (jnp.einsum('bhcd,bhdv->bhcv', qc * jnp.exp(gc)[..., None], S)
             + jnp.einsum('bhij,bhjv->bhiv', qkc, v_new))
        S = (S * jnp.exp(glc)[..., None, None]
             + jnp.einsum('bhcd,bhcv->bhdv', kc * jnp.exp(glc[..., None] - gc)[..., None], v_new))
        return S, o

    mv = lambda t: jnp.moveaxis(t, 2, 0)
    S0 = jnp.zeros((B, H, dk, dv), jnp.float32)
    _, o = lax.scan(step, S0, (mv(q), mv(k), mv(u), mv(w), mv(qk), mv(g), mv(g_last)))
    return jnp.moveaxis(o, 0, 2).reshape(B, H, T, dv)


def hybrid_layer(x, norm_w, w_in, w_out, lq1, lk1, lq2, lk2, subln_w, rel_bias,
                 conv_w, a_log, dt_bias, gdn_norm_w, lambda_init):
    B, T, _ = x.shape
    h = rms_norm(x, norm_w)
    proj = jnp.einsum('btd,dc->btc', h, w_in)

    def heads_pair(t):
        t = t.reshape(B, T, DA_HEADS, 2, DA_HEAD_DIM)
        return jnp.transpose(t[..., 0, :], (0, 2, 1, 3)), jnp.transpose(t[..., 1, :], (0, 2, 1, 3))
    q1, q2 = heads_pair(proj[..., 0:DA_WIDTH])
    k1, k2 = heads_pair(proj[..., DA_WIDTH:2 * DA_WIDTH])
    v_da = jnp.transpose(proj[..., 2 * DA_WIDTH:3 * DA_WIDTH].reshape(B, T, DA_HEADS, 2 * DA_HEAD_DIM), (0, 2, 1, 3))
    gate_da = proj[..., 3 * DA_WIDTH:4 * DA_WIDTH]
    lam = (jnp.exp(jnp.sum(lq1.astype(jnp.float32) * lk1.astype(jnp.float32)))
           - jnp.exp(jnp.sum(lq2.astype(jnp.float32) * lk2.astype(jnp.float32)))
           + lambda_init)
    o_da = differential_attention(q1, q2, k1, k2, v_da, lam, rel_bias)
    o_da = rms_norm(o_da, subln_w) * (1.0 - lambda_init)
    o_da = jnp.transpose(o_da, (0, 2, 1, 3)).reshape(B, T, DA_WIDTH)
    o_da = (o_da * jax.nn.silu(gate_da)).astype(x.dtype)

    off = 4 * DA_WIDTH
    qkv = proj[..., off:off + 3 * GDN_WIDTH]
    z = proj[..., off + 3 * GDN_WIDTH:off + 4 * GDN_WIDTH]
    b_raw = proj[..., off + 4 * GDN_WIDTH:off + 4 * GDN_WIDTH + GDN_HEADS]
    a_raw = proj[..., off + 4 * GDN_WIDTH + GDN_HEADS:off + 4 * GDN_WIDTH + 2 * GDN_HEADS]
    qkv = jax.nn.silu(causal_depthwise_conv(qkv, conv_w)).astype(jnp.float32)
    to_heads = lambda t: jnp.transpose(t.reshape(B, T, GDN_HEADS, GDN_HEAD_DIM), (0, 2, 1, 3))
    q = l2_normalize(to_heads(qkv[..., 0:GDN_WIDTH])) * (GDN_HEAD_DIM ** -0.5)
    k = l2_normalize(to_heads(qkv[..., GDN_WIDTH:2 * GDN_WIDTH]))
    v = to_heads(qkv[..., 2 * GDN_WIDTH:3 * GDN_WIDTH])
    beta = jnp.transpose(jax.nn.sigmoid(b_raw.astype(jnp.float32)), (0, 2, 1))
    g = -jnp.exp(a_log.astype(jnp.float32)) * jax.nn.softplus(a_raw.astype(jnp.float32) + dt_bias.astype(jnp.float32))
    g = jnp.transpose(g, (0, 2, 1))
    o_gdn = gated_delta_rule_chunked(q, k, v, g, beta)
    o_gdn = rms_norm(jnp.transpose(o_gdn, (0, 2, 1, 3)), gdn_norm_w)
    o_gdn = o_gdn * jax.nn.silu(z.astype(jnp.float32)).reshape(B, T, GDN_HEADS, GDN_HEAD_DIM)
    o_gdn = o_gdn.reshape(B, T, GDN_WIDTH).astype(x.dtype)

    mixed = jnp.concatenate([o_da, o_gdn], axis=-1)
    return x + jnp.einsum('btc,cd->btd', mixed, w_out).astype(x.dtype)


def setup_inputs(seed: int = 0) -> dict:
    key = jax.random.key(seed)
    ks = jax.random.split(key, 16)
    f32 = jnp.float32
    x = jax.random.normal(ks[0], (BATCH, SEQ, D_MODEL), f32)
    norm_w = 1.0 + 0.02 * jax.random.normal(ks[1], (DEPTH, D_MODEL), f32)
    w_in = jax.random.normal(ks[2], (DEPTH, D_MODEL, IN_COLS), f32) * (D_MODEL ** -0.5)
    w_out = jax.random.normal(ks[3], (DEPTH, MIX_WIDTH, D_MODEL), f32) * (MIX_WIDTH ** -0.5)
    lambda_q1 = 0.1 * jax.random.normal(ks[4], (DEPTH, DA_HEAD_DIM), f32)
    lambda_k1 = 0.1 * jax.random.normal(ks[5], (DEPTH, DA_HEAD_DIM), f32)
    lambda_q2 = 0.1 * jax.random.normal(ks[6], (DEPTH, DA_HEAD_DIM), f32)
    lambda_k2 = 0.1 * jax.random.normal(ks[7], (DEPTH, DA_HEAD_DIM), f32)
    da_subln_w = 1.0 + 0.02 * jax.random.normal(ks[8], (DEPTH, 2 * DA_HEAD_DIM), f32)
    rel_bias = 0.5 * jax.random.normal(ks[9], (REL_BUCKETS, DA_HEADS), f32)
    conv_w = jax.random.normal(ks[10], (DEPTH, CONV_K, 3 * GDN_WIDTH), f32) * (CONV_K ** -0.5)
    a_log = jnp.log(jax.random.uniform(ks[11], (DEPTH, GDN_HEADS), f32, 1.0, 16.0))
    dt_bias = 0.1 * jax.random.normal(ks[12], (DEPTH, GDN_HEADS), f32)
    gdn_norm_w = 1.0 + 0.02 * jax.random.normal(ks[13], (DEPTH, GDN_HEAD_DIM), f32)
    final_norm_w = 1.0 + 0.02 * jax.random.normal(ks[14], (D_MODEL,), f32)
    return {"x": x, "norm_w": norm_w, "w_in": w_in, "w_out": w_out,
            "lambda_q1": lambda_q1, "lambda_k1": lambda_k1, "lambda_q2": lambda_q2, "lambda_k2": lambda_k2,
            "da_subln_w": da_subln_w, "rel_bias": rel_bias, "conv_w": conv_w, "a_log": a_log,
            "dt_bias": dt_bias, "gdn_norm_w": gdn_norm_w, "final_norm_w": final_norm_w}


def reference(x, norm_w, w_in, w_out, lambda_q1, lambda_k1, lambda_q2, lambda_k2,
              da_subln_w, rel_bias, conv_w, a_log, dt_bias, gdn_norm_w, final_norm_w):
    for l in range(DEPTH):
        lambda_init = 0.8 - 0.6 * math.exp(-0.3 * l)
        x = hybrid_layer(x, norm_w[l], w_in[l], w_out[l],
                         lambda_q1[l], lambda_k1[l], lambda_q2[l], lambda_k2[l],
                         da_subln_w[l], rel_bias, conv_w[l], a_log[l], dt_bias[l], gdn_norm_w[l],
                         lambda_init)
    return rms_norm(x, final_norm_w)
```

```python
import math
import contextlib
import numpy as np
import ml_dtypes
import concourse.bass as bass
import concourse.mybir as mybir
from concourse.bass_utils import run_bass_kernel_spmd

F32 = mybir.dt.float32
BF16 = mybir.dt.bfloat16
AF = mybir.ActivationFunctionType
ALU = mybir.AluOpType

D = 2048
SEQ = 4096
NB = 4
DEPTH = 2
NCOL = 4104
NT = SEQ // 128
RMS_EPS = 1e-6
SCALE = 0.125
NEG = -30000.0

SEM_EPOCH = 24000
N_LANES = 8


class T:
    __slots__ = ("name", "lw", "rd", "excl")

    def __init__(self, name="", excl=False):
        self.name = name
        self.lw = None
        self.rd = []
        self.excl = excl


def TP():
    return T(excl=True)


class Ctx:
    def __init__(self, nc, es):
        self.nc = nc
        self.es = es
        self.engs = {"pe": nc.tensor, "act": nc.scalar, "dve": nc.vector,
                     "pool": nc.gpsimd, "sp": nc.sync}
        self.sems = {}
        self.cur = {}
        self.epoch = {e: 0 for e in self.engs}
        self.seen = {e: {} for e in self.engs}
        for e in self.engs:
            self._new_epoch(e)
        self.lanes = {}
        self.lane_rr = {}
        self.n_wait = 0
        self.n_ins = 0
        self.uid = 0

    def _mksem(self, key):
        h = self.es.enter_context(self.nc.semaphore(key))
        self.sems[key] = h
        return h

    def _new_epoch(self, e):
        key = f"s_{e}_{self.epoch[e]}"
        self.epoch[e] += 1
        self._mksem(key)
        self.cur[e] = [key, 0]

    def _lanes(self, q):
        if q not in self.lanes:
            self.lanes[q] = []
            for i in range(N_LANES):
                key = f"l_{q}_{i}"
                self._mksem(key)
                self.lanes[q].append([key, 0])
            self.lane_rr[q] = 0
        return self.lanes[q]

    def _wait(self, e, ev):
        if ev is None:
            return
        key, val = ev
        if e == "pe" and key.startswith("s_pe_"):
            return
        if self.seen[e].get(key, 0) >= val:
            return
        self.engs[e].wait_ge(self.sems[key], val)
        self.seen[e][key] = val
        self.n_wait += 1

    def _deps(self, e, reads, writes):
        for t in reads:
            self._wait(e, t.lw)
            if t.excl:
                for ev in t.rd:
                    if ev[0].split("_")[1] != e:
                        self._wait(e, ev)
        for t in writes:
            self._wait(e, t.lw)
            for ev in t.rd:
                self._wait(e, ev)

    def _record(self, ev, reads, writes):
        for t in reads:
            t.rd = [r for r in t.rd if r[0] != ev[0]] + [ev]
        for t in writes:
            t.lw = ev
            t.rd = []

    def op(self, e, fn, reads=(), writes=()):
        self._deps(e, reads, writes)
        ins = fn(self.engs[e])
        c = self.cur[e]
        c[1] += 1
        ins.then_inc(self.sems[c[0]], 1)
        ev = (c[0], c[1])
        self._record(ev, reads, writes)
        self.n_ins += 1
        if c[1] >= SEM_EPOCH:
            self._new_epoch(e)
        return ev

    def dma(self, q, out, in_, reads=(), writes=(), **kw):
        lanes = self._lanes(q)
        i = self.lane_rr[q]
        self.lane_rr[q] = (i + 1) % N_LANES
        lane = lanes[i]
        if lane[1] > 0:
            self._wait(q, (lane[0], lane[1]))
        self._deps(q, reads, writes)
        ins = self.engs[q].dma_start(out=out, in_=in_, **kw)
        lane[1] += 16
        ins.then_inc(self.sems[lane[0]], 16)
        ev = (lane[0], lane[1])
        self._record(ev, reads, writes)
        self.n_ins += 1
        return ev

    def collective(self, fn, reads=(), writes=()):
        if "cc" not in self.sems:
            self._mksem("cc")
            self.cc_count = 0
        self._deps("pool", reads, writes)
        ins = fn(self.engs["pool"])
        self.cc_count += 1
        ins.then_inc(self.sems["cc"])
        ev = ("cc", self.cc_count)
        self._record(ev, reads, writes)
        self.n_ins += 1
        return ev

    def all_events(self):
        evs = []
        for e in self.engs:
            for ep in range(self.epoch[e]):
                key = f"s_{e}_{ep}"
                val = self.cur[e][1] if key == self.cur[e][0] else SEM_EPOCH
                if val > 0:
                    evs.append((key, val))
        for q, lanes in self.lanes.items():
            for lane in lanes:
                if lane[1] > 0:
                    evs.append((lane[0], lane[1]))
        if "cc" in self.sems and self.cc_count > 0:
            evs.append(("cc", self.cc_count))
        return evs

    def barrier(self):
        evs = self.all_events()
        for e in self.engs:
            for ev in evs:
                if ev[0] == self.cur[e][0]:
                    continue
                self._wait(e, ev)

    def finish(self, e="sp"):
        for ev in self.all_events():
            if ev[0] == self.cur[e][0]:
                continue
            self._wait(e, ev)


class Ring:
    def __init__(self, items):
        self.items = items
        self.i = 0

    def next(self):
        it = self.items[self.i]
        self.i = (self.i + 1) % len(self.items)
        return it


def _alt(i):
    return "act" if (i % 2) else "dve"


def evac_copy(c, eng, out, in_, reads, writes):
    if eng == "act":
        return c.op("act", lambda e: e.copy(out=out, in_=in_), reads=reads, writes=writes)
    return c.op(eng, lambda e: e.tensor_copy(out=out, in_=in_), reads=reads, writes=writes)


def evac_scale(c, eng, out, in_, sc, reads, writes):
    if eng == "act":
        return c.op("act", lambda e: e.mul(out=out, in_=in_, mul=sc), reads=reads, writes=writes)
    return c.op("dve", lambda e: e.tensor_scalar(out=out, in0=in_, scalar1=sc, scalar2=None, op0=ALU.mult),
                reads=reads, writes=writes)


def rsqrt_chain(c, out, in_, scale, epsb, tmp, reads, writes, tmpT):
    c.op("act", lambda e: e.activation(out=tmp, in_=in_, func=AF.Ln, bias=epsb, scale=scale),
         reads=reads, writes=[tmpT])
    c.op("act", lambda e: e.activation(out=out, in_=tmp, func=AF.Exp, scale=-0.5),
         reads=[tmpT], writes=writes)


def stage_inproj(c, nc, G, l, x_src, Tx_src):
    Wbf = G["Wbf"][l]
    TW = G["TWbf"][l]
    with contextlib.ExitStack() as es:
        sb = lambda n, s, d: es.enter_context(nc.sbuf_tensor(n, s, d))
        ps = lambda n, s, d: es.enter_context(nc.psum_tensor(n, s, d))
        xt = [(sb(f"ip_x{i}", [128, D], F32), T()) for i in range(3)]
        junk = sb("ip_junk", [128, D], BF16)
        Tjunk = T()
        st = sb("ip_st", [128, NT, 4], F32)
        Tst = [T() for _ in range(NT)]
        xn = [(sb(f"ip_xn{i}", [128, 4, D], BF16), T()) for i in range(2)]
        hT = [(sb(f"ip_hT{i}", [128, 16, 512], BF16), T()) for i in range(2)]
        wb = Ring([(sb(f"ip_w{i}", [128, 16, 512], BF16), T()) for i in range(3)])
        wl = [(sb(f"ip_wl{i}", [128, 16, 8], BF16), T()) for i in range(2)]
        sg_b = Ring([(sb(f"ip_sb{i}", [128, 512], BF16), T()) for i in range(4)])
        sg_f = Ring([(sb(f"ip_sf{i}", [128, 512], F32), T()) for i in range(4)])
        trp = Ring([(ps(f"ip_tr{i}", [128, 1024], BF16)[:, 0:512], TP()) for i in range(2)])
        mmp = Ring([(ps(f"ip_mm{i}", [128, 512], F32), TP()) for i in range(6)])
        gpk, Tgpk = G["gpk"][l]
        identb, Tidb = G["identb"]
        epsb = G["cst"][0][:, 0:1]
        Tcst = G["cst"][1]

        c.op("pool", lambda e: e.memset(st[:], 0.0), writes=Tst)

        def load_x(blk):
            for j in range(4):
                t = blk * 4 + j
                xa, Txa = xt[t % 3]
                c.dma("sp", xa[:], x_src[t * 128:(t + 1) * 128, :], reads=[Tx_src], writes=[Txa])

        def load_w(cb):
            if cb < 8:
                w, Tw = wb.next()
                c.dma("sp", w[:], Wbf[:, :, cb * 512:(cb + 1) * 512], reads=[TW], writes=[Tw])
            else:
                w, Tw = wl[load_w.n8 % 2]
                load_w.n8 += 1
                c.dma("sp", w[:], Wbf[:, :, 4096:4104], reads=[TW], writes=[Tw])
            return w, Tw
        load_w.n8 = 0

        nev = 0
        for blk in range(8):
            xnb, Txn = xn[blk % 2]
            hTb, ThT = hT[blk % 2]
            for j in range(4):
                t = blk * 4 + j
                xa, Txa = xt[t % 3]
                c.dma("sp", xa[:], x_src[t * 128:(t + 1) * 128, :], reads=[Tx_src], writes=[Txa])
                c.op("act", lambda e: e.activation(out=junk[:], in_=xa[:], func=AF.Square,
                                                   accum_out=st[:, t, 0:1]),
                     reads=[Txa], writes=[Tjunk, Tst[t]])
                rsqrt_chain(c, st[:, t, 2:3], st[:, t, 0:1], 1.0 / D, epsb, st[:, t, 1:2],
                            [Tst[t], Tcst], [Tst[t]], Tst[t])
                c.op("dve", lambda e: e.tensor_scalar(out=xnb[:, j, :], in0=xa[:], scalar1=st[:, t, 2:3],
                                                      scalar2=None, op0=ALU.mult),
                     reads=[Txa, Tst[t]], writes=[Txn])
            for k in range(16):
                p, Tp = trp.next()
                for j in range(4):
                    c.op("pe", lambda e: e.transpose(p[:, j * 128:(j + 1) * 128],
                                                     xnb[:, j, k * 128:(k + 1) * 128], identb[:]),
                         reads=[Txn, Tidb], writes=[Tp])
                evac_scale(c, _alt(k), hTb[:, k, :], p[:], gpk[:, k:k + 1], [Tp, Tgpk], [ThT])
            nxt = load_w(0)
            for cb in range(9):
                w, Tw = nxt
                if cb + 1 < 9:
                    nxt = load_w(cb + 1)
                if cb < 5:
                    for m in range(4):
                        p, Tp = mmp.next()
                        for k in range(16):
                            c.op("pe", lambda e: e.matmul(p[:], lhsT=w[:, k, m * 128:(m + 1) * 128],
                                                          rhs=hTb[:, k, :], start=(k == 0), stop=(k == 15)),
                                 reads=[Tw, ThT], writes=[Tp])
                        if cb < 2:
                            s, Ts = sg_b.next()
                            dst = G["QKT"][0][cb * 4 + m, :, blk * 512:(blk + 1) * 512]
                            Tdst = G["QKT"][1]
                        else:
                            s, Ts = sg_f.next()
                            dst = G["GQKV"][0][(cb - 2) * 4 + m, :, blk * 512:(blk + 1) * 512]
                            Tdst = G["GQKV"][1]
                        evac_copy(c, _alt(nev), s[:], p[:], [Tp], [Ts])
                        nev += 1
                        c.dma("pool", dst, s[:], reads=[Ts], writes=[Tdst])
                elif cb < 8:
                    for j in range(4):
                        t = blk * 4 + j
                        p, Tp = mmp.next()
                        for k in range(16):
                            c.op("pe", lambda e: e.matmul(p[:], lhsT=hTb[:, k, j * 128:(j + 1) * 128],
                                                          rhs=w[:, k, :], start=(k == 0), stop=(k == 15)),
                                 reads=[Tw, ThT], writes=[Tp])
                        if cb == 5:
                            s, Ts = sg_b.next()
                            dst, Tdst = G["DAV"][0][t * 128:(t + 1) * 128, :], G["DAV"][1]
                        elif cb == 6:
                            s, Ts = sg_f.next()
                            dst, Tdst = G["DAG"][0][t * 128:(t + 1) * 128, :], G["DAG"][1]
                        else:
                            s, Ts = sg_f.next()
                            dst, Tdst = G["GZ"][0][t * 128:(t + 1) * 128, :], G["GZ"][1]
                        evac_copy(c, _alt(nev), s[:], p[:], [Tp], [Ts])
                        nev += 1
                        c.dma("pool", dst, s[:], reads=[Ts], writes=[Tdst])
                else:
                    for j in range(4):
                        t = blk * 4 + j
                        p, Tp = mmp.next()
                        for k in range(16):
                            c.op("pe", lambda e: e.matmul(p[:, 0:8], lhsT=hTb[:, k, j * 128:(j + 1) * 128],
                                                          rhs=w[:, k, :], start=(k == 0), stop=(k == 15)),
                                 reads=[Tw, ThT], writes=[Tp])
                        s, Ts = sg_f.next()
                        evac_copy(c, _alt(nev), s[:, 0:8], p[:, 0:8], [Tp], [Ts])
                        nev += 1
                        c.dma("pool", G["GBA"][0][t * 128:(t + 1) * 128, :], s[:, 0:8], reads=[Ts],
                              writes=[G["GBA"][1]])
        c.barrier()


def stage_outproj(c, nc, G, l, x_src, Tx_src, mixf, Tmixf, dst, Tdst, tok0, ntile, final):
    Wo = G["Wobf"][l]
    TWo = G["TWobf"][l]
    with contextlib.ExitStack() as es:
        sb = lambda n, s, d: es.enter_context(nc.sbuf_tensor(n, s, d))
        ps = lambda n, s, d: es.enter_context(nc.psum_tensor(n, s, d))
        wo = sb("op_w", [128, 16, D], BF16)
        Two = T()
        mt = [(sb(f"op_m{i}", [128, D], BF16), T()) for i in range(2)]
        mT = [(sb(f"op_mT{i}", [128, 16, 128], BF16), T()) for i in range(2)]
        xt = [(sb(f"op_x{i}", [128, D], F32), T()) for i in range(2)]
        ot = [(sb(f"op_o{i}", [128, D], F32), T()) for i in range(2)]
        junk = sb("op_junk", [128, D], BF16)
        Tjunk = T()
        st = sb("op_st", [128, NT, 4], F32)
        Tst = [T() for _ in range(NT)]
        fnw = sb("op_fnw", [128, D], F32)
        Tfnw = T()
        trp = Ring([(ps(f"op_tr{i}", [128, 1024], BF16)[:, 0:512], TP()) for i in range(2)])
        mmp = Ring([(ps(f"op_mm{i}", [128, 512], F32), TP()) for i in range(6)])
        identb, Tidb = G["identb"]
        epsb = G["cst"][0][:, 0:1]
        Tcst = G["cst"][1]
        for k4 in range(4):
            c.dma("sp", wo[:, k4 * 4:(k4 + 1) * 4, :], Wo[:, k4 * 4:(k4 + 1) * 4, :], reads=[TWo], writes=[Two])
        if final:
            c.dma("sp", fnw[:], G["fnw_d"][0:1, :].to_broadcast([128, D]), writes=[Tfnw])
            c.op("pool", lambda e: e.memset(st[:], 0.0), writes=Tst)
        nev = 0
        for i in range(ntile):
            tok = tok0 + i * 128
            m, Tm = mt[i % 2]
            mTt, TmT = mT[i % 2]
            xa, Txa = xt[i % 2]
            o, To = ot[i % 2]
            for r in range(2):
                c.dma("sp", m[:, r * 1024:(r + 1) * 1024], mixf[r, tok:tok + 128, :], reads=[Tmixf], writes=[Tm])
            c.dma("sp", xa[:], x_src[tok:tok + 128, :], reads=[Tx_src], writes=[Txa])
            for k4 in range(4):
                p, Tp = trp.next()
                for j in range(4):
                    k = k4 * 4 + j
                    c.op("pe", lambda e: e.transpose(p[:, j * 128:(j + 1) * 128], m[:, k * 128:(k + 1) * 128],
                                                     identb[:]),
                         reads=[Tm, Tidb], writes=[Tp])
                evac_copy(c, _alt(k4), mTt[:, k4 * 4:(k4 + 1) * 4, :],
                          p[:].rearrange("p (j t) -> p j t", j=4), [Tp], [TmT])
            for nb in range(4):
                p, Tp = mmp.next()
                for k in range(16):
                    c.op("pe", lambda e: e.matmul(p[:], lhsT=mTt[:, k, :], rhs=wo[:, k, nb * 512:(nb + 1) * 512],
                                                  start=(k == 0), stop=(k == 15)),
                         reads=[TmT, Two], writes=[Tp])
                c.op("dve", lambda e: e.tensor_tensor(out=o[:, nb * 512:(nb + 1) * 512], in0=p[:],
                                                      in1=xa[:, nb * 512:(nb + 1) * 512], op=ALU.add),
                     reads=[Tp, Txa], writes=[To])
            if not final:
                c.dma("pool", dst[tok:tok + 128, :], o[:], reads=[To], writes=[Tdst])
            else:
                c.op("act", lambda e: e.activation(out=junk[:], in_=o[:], func=AF.Square,
                                                   accum_out=st[:, i, 0:1]),
                     reads=[To], writes=[Tjunk, Tst[i]])
                rsqrt_chain(c, st[:, i, 2:3], st[:, i, 0:1], 1.0 / D, epsb, st[:, i, 1:2],
                            [Tst[i], Tcst], [Tst[i]], Tst[i])
                c.op("dve", lambda e: e.scalar_tensor_tensor(out=xa[:], in0=o[:], scalar=st[:, i, 2:3],
                                                             in1=fnw[:], op0=ALU.mult, op1=ALU.mult),
                     reads=[To, Tst[i], Tfnw], writes=[Txa])
                c.dma("pool", dst[i * 128:(i + 1) * 128, :], xa[:], reads=[Txa], writes=[Tdst])
        c.barrier()


def stage_da(c, nc, G, l):
    lam_init = 0.8 - 0.6 * math.exp(-0.3 * l)
    QKT, TQKT = G["QKT"]
    DAV, TDAV = G["DAV"]
    DAG, TDAG = G["DAG"]
    MIXH, TMIXH = G["MIXH"]
    DAVr = DAV.rearrange("(kb p) c -> p kb c", p=128)
    DAGr = DAG.rearrange("(j p) c -> p j c", p=128)
    MIXr = MIXH.rearrange("(j p) c -> p j c", p=128)
    with contextlib.ExitStack() as es:
        sb = lambda n, s, d: es.enter_context(nc.sbuf_tensor(n, s, d))
        ps = lambda n, s, d: es.enter_context(nc.psum_tensor(n, s, d))
        KT = [(sb(f"da_kt{i}", [128, SEQ], BF16), T()) for i in range(2)]
        QT = [(sb(f"da_qt{i}", [128, SEQ], BF16), T()) for i in range(2)]
        V = [(sb(f"da_v{i}", [128, NT, 129], BF16), T()) for i in range(2)]
        braw = sb("da_braw", [128, 5, 512], F32)
        Tbraw = T()
        mpat = sb("da_mpat", [128, 5, 512], F32)
        Tmpat = T()
        biasT = [(sb(f"da_bias{i}", [128, 5, 512], BF16), T()) for i in range(2)]
        ering = Ring([(sb(f"da_e{i}", [128, 512], BF16), T()) for i in range(4)])
        om = [(sb(f"da_om{i}", [128, 4, 128], F32), T()) for i in range(2)]
        rec = [(sb(f"da_rec{i}", [128, 4], F32), T()) for i in range(2)]
        dlt = sb("da_dlt", [128, 4, 128], F32)
        Tdlt = T()
        junk = sb("da_junk", [128, 128], BF16)
        Tjunk = T()
        ss = sb("da_ss", [128, 8, 4], F32)
        Tss = T()
        gate = [(sb(f"da_g{i}", [128, 4, 128], F32), T()) for i in range(2)]
        gm = sb("da_gm", [128, 4, 128], F32)
        Tgm = T()
        tmp = sb("da_tmp", [128, 4, 128], F32)
        Ttmp = T()
        fin = [(sb(f"da_fin{i}", [128, 4, 128], BF16), T()) for i in range(2)]
        wsub = sb("da_wsub", [128, 4, 128], F32)
        Twsub = T()
        lamv = sb("da_lamv", [128, 4, 64], F32)
        Tlamv = T()
        lamp = sb("da_lamp", [128, 2, 64], F32)
        lams = sb("da_lams", [128, 8], F32)
        Tlams = T()
        sring = Ring([(ps(f"da_s{i}", [128, 512], F32), TP()) for i in range(3)])
        Oacc = [[(ps(f"da_o{m}{b}", [128, 512], F32)[:, 0:258].rearrange("p (a b) -> p a b", a=2), TP())
                 for b in range(2)] for m in range(2)]
        identb, Tidb = G["identb"]
        cst, Tcst = G["cst"]
        epsb = cst[:, 0:1]
        zerob = cst[:, 1:2]
        cfar, Tcfar = G["cfar"]

        c.dma("sp", lamv[:], G["lam_d"][l:l + 1, :, :].to_broadcast([128, 4, 64]), writes=[Tlamv])
        c.op("pool", lambda e: e.memset(lams[:], 0.0), writes=[Tlams])
        c.op("dve", lambda e: e.tensor_tensor(out=lamp[:, 0, :], in0=lamv[:, 0, :], in1=lamv[:, 1, :], op=ALU.mult),
             reads=[Tlamv], writes=[Tlamv])
        c.op("dve", lambda e: e.tensor_tensor(out=lamp[:, 1, :], in0=lamv[:, 2, :], in1=lamv[:, 3, :], op=ALU.mult),
             reads=[Tlamv], writes=[Tlamv])
        for i in range(2):
            c.op("act", lambda e: e.activation(out=lamv[:, i, :], in_=lamp[:, i, :], func=AF.Identity,
                                               accum_out=lams[:, i:i + 1]),
                 reads=[Tlamv], writes=[Tlamv, Tlams])
        c.op("act", lambda e: e.activation(out=lams[:, 2:4], in_=lams[:, 0:2], func=AF.Exp),
             reads=[Tlams], writes=[Tlams])
        c.op("dve", lambda e: e.tensor_tensor(out=lams[:, 4:5], in0=lams[:, 3:4], in1=lams[:, 2:3], op=ALU.subtract),
             reads=[Tlams], writes=[Tlams])
        c.op("dve", lambda e: e.tensor_scalar(out=lams[:, 5:6], in0=lams[:, 4:5], scalar1=-lam_init, scalar2=None,
                                              op0=ALU.add),
             reads=[Tlams], writes=[Tlams])
        neglam = lams[:, 5:6]
        c.dma("sp", wsub[:], G["subw_d"][l:l + 1, :].unsqueeze(1).to_broadcast([128, 4, 128]), writes=[Twsub])
        c.op("dve", lambda e: e.tensor_scalar(out=wsub[:], in0=wsub[:], scalar1=1.0 - lam_init, scalar2=None,
                                              op0=ALU.mult),
             reads=[Twsub], writes=[Twsub])
        c.dma("sp", mpat[:], G["mpat_d"].rearrange("a p q -> p a q"), writes=[Tmpat])
        for i in range(2):
            c.op("pool", lambda e: e.memset(V[i][0][:, :, 128:129], 1.0), writes=[V[i][1]])

        def load_head(hl):
            kt, Tkt = KT[hl % 2]
            qt, Tqt = QT[hl % 2]
            v, Tv = V[hl % 2]
            bT, TbT = biasT[hl % 2]
            c.dma("sp", kt[:], QKT[4 + hl, :, :], reads=[TQKT], writes=[Tkt])
            c.dma("sp", qt[:], QKT[hl, :, :], reads=[TQKT], writes=[Tqt])
            for a in range(4):
                c.dma("sp", v[:, a * 8:(a + 1) * 8, 0:128], DAVr[:, a * 8:(a + 1) * 8, hl * 128:(hl + 1) * 128],
                      reads=[TDAV], writes=[Tv])
            c.dma("sp", braw[:], G["braw_d"][hl].rearrange("a p q -> p a q"), writes=[Tbraw])
            c.op("dve", lambda e: e.scalar_tensor_tensor(out=bT[:], in0=braw[:], scalar=1.0 / SCALE, in1=mpat[:],
                                                         op0=ALU.mult, op1=ALU.add),
                 reads=[Tbraw, Tmpat], writes=[TbT])

        load_head(0)
        it = 0
        for hl in range(4):
            if hl + 1 < 4:
                load_head(hl + 1)
            kt, Tkt = KT[hl % 2]
            qt, Tqt = QT[hl % 2]
            v, Tv = V[hl % 2]
            bT, TbT = biasT[hl % 2]
            for qb in range(8):
                g, Tg = gate[it % 2]
                f, Tf = fin[it % 2]
                it += 1
                c.dma("sp", g[:], DAGr[:, qb * 4:(qb + 1) * 4, hl * 128:(hl + 1) * 128], reads=[TDAG], writes=[Tg])
                for m in range(2):
                    r0 = 64 * m
                    nkb = 4 * qb + 4
                    for kb in range(nkb):
                        sp_, Tsp = sring.next()
                        delta = kb * 128 - qb * 512
                        special = delta >= -128
                        c.op("pe", lambda e: e.matmul(sp_[:], lhsT=kt[r0:r0 + 64, kb * 128:(kb + 1) * 128],
                                                      rhs=qt[r0:r0 + 64, qb * 512:(qb + 1) * 512],
                                                      start=True, stop=not special),
                             reads=[Tkt, Tqt], writes=[Tsp])
                        if special:
                            pat = (delta + 128) // 128
                            c.op("pe", lambda e: e.matmul(sp_[:], lhsT=identb[:], rhs=bT[:, pat, :],
                                                          start=False, stop=True),
                                 reads=[Tidb, TbT], writes=[Tsp])
                        E, TE = ering.next()
                        bias_ap = zerob if special else cfar[:, hl:hl + 1]
                        c.op("act", lambda e: e.activation(out=E[:], in_=sp_[:], func=AF.Exp, bias=bias_ap,
                                                           scale=SCALE),
                             reads=[Tsp, Tcst, Tcfar], writes=[TE])
                        for j in range(4):
                            if kb > qb * 4 + j:
                                continue
                            acc, Tacc = Oacc[m][j // 2]
                            c.op("pe", lambda e: e.matmul(acc[:, j % 2, :], lhsT=E[:, j * 128:(j + 1) * 128],
                                                          rhs=v[:, kb, :], start=(kb == 0 and j % 2 == 0), stop=(kb == qb * 4 + j)),
                                 reads=[TE, Tv], writes=[Tacc])
                    o_m, Tom = om[m]
                    rc, Trc = rec[m]
                    for b in range(2):
                        acc, Tacc = Oacc[m][b]
                        c.op("dve", lambda e: e.reciprocal(out=rc[:, 2 * b:2 * b + 2], in_=acc[:, :, 128]),
                             reads=[Tacc], writes=[Trc])
                        c.op("dve", lambda e: e.tensor_tensor(
                            out=o_m[:, 2 * b:2 * b + 2, :], in0=acc[:, :, 0:128],
                            in1=rc[:, 2 * b:2 * b + 2].unsqueeze(2).to_broadcast([128, 2, 128]), op=ALU.mult),
                             reads=[Tacc, Trc], writes=[Tom])
                c.op("dve", lambda e: e.scalar_tensor_tensor(out=dlt[:], in0=om[1][0][:], scalar=neglam,
                                                             in1=om[0][0][:], op0=ALU.mult, op1=ALU.add),
                     reads=[om[0][1], om[1][1], Tlams], writes=[Tdlt])
                c.op("pool", lambda e: e.memset(ss[:, 0:4, 0], 0.0), writes=[Tss])
                for j in range(4):
                    c.op("act", lambda e: e.activation(out=junk[:], in_=dlt[:, j, :], func=AF.Square,
                                                       accum_out=ss[:, j, 0:1]),
                         reads=[Tdlt], writes=[Tjunk, Tss])
                rsqrt_chain(c, ss[:, 4:8, 0], ss[:, 0:4, 0], 1.0 / 128, epsb, ss[:, 0:4, 1],
                            [Tss, Tcst], [Tss], Tss)
                c.op("act", lambda e: e.activation(out=gm[:], in_=g[:], func=AF.Silu), reads=[Tg], writes=[Tgm])
                c.op("pool", lambda e: e.tensor_tensor(out=gm[:], in0=gm[:], in1=wsub[:], op=ALU.mult),
                     reads=[Tgm, Twsub], writes=[Tgm])
                c.op("dve", lambda e: e.tensor_tensor(out=tmp[:], in0=dlt[:],
                                                      in1=ss[:, 4:8, 0:1].to_broadcast([128, 4, 128]), op=ALU.mult),
                     reads=[Tdlt, Tss], writes=[Ttmp])
                c.op("dve", lambda e: e.tensor_tensor(out=f[:], in0=tmp[:], in1=gm[:], op=ALU.mult),
                     reads=[Ttmp, Tgm], writes=[Tf])
                c.dma("pool", MIXr[:, qb * 4:(qb + 1) * 4, hl * 128:(hl + 1) * 128], f[:], reads=[Tf], writes=[TMIXH])
        c.barrier()


class _Stop(Exception):
    pass


def _chk(level):
    import os
    if int(os.environ.get("GDN_STOP", "99")) == level:
        _chk.c.barrier()
        raise _Stop()


def stage_gdn(c, nc, G, l):
    with contextlib.ExitStack() as es:
        try:
            _stage_gdn(c, nc, G, l, es)
        except _Stop:
            pass


def _stage_gdn(c, nc, G, l, es):
    _chk.c = c
    GQKV, TGQKV = G["GQKV"]
    GZ, TGZ = G["GZ"]
    GBA, TGBA = G["GBA"]
    MIXH, TMIXH = G["MIXH"]
    if True:
        sb = lambda n, s, d: es.enter_context(nc.sbuf_tensor(n, s, d))
        ps = lambda n, s, d: es.enter_context(nc.psum_tensor(n, s, d))
        cst, Tcst = G["cst"]
        epsb = cst[:, 0:1]
        oneb = cst[:, 2:3]
        identb, Tidb = G["identb"]
        identf, Tidf = G["identf"]
        Uf, TUf = G["Uf"]
        onesf, Tonesf = G["onesf"]
        negonesf, Tnegonesf = G["negonesf"]
        onesb, Tonesb = G["onesb"]
        negmask, Tnegmask = G["negmask"]
        strict, Tstrict = G["strict"]
        cw, Tcw = G["cw"][l]
        hpar, Thpar = G["hpar"][l]

        pring = Ring([(ps(f"gd_p{i}", [128, 4, 128], F32), TP()) for i in range(3)])
        sbanks = [ps(f"gd_scan{i}", [128, 4, 128], F32) for i in range(3)]
        Tsb = [TP() for _ in range(3)]
        ws_ps = [(sbanks[0][:, h, :], Tsb[0]) for h in range(4)]
        O_ps = [(sbanks[1][:, h, :], Tsb[1]) for h in range(4)]
        Sd_ps = [(sbanks[2][:, h, :], Tsb[2]) for h in range(4)]
        trbank = ps("gd_tr", [128, 8, 128], BF16)
        Ttr = TP()
        ssq_ps = ps("gd_ssq", [128, 512], F32)
        Tssq = TP()

        ba = sb("gd_ba", [128, NT, 8], F32)
        Tba = T()
        sc = {}
        for nm in ("beta", "negb", "xa", "nx", "mn", "ex", "lg", "mx", "g", "gc", "eg", "egl", "ekd", "dd"):
            sc[nm] = sb("gd_sc_" + nm, [128, NT, 4], F32)
        Tsc = T()
        nea = sb("gd_nea", [128, 4], F32)
        GBAr = GBA.rearrange("(n p) j -> p n j", p=128)
        for a in range(8):
            c.dma("sp", ba[:, a * 4:(a + 1) * 4, :], GBAr[:, a * 4:(a + 1) * 4, :], reads=[TGBA], writes=[Tba])
        c.op("act", lambda e: e.activation(out=sc["beta"][:], in_=ba[:, :, 0:4], func=AF.Sigmoid),
             reads=[Tba], writes=[Tsc])
        c.op("dve", lambda e: e.tensor_scalar(out=sc["negb"][:], in0=sc["beta"][:], scalar1=-1.0, scalar2=None,
                                              op0=ALU.mult), reads=[Tsc], writes=[Tsc])
        c.op("dve", lambda e: e.tensor_tensor(out=sc["xa"][:], in0=ba[:, :, 4:8],
                                              in1=hpar[:, 4:8].unsqueeze(1).to_broadcast([128, NT, 4]), op=ALU.add),
             reads=[Tba, Thpar], writes=[Tsc])
        c.op("dve", lambda e: e.tensor_scalar(out=sc["nx"][:], in0=sc["xa"][:], scalar1=-1.0, scalar2=None,
                                              op0=ALU.mult), reads=[Tsc], writes=[Tsc])
        c.op("dve", lambda e: e.tensor_tensor(out=sc["mn"][:], in0=sc["xa"][:], in1=sc["nx"][:], op=ALU.min),
             reads=[Tsc], writes=[Tsc])
        c.op("act", lambda e: e.activation(out=sc["ex"][:], in_=sc["mn"][:], func=AF.Exp), reads=[Tsc], writes=[Tsc])
        c.op("act", lambda e: e.activation(out=sc["lg"][:], in_=sc["ex"][:], func=AF.Ln, bias=oneb),
             reads=[Tsc, Tcst], writes=[Tsc])
        c.op("dve", lambda e: e.tensor_scalar(out=sc["mx"][:], in0=sc["xa"][:], scalar1=0.0, scalar2=None,
                                              op0=ALU.max), reads=[Tsc], writes=[Tsc])
        c.op("dve", lambda e: e.tensor_tensor(out=sc["lg"][:], in0=sc["lg"][:], in1=sc["mx"][:], op=ALU.add),
             reads=[Tsc], writes=[Tsc])
        c.op("act", lambda e: e.activation(out=nea[:], in_=hpar[:, 0:4], func=AF.Exp), reads=[Thpar], writes=[Tsc])
        c.op("dve", lambda e: e.tensor_scalar(out=nea[:], in0=nea[:], scalar1=-1.0, scalar2=None, op0=ALU.mult),
             reads=[Tsc], writes=[Tsc])
        c.op("dve", lambda e: e.tensor_tensor(out=sc["g"][:], in0=sc["lg"][:],
                                              in1=nea[:].unsqueeze(1).to_broadcast([128, NT, 4]), op=ALU.mult),
             reads=[Tsc], writes=[Tsc])
        gflat = sc["g"][:].rearrange("p n h -> p (n h)")
        bkA, TpA = pring.next()
        bkB, TpB = pring.next()
        pA = bkA[:].rearrange("p a b -> p (a b)")[:, 0:128]
        pB = bkB[:].rearrange("p a b -> p (a b)")[:, 0:128]
        c.op("pe", lambda e: e.matmul(pA, lhsT=Uf[:], rhs=gflat, start=True, stop=True),
             reads=[TUf, Tsc], writes=[TpA])
        c.op("pe", lambda e: e.matmul(pB, lhsT=onesf[:], rhs=gflat, start=True, stop=True),
             reads=[Tonesf, Tsc], writes=[TpB])
        fl = lambda nm: sc[nm][:].rearrange("p n h -> p (n h)")
        c.op("dve", lambda e: e.tensor_copy(out=fl("gc"), in_=pA), reads=[TpA], writes=[Tsc])
        c.op("act", lambda e: e.activation(out=fl("eg"), in_=pA, func=AF.Exp), reads=[TpA], writes=[Tsc])
        c.op("act", lambda e: e.activation(out=fl("egl"), in_=pB, func=AF.Exp), reads=[TpB], writes=[Tsc])
        c.op("dve", lambda e: e.tensor_tensor(out=fl("dd"), in0=pB, in1=fl("gc"), op=ALU.subtract),
             reads=[TpB, Tsc], writes=[Tsc])
        c.op("act", lambda e: e.activation(out=fl("ekd"), in_=fl("dd"), func=AF.Exp), reads=[Tsc], writes=[Tsc])

        _chk(1)
        Xr = Ring([(sb(f"gd_X{i}", [128, 515], F32), T()) for i in range(3)])
        yr = Ring([(sb(f"gd_y{i}", [128, 512], F32), T()) for i in range(2)])
        sr = Ring([(sb(f"gd_s{i}", [128, 512], F32), T()) for i in range(2)])
        sqr = Ring([(sb(f"gd_sq{i}", [128, 512], BF16), T()) for i in range(2)])
        rnr = Ring([(sb(f"gd_rn{i}", [128, 512], F32), T()) for i in range(2)])
        lnr = Ring([(sb(f"gd_ln{i}", [128, 512], F32), T()) for i in range(2)])
        qkvT = [[(sb(f"gd_qkv{p}_{t}", [128, 4, 512], BF16), T()) for t in range(3)] for p in range(2)]
        zt = [(sb(f"gd_z{i}", [128, 512], F32), T()) for i in range(2)]
        gzp = [(sb(f"gd_gz{i}", [128, 512], F32), T()) for i in range(2)]
        mixs = [(sb(f"gd_mix{i}", [128, 512], BF16), T()) for i in range(2)]
        gnw4 = sb("gd_gnw4", [128, 4, 128], F32)
        Tgnw4 = T()
        c.dma("sp", gnw4[:], G["gnw_d"][l:l + 1, :].unsqueeze(1).to_broadcast([128, 4, 128]), writes=[Tgnw4])
        S32 = [(sb(f"gd_S32_{h}", [128, 128], F32), T()) for h in range(4)]
        Sb = [(sb(f"gd_Sb_{h}", [128, 128], BF16), T()) for h in range(4)]
        for h in range(4):
            c.op("pool", lambda e: e.memset(S32[h][0][:], 0.0), writes=[S32[h][1]])
            c.op("pool", lambda e: e.memset(Sb[h][0][:], 0.0), writes=[Sb[h][1]])
        ost = sb("gd_ost", [128, NT, 4, 4], F32)
        Tost = [[T() for _ in range(4)] for _ in range(NT)]
        c.op("pool", lambda e: e.memset(ost[:], 0.0), writes=[t for row in Tost for t in row])
        junk = sb("gd_junk", [128, 128], BF16)
        Tjunk = T()

        def mk(nm, dt):
            return [[(sb(f"gd_{nm}_{p}_{h}", [128, 128], dt), T()) for h in range(4)] for p in range(2)]
        B_kg, B_kd, B_vt = mk("kg", BF16), mk("kd", BF16), mk("vt", BF16)
        B_Gm, B_egbc, B_dTi, B_dTs = mk("Gm", F32), mk("egbc", F32), mk("dTi", F32), mk("dTs", F32)
        B_N, B_NT, B_X = mk("N", BF16), mk("NT", BF16), mk("X", BF16)
        B_P = [mk("P0", BF16), mk("P1", BF16)]
        B_PT = [mk("PT0", BF16), mk("PT1", BF16)]
        B_qk, B_qg, B_u, B_w, B_vn = mk("qk", BF16), mk("qg", BF16), mk("u", F32), mk("w", BF16), mk("vn", BF16)

        for blk in range(8):
            par = blk % 2
            qT, kT, vT = qkvT[par]
            for r in range(12):
                t, hl = r // 4, r % 4
                X, TX = Xr.next()
                if blk == 0:
                    c.op("pool", lambda e: e.memset(X[:, 0:3], 0.0), writes=[TX])
                    c.dma("sp", X[:, 3:515], GQKV[r, :, 0:512], reads=[TGQKV], writes=[TX])
                else:
                    c.dma("sp", X[:], GQKV[r, :, blk * 512 - 3:blk * 512 + 512], reads=[TGQKV], writes=[TX])
                y, Ty = yr.next()
                c.op("dve", lambda e: e.tensor_scalar(out=y[:], in0=X[:, 0:512], scalar1=cw[:, r, 0:1], scalar2=None,
                                                      op0=ALU.mult), reads=[TX, Tcw], writes=[Ty])
                for j in range(1, 4):
                    c.op("dve", lambda e: e.scalar_tensor_tensor(out=y[:], in0=X[:, j:j + 512],
                                                                 scalar=cw[:, r, j:j + 1], in1=y[:],
                                                                 op0=ALU.mult, op1=ALU.add),
                         reads=[TX, Tcw, Ty], writes=[Ty])
                if t == 2:
                    c.op("act", lambda e: e.activation(out=vT[0][:, hl, :], in_=y[:], func=AF.Silu),
                         reads=[Ty], writes=[vT[1]])
                    continue
                s, Ts = sr.next()
                c.op("act", lambda e: e.activation(out=s[:], in_=y[:], func=AF.Silu), reads=[Ty], writes=[Ts])
                sq, Tsq = sqr.next()
                c.op("pool", lambda e: e.tensor_tensor(out=sq[:], in0=s[:], in1=s[:], op=ALU.mult),
                     reads=[Ts], writes=[Tsq])
                c.op("pe", lambda e: e.matmul(ssq_ps[:], lhsT=onesb[:], rhs=sq[:], start=True, stop=True),
                     reads=[Tonesb, Tsq], writes=[Tssq])
                rn, Trn = rnr.next()
                ln_, Tln = lnr.next()
                rsqrt_chain(c, rn[:], ssq_ps[:], 1.0, epsb, ln_[:], [Tssq, Tcst], [Trn], Tln)
                dstT = qT if t == 0 else kT
                scl = (128.0 ** -0.5) if t == 0 else 1.0
                c.op("dve", lambda e: e.scalar_tensor_tensor(out=dstT[0][:, hl, :], in0=s[:], scalar=scl, in1=rn[:],
                                                             op0=ALU.mult, op1=ALU.mult),
                     reads=[Ts, Trn], writes=[dstT[1]])

            _chk(2)
            for cc in range(4):
                n = blk * 4 + cc
                cp = n % 2
                cs = slice(cc * 128, (cc + 1) * 128)
                z, Tz = zt[cp]
                gz, Tgz = gzp[cp]
                mx, Tmx = mixs[cp]
                c.dma("sp", z[:], GZ[n * 128:(n + 1) * 128, :], reads=[TGZ], writes=[Tz])
                c.op("act", lambda e: e.activation(out=gz[:], in_=z[:], func=AF.Silu), reads=[Tz], writes=[Tgz])
                c.op("pool", lambda e: e.tensor_tensor(out=gz[:], in0=gz[:], in1=gnw4[:].rearrange("p a b -> p (a b)"),
                                                       op=ALU.mult), reads=[Tgz, Tgnw4], writes=[Tgz])
                col = lambda nm, h: sc[nm][:, n, h:h + 1]
                _chk(25)
                for h in range(4):
                    c.op("pe", lambda e: e.transpose(trbank[:, 2 * h, :], kT[0][:, h, cs], identb[:]),
                         reads=[kT[1], Tidb], writes=[Ttr])
                    c.op("pe", lambda e: e.transpose(trbank[:, 2 * h + 1, :], vT[0][:, h, cs], identb[:]),
                         reads=[vT[1], Tidb], writes=[Ttr])
                _chk(26)
                for h in range(4):
                    kg, Tkg = B_kg[cp][h]
                    kd, Tkd = B_kd[cp][h]
                    vt, Tvt = B_vt[cp][h]
                    p1, Tp1 = trbank[:, 2 * h, :], Ttr
                    p2, Tp2 = trbank[:, 2 * h + 1, :], Ttr
                    evac_scale(c, "act", kg[:], p1, col("eg", h), [Tp1, Tsc], [Tkg])
                    evac_scale(c, "dve", kd[:], p1, col("ekd", h), [Tp1, Tsc], [Tkd])
                    evac_copy(c, "act", vt[:], p2, [Tp2], [Tvt])
                _chk(3)
                bka, Tpa = pring.next()
                bkb, Tpb = pring.next()
                for h in range(4):
                    Gm, TGm = B_Gm[cp][h]
                    c.op("dve", lambda e: e.tensor_scalar(out=Gm[:], in0=Uf[:], scalar1=col("g", h), scalar2=None,
                                                          op0=ALU.mult), reads=[TUf, Tsc], writes=[TGm])
                    pa = bka[:, h, :]
                    pb = bkb[:, h, :]
                    c.op("pe", lambda e: e.matmul(pa, lhsT=onesf[:], rhs=Gm[:], start=True, stop=True),
                         reads=[Tonesf, TGm], writes=[Tpa])
                    c.op("pe", lambda e: e.matmul(pb, lhsT=onesf[:], rhs=Gm[:], start=True, stop=False),
                         reads=[Tonesf, TGm], writes=[Tpb])
                    c.op("pe", lambda e: e.matmul(pb, lhsT=Gm[:], rhs=negonesf[:], start=False, stop=False),
                         reads=[Tnegonesf, TGm], writes=[Tpb])
                    c.op("pe", lambda e: e.matmul(pb, lhsT=identf[:], rhs=negmask[:], start=False, stop=True),
                         reads=[Tidf, Tnegmask], writes=[Tpb])
                for h in range(4):
                    egbc, Tegbc = B_egbc[cp][h]
                    dTi, TdTi = B_dTi[cp][h]
                    dTs, TdTs = B_dTs[cp][h]
                    pa = bka[:, h, :]
                    pb = bkb[:, h, :]
                    c.op("act", lambda e: e.activation(out=egbc[:], in_=pa, func=AF.Exp), reads=[Tpa], writes=[Tegbc])
                    c.op("act", lambda e: e.activation(out=dTi[:], in_=pb, func=AF.Exp), reads=[Tpb], writes=[TdTi])
                    c.op("pool", lambda e: e.tensor_tensor(out=dTs[:], in0=dTi[:], in1=strict[:], op=ALU.mult),
                         reads=[TdTi, Tstrict], writes=[TdTs])
                _chk(4)
                bkk, Tpk = pring.next()
                bkq, Tpq = pring.next()
                for h in range(4):
                    pk = bkk[:, h, :]
                    pq = bkq[:, h, :]
                    c.op("pe", lambda e: e.matmul(pk, lhsT=kT[0][:, h, cs], rhs=kT[0][:, h, cs], start=True, stop=True),
                         reads=[kT[1]], writes=[Tpk])
                    c.op("pe", lambda e: e.matmul(pq, lhsT=kT[0][:, h, cs], rhs=qT[0][:, h, cs], start=True, stop=True),
                         reads=[kT[1], qT[1]], writes=[Tpq])
                for h in range(4):
                    N_, TN = B_N[cp][h]
                    qk, Tqk = B_qk[cp][h]
                    qg, Tqg = B_qg[cp][h]
                    dTi, TdTi = B_dTi[cp][h]
                    dTs, TdTs = B_dTs[cp][h]
                    egbc, Tegbc = B_egbc[cp][h]
                    pk = bkk[:, h, :]
                    pq = bkq[:, h, :]
                    c.op("dve", lambda e: e.scalar_tensor_tensor(out=N_[:], in0=pk, scalar=col("beta", h), in1=dTs[:],
                                                                 op0=ALU.mult, op1=ALU.mult),
                         reads=[Tpk, Tsc, TdTs], writes=[TN])
                    c.op("dve", lambda e: e.tensor_tensor(out=qk[:], in0=pq, in1=dTi[:], op=ALU.mult),
                         reads=[Tpq, TdTi], writes=[Tqk])
                    c.op("pool", lambda e: e.tensor_tensor(out=qg[:], in0=qT[0][:, h, cs], in1=egbc[:], op=ALU.mult),
                         reads=[qT[1], Tegbc], writes=[Tqg])
                _chk(5)
                bkt, Tpt = pring.next()
                for h in range(4):
                    N_, TN = B_N[cp][h]
                    X_, TX_ = B_X[cp][h]
                    c.op("pool", lambda e: e.tensor_tensor(out=X_[:], in0=identb[:], in1=N_[:], op=ALU.subtract),
                         reads=[Tidb, TN], writes=[TX_])
                    c.op("pe", lambda e: e.matmul(bkt[:, h, :], lhsT=N_[:], rhs=identb[:], start=True, stop=True),
                         reads=[TN, Tidb], writes=[Tpt])
                for h in range(4):
                    NT_, TNT = B_NT[cp][h]
                    evac_copy(c, _alt(h), NT_[:], bkt[:, h, :], [Tpt], [TNT])
                for k in range(1, 7):
                    bkt, Tpt = pring.next()
                    for h in range(4):
                        Pp, TPp = (B_N[cp][h] if k == 1 else B_P[(k - 1) % 2][cp][h])
                        PTp, TPTp = (B_NT[cp][h] if k == 1 else B_PT[(k - 1) % 2][cp][h])
                        c.op("pe", lambda e: e.matmul(bkt[:, h, :], lhsT=Pp[:], rhs=PTp[:], start=True, stop=True),
                             reads=[TPp, TPTp], writes=[Tpt])
                    if k < 6:
                        bkp, Tpp = pring.next()
                        for h in range(4):
                            Pp, TPp = (B_N[cp][h] if k == 1 else B_P[(k - 1) % 2][cp][h])
                            PTp, TPTp = (B_NT[cp][h] if k == 1 else B_PT[(k - 1) % 2][cp][h])
                            c.op("pe", lambda e: e.matmul(bkp[:, h, :], lhsT=PTp[:], rhs=Pp[:], start=True, stop=True),
                                 reads=[TPp, TPTp], writes=[Tpp])
                    for h in range(4):
                        PTn, TPTn = B_PT[k % 2][cp][h]
                        evac_copy(c, "act", PTn[:], bkt[:, h, :], [Tpt], [TPTn])
                    bkx, Tpx = pring.next()
                    for h in range(4):
                        PTn, TPTn = B_PT[k % 2][cp][h]
                        X_, TX_ = B_X[cp][h]
                        c.op("pe", lambda e: e.matmul(bkx[:, h, :], lhsT=PTn[:], rhs=X_[:], start=True, stop=True),
                             reads=[TPTn, TX_], writes=[Tpx])
                    if k < 6:
                        for h in range(4):
                            Pn, TPn = B_P[k % 2][cp][h]
                            evac_copy(c, "dve", Pn[:], bkp[:, h, :], [Tpp], [TPn])
                    for h in range(4):
                        X_, TX_ = B_X[cp][h]
                        c.op("dve", lambda e: e.tensor_tensor(out=X_[:], in0=bkx[:, h, :], in1=X_[:], op=ALU.add),
                             reads=[Tpx, TX_], writes=[TX_])
                _chk(6)
                bku, Tpu = pring.next()
                bkw, Tpw = pring.next()
                for h in range(4):
                    X_, TX_ = B_X[cp][h]
                    c.op("pe", lambda e: e.matmul(bku[:, h, :], lhsT=X_[:], rhs=B_vt[cp][h][0][:], start=True, stop=True),
                         reads=[TX_, B_vt[cp][h][1]], writes=[Tpu])
                    c.op("pe", lambda e: e.matmul(bkw[:, h, :], lhsT=B_kg[cp][h][0][:], rhs=X_[:], start=True, stop=True),
                         reads=[TX_, B_kg[cp][h][1]], writes=[Tpw])
                for h in range(4):
                    u, Tu = B_u[cp][h]
                    w, Tw = B_w[cp][h]
                    evac_scale(c, "dve", u[:], bku[:, h, :], col("beta", h), [Tpu, Tsc], [Tu])
                    evac_copy(c, "act", w[:], bkw[:, h, :], [Tpw], [Tw])
                _chk(7)
                for h in range(4):
                    c.op("pe", lambda e: e.matmul(ws_ps[h][0], lhsT=B_w[cp][h][0][:], rhs=Sb[h][0][:],
                                                  start=True, stop=True),
                         reads=[B_w[cp][h][1], Sb[h][1]], writes=[ws_ps[h][1]])
                for h in range(4):
                    vn, Tvn = B_vn[cp][h]
                    c.op("dve", lambda e: e.scalar_tensor_tensor(out=vn[:], in0=ws_ps[h][0], scalar=col("negb", h),
                                                                 in1=B_u[cp][h][0][:], op0=ALU.mult, op1=ALU.add),
                         reads=[ws_ps[h][1], Tsc, B_u[cp][h][1]], writes=[Tvn])
                for h in range(4):
                    vn, Tvn = B_vn[cp][h]
                    c.op("pe", lambda e: e.matmul(Sd_ps[h][0], lhsT=B_kd[cp][h][0][:], rhs=vn[:],
                                                  start=True, stop=True),
                         reads=[B_kd[cp][h][1], Tvn], writes=[Sd_ps[h][1]])
                for h in range(4):
                    vn, Tvn = B_vn[cp][h]
                    c.op("pe", lambda e: e.matmul(O_ps[h][0], lhsT=B_qg[cp][h][0][:], rhs=Sb[h][0][:],
                                                  start=True, stop=False),
                         reads=[B_qg[cp][h][1], Sb[h][1]], writes=[O_ps[h][1]])
                    c.op("pe", lambda e: e.matmul(O_ps[h][0], lhsT=B_qk[cp][h][0][:], rhs=vn[:],
                                                  start=False, stop=True),
                         reads=[B_qk[cp][h][1], Tvn], writes=[O_ps[h][1]])
                for h in range(4):
                    c.op("dve", lambda e: e.scalar_tensor_tensor(out=Sb[h][0][:], in0=S32[h][0][:],
                                                                 scalar=col("egl", h), in1=Sd_ps[h][0],
                                                                 op0=ALU.mult, op1=ALU.add),
                         reads=[S32[h][1], Tsc, Sd_ps[h][1]], writes=[Sb[h][1]])
                    c.op("dve", lambda e: e.scalar_tensor_tensor(out=S32[h][0][:], in0=S32[h][0][:],
                                                                 scalar=col("egl", h), in1=Sd_ps[h][0],
                                                                 op0=ALU.mult, op1=ALU.add),
                         reads=[S32[h][1], Tsc, Sd_ps[h][1]], writes=[S32[h][1]])
                for h in range(4):
                    To = Tost[n][h]
                    c.op("act", lambda e: e.activation(out=junk[:], in_=O_ps[h][0], func=AF.Square,
                                                       accum_out=ost[:, n, h, 0:1]),
                         reads=[O_ps[h][1]], writes=[Tjunk, To])
                    rsqrt_chain(c, ost[:, n, h, 2:3], ost[:, n, h, 0:1], 1.0 / 128, epsb, ost[:, n, h, 1:2],
                                [To, Tcst], [To], To)
                    c.op("dve", lambda e: e.scalar_tensor_tensor(out=mx[:, h * 128:(h + 1) * 128], in0=O_ps[h][0],
                                                                 scalar=ost[:, n, h, 2:3],
                                                                 in1=gz[:, h * 128:(h + 1) * 128],
                                                                 op0=ALU.mult, op1=ALU.mult),
                         reads=[O_ps[h][1], To, Tgz], writes=[Tmx])
                c.dma("pool", MIXH[n * 128:(n + 1) * 128, 512:1024], mx[:], reads=[Tmx], writes=[TMIXH])
                _chk(8)
        c.barrier()


def build_program(mode, layers=(0, 1), dbg_stages=None):
    nc = bass.Bass("TRN2", target_bir_lowering=False)
    dram = lambda n, s, d, kind: nc.dram_tensor(n, s, d, kind=kind).ap()
    IN, OUT, INT = "ExternalInput", "ExternalOutput", "Internal"
    SK = OUT if mode == "dbg" else INT
    G = {}
    need_in = {"fused": (0, 1), "p1": (0,), "p2": (1,), "p3": (), "dbg": tuple(layers)}[mode]
    need_out = {"fused": (0, 1), "p1": (), "p2": (0,), "p3": (1,), "dbg": tuple(layers)}[mode]
    with contextlib.ExitStack() as es:
        c = Ctx(nc, es)
        sb = lambda n, s, d: es.enter_context(nc.sbuf_tensor(n, s, d))
        win = {l: dram(f"win{l}", [128, 16, NCOL], F32, IN) for l in need_in}
        wout = {l: dram(f"wout{l}", [128, 16, D], F32, IN) for l in need_out}
        G["Wbf"] = {l: dram(f"wbf{l}", [128, 16, NCOL], BF16, INT) for l in need_in}
        G["TWbf"] = {l: T() for l in need_in}
        G["Wobf"] = {l: dram(f"wobf{l}", [128, 16, D], BF16, INT) for l in need_out}
        G["TWobf"] = {l: T() for l in need_out}
        gpk_d = dram("gpk_d", [DEPTH, 128, 16], F32, IN)
        cw_d = dram("cw_d", [DEPTH, 128, 12, 4], F32, IN)
        hpar_d = dram("hpar_d", [DEPTH, 128, 8], F32, IN)
        G["gnw_d"] = dram("gnw_d", [DEPTH, 128], F32, IN)
        G["subw_d"] = dram("subw_d", [DEPTH, 128], F32, IN)
        G["lam_d"] = dram("lam_d", [DEPTH, 4, 64], F32, IN)
        G["fnw_d"] = dram("fnw_d", [1, D], F32, IN)
        G["braw_d"] = dram("braw_d", [4, 5, 128, 512], F32, IN)
        G["mpat_d"] = dram("mpat_d", [5, 128, 512], F32, IN)
        cfar_d = dram("cfar_d", [128, 4], F32, IN)
        cmat_d = dram("cmat_d", [6, 128, 128], F32, IN)
        if mode == "dbg" and "inproj" not in dbg_stages:
            SK = IN
        G["QKT"] = (dram("s_qkt", [8, 128, SEQ], BF16, SK), T())
        G["GQKV"] = (dram("s_gqkv", [12, 128, SEQ], F32, SK), T())
        G["DAV"] = (dram("s_dav", [SEQ, 512], BF16, SK), T())
        G["DAG"] = (dram("s_dag", [SEQ, 512], F32, SK), T())
        G["GZ"] = (dram("s_gz", [SEQ, 512], F32, SK), T())
        G["GBA"] = (dram("s_gba", [SEQ, 8], F32, SK), T())
        cst = sb("cst", [128, 4], F32)
        Tcst = T()
        c.op("pool", lambda e: e.memset(cst[:, 0:1], RMS_EPS), writes=[Tcst])
        c.op("pool", lambda e: e.memset(cst[:, 1:2], 0.0), writes=[Tcst])
        c.op("pool", lambda e: e.memset(cst[:, 2:3], 1.0), writes=[Tcst])
        G["cst"] = (cst, Tcst)
        cm = sb("cmat", [128, 6, 128], F32)
        Tcm = T()
        c.dma("sp", cm[:], cmat_d.rearrange("a p q -> p a q"), writes=[Tcm])
        G["identf"] = (cm[:, 0, :], Tcm)
        G["Uf"] = (cm[:, 1, :], Tcm)
        G["onesf"] = (cm[:, 2, :], Tcm)
        G["negonesf"] = (cm[:, 3, :], Tcm)
        G["negmask"] = (cm[:, 4, :], Tcm)
        G["strict"] = (cm[:, 5, :], Tcm)
        identb = sb("identb", [128, 128], BF16)
        onesb = sb("onesb", [128, 128], BF16)
        Tib, Tob = T(), T()
        c.op("dve", lambda e: e.tensor_copy(out=identb[:], in_=cm[:, 0, :]), reads=[Tcm], writes=[Tib])
        c.op("dve", lambda e: e.tensor_copy(out=onesb[:], in_=cm[:, 2, :]), reads=[Tcm], writes=[Tob])
        G["identb"] = (identb, Tib)
        G["onesb"] = (onesb, Tob)
        cfar = sb("cfar", [128, 4], F32)
        Tcfar = T()
        c.dma("sp", cfar[:], cfar_d, writes=[Tcfar])
        G["cfar"] = (cfar, Tcfar)
        G["gpk"], G["cw"], G["hpar"] = {}, {}, {}
        for l in range(DEPTH):
            t1 = sb(f"gpk{l}", [128, 16], F32)
            t2 = sb(f"cw{l}", [128, 12, 4], F32)
            t3 = sb(f"hpar{l}", [128, 8], F32)
            T1, T2, T3 = T(), T(), T()
            c.dma("sp", t1[:], gpk_d[l], writes=[T1])
            c.dma("sp", t2[:], cw_d[l], writes=[T2])
            c.dma("sp", t3[:], hpar_d[l], writes=[T3])
            G["gpk"][l], G["cw"][l], G["hpar"][l] = (t1, T1), (t2, T2), (t3, T3)
        for l in need_in:
            for k in range(16):
                c.dma("pool", G["Wbf"][l][:, k, :], win[l][:, k, :], writes=[G["TWbf"][l]])
        for l in need_out:
            for k in range(16):
                c.dma("pool", G["Wobf"][l][:, k, :], wout[l][:, k, :], writes=[G["TWobf"][l]])

        if mode == "dbg":
            l = layers[0]
            x_in = dram("x_in", [SEQ, D], F32, IN)
            G["MIXH"] = (dram("mixh", [SEQ, 1024], BF16, OUT), T())
            mixf = dram("mixf_in", [2, SEQ, 1024], BF16, IN)
            x1 = dram("x1", [SEQ, D], F32, OUT)
            if "inproj" in dbg_stages:
                stage_inproj(c, nc, G, l, x_in, T())
            if "da" in dbg_stages:
                stage_da(c, nc, G, l)
            if "gdn" in dbg_stages:
                stage_gdn(c, nc, G, l)
            if "outproj" in dbg_stages:
                stage_outproj(c, nc, G, l, x_in, T(), mixf, T(), x1, T(), 0, NT, False)
        elif mode == "p1":
            x_in = dram("x_in", [SEQ, D], F32, IN)
            G["MIXH"] = (dram("mixh", [SEQ, 1024], BF16, OUT), T())
            stage_inproj(c, nc, G, 0, x_in, T())
            stage_da(c, nc, G, 0)
            stage_gdn(c, nc, G, 0)
        elif mode == "p2":
            x_in = dram("x_in", [SEQ, D], F32, IN)
            mixf = dram("mixf_in", [2, SEQ, 1024], BF16, IN)
            x1 = dram("x1", [SEQ, D], F32, OUT)
            Tx1 = T()
            G["MIXH"] = (dram("mixh", [SEQ, 1024], BF16, OUT), T())
            stage_outproj(c, nc, G, 0, x_in, T(), mixf, T(), x1, Tx1, 0, NT, False)
            stage_inproj(c, nc, G, 1, x1, Tx1)
            stage_da(c, nc, G, 1)
            stage_gdn(c, nc, G, 1)
        elif mode == "p3":
            x_in = dram("x_in", [SEQ // 2, D], F32, IN)
            mixf = dram("mixf_in", [2, SEQ // 2, 1024], BF16, IN)
            out = dram("out", [SEQ // 2, D], F32, OUT)
            stage_outproj(c, nc, G, 1, x_in, T(), mixf, T(), out, T(), 0, NT // 2, True)
        elif mode == "fused":
            x_in = dram("x_in", [SEQ, D], F32, IN)
            out = dram("out", [SEQ, D], F32, OUT)
            G["MIXH"] = (dram("s_mixh", [SEQ, 1024], BF16, INT), T())
            mixf, Tmixf = dram("s_mixf", [2, SEQ, 1024], BF16, INT), T()
            x1, Tx1 = dram("s_x1", [SEQ, D], F32, INT), T()
            Tx0 = T()
            groups = [[2 * i, 2 * i + 1] for i in range(NB)]
            for l in range(DEPTH):
                src, Tsrc = (x_in, Tx0) if l == 0 else (x1, Tx1)
                stage_inproj(c, nc, G, l, src, Tsrc)
                stage_da(c, nc, G, l)
                stage_gdn(c, nc, G, l)
                c.collective(lambda e: e.collective_compute(
                    "AllGather", ALU.bypass, replica_groups=groups,
                    ins=[G["MIXH"][0].rearrange("t c -> (t c)")], outs=[mixf.rearrange("r t c -> (r t c)")]),
                    reads=[G["MIXH"][1]], writes=[Tmixf])
                if l == 0:
                    stage_outproj(c, nc, G, l, x_in, Tx0, mixf, Tmixf, x1, Tx1, 0, NT, False)
                else:
                    stage_outproj(c, nc, G, l, x1, Tx1, mixf, Tmixf, out, T(), 0, NT, True)
        c.finish("sp")
        print(f"[build {mode}] instructions={c.n_ins} waits={c.n_wait} sems={len(c.sems)}")
    return nc


def _bucket_np(dist):
    n = np.maximum(dist, 0)
    max_exact = 16
    large = max_exact + (np.log(np.maximum(n, max_exact).astype(np.float32) / np.float32(max_exact))
                         / np.float32(math.log(128 / max_exact)) * np.float32(32 - max_exact)).astype(np.int32)
    large = np.minimum(large, 31)
    return np.where(n < max_exact, n, large)


def _consts():
    j = np.arange(128)[:, None]
    i = np.arange(128)[None, :]
    cm = np.zeros((6, 128, 128), np.float32)
    cm[0] = np.eye(128)
    cm[1] = (j <= i)
    cm[2] = 1.0
    cm[3] = -1.0
    cm[4] = np.where(i < j, NEG, 0.0)
    cm[5] = (i > j)
    k = np.arange(128)[:, None]
    q = np.arange(512)[None, :]
    dists = [q - k - (a * 128 - 128) for a in range(5)]
    mpat = np.stack([np.where(d < 0, NEG, 0.0) for d in dists]).astype(np.float32)
    bidx = np.stack([_bucket_np(d) for d in dists])
    return cm, mpat, bidx


def prep_core_inputs(inp, b, hh):
    cm, mpat, bidx = _consts()
    H = np.arange(4 * hh, 4 * hh + 4)
    cols = np.concatenate([
        hh * 512 + np.arange(512), 1024 + hh * 512 + np.arange(512),
        4096 + hh * 512 + np.arange(512), 5120 + hh * 512 + np.arange(512), 6144 + hh * 512 + np.arange(512),
        2048 + hh * 512 + np.arange(512), 3072 + hh * 512 + np.arange(512), 7168 + hh * 512 + np.arange(512),
        8192 + hh * 4 + np.arange(4), 8200 + hh * 4 + np.arange(4)])
    rows = np.concatenate([np.concatenate([r * 512 + np.arange(512), 1024 + r * 512 + np.arange(512)])
                           for r in range(2)])
    m = {}
    for l in range(DEPTH):
        m[f"win{l}"] = np.ascontiguousarray(
            inp["w_in"][l][:, cols].reshape(16, 128, NCOL).transpose(1, 0, 2))
        m[f"wout{l}"] = np.ascontiguousarray(
            inp["w_out"][l][rows, :].reshape(16, 128, D).transpose(1, 0, 2))
    m["gpk_d"] = np.ascontiguousarray(inp["norm_w"].reshape(DEPTH, 16, 128).transpose(0, 2, 1))
    ch = np.stack([t * 1024 + (4 * hh + hl) * 128 + np.arange(128) for t in range(3) for hl in range(4)])
    m["cw_d"] = np.ascontiguousarray(inp["conv_w"][:, :, ch].transpose(0, 3, 2, 1))
    hp = np.concatenate([inp["a_log"][:, H], inp["dt_bias"][:, H]], axis=1)
    m["hpar_d"] = np.ascontiguousarray(np.broadcast_to(hp[:, None, :], (DEPTH, 128, 8)))
    m["gnw_d"] = np.ascontiguousarray(inp["gdn_norm_w"])
    m["subw_d"] = np.ascontiguousarray(inp["da_subln_w"])
    m["lam_d"] = np.ascontiguousarray(np.stack([inp["lambda_q1"], inp["lambda_k1"],
                                                inp["lambda_q2"], inp["lambda_k2"]], axis=1))
    m["fnw_d"] = np.ascontiguousarray(inp["final_norm_w"].reshape(1, D))
    rb = inp["rel_bias"]
    m["braw_d"] = np.ascontiguousarray(np.stack([rb[:, h][bidx] for h in H]).astype(np.float32))
    m["mpat_d"] = mpat
    m["cfar_d"] = np.ascontiguousarray(np.broadcast_to(rb[31, H][None, :], (128, 4)))
    m["cmat_d"] = cm
    return {k: np.asarray(v, dtype=np.float32) if v.dtype != np.float32 else v for k, v in m.items()}


_PROGS = {}


def _prog(mode):
    if mode not in _PROGS:
        _PROGS[mode] = build_program(mode)
    return _PROGS[mode]


_COMMON = ["gpk_d", "cw_d", "hpar_d", "gnw_d", "subw_d", "lam_d", "fnw_d", "braw_d", "mpat_d", "cfar_d", "cmat_d"]
FUSED = False


def kernel(**inputs):
    inp = {k: np.asarray(v) for k, v in inputs.items()}
    x = np.ascontiguousarray(inp["x"], dtype=np.float32)
    cores = [(b, hh) for b in range(NB) for hh in range(2)]
    prep = [prep_core_inputs(inp, b, hh) for (b, hh) in cores]
    ids = list(range(8))
    out = np.empty((NB, SEQ, D), np.float32)
    if FUSED:
        maps = []
        for ci, (b, hh) in enumerate(cores):
            m = {k: prep[ci][k] for k in _COMMON}
            for l in range(DEPTH):
                m[f"win{l}"] = prep[ci][f"win{l}"]
                m[f"wout{l}"] = prep[ci][f"wout{l}"]
            m["x_in"] = x[b]
            maps.append(m)
        res = run_bass_kernel_spmd(_prog("fused"), maps, core_ids=ids).results
        for ci, (b, hh) in enumerate(cores):
            out[b, hh * 2048:(hh + 1) * 2048] = np.asarray(res[ci]["out"])[hh * 2048:(hh + 1) * 2048]
        return out
    maps = []
    for ci, (b, hh) in enumerate(cores):
        m = {k: prep[ci][k] for k in _COMMON}
        m["win0"] = prep[ci]["win0"]
        m["x_in"] = x[b]
        maps.append(m)
    r1 = run_bass_kernel_spmd(_prog("p1"), maps, core_ids=ids).results
    mixf0 = [np.stack([np.asarray(r1[2 * b]["mixh"]), np.asarray(r1[2 * b + 1]["mixh"])]) for b in range(NB)]
    maps = []
    for ci, (b, hh) in enumerate(cores):
        m = {k: prep[ci][k] for k in _COMMON}
        m["win1"] = prep[ci]["win1"]
        m["wout0"] = prep[ci]["wout0"]
        m["x_in"] = x[b]
        m["mixf_in"] = mixf0[b]
        maps.append(m)
    r2 = run_bass_kernel_spmd(_prog("p2"), maps, core_ids=ids).results
    mixf1 = [np.stack([np.asarray(r2[2 * b]["mixh"]), np.asarray(r2[2 * b + 1]["mixh"])]) for b in range(NB)]
    maps = []
    for ci, (b, hh) in enumerate(cores):
        m = {k: prep[ci][k] for k in _COMMON}
        m["wout1"] = prep[ci]["wout1"]
        m["x_in"] = np.ascontiguousarray(np.asarray(r2[ci]["x1"])[hh * 2048:(hh + 1) * 2048])
        m["mixf_in"] = np.ascontiguousarray(mixf1[b][:, hh * 2048:(hh + 1) * 2048, :])
        maps.append(m)
    r3 = run_bass_kernel_spmd(_prog("p3"), maps, core_ids=ids).results
    for ci, (b, hh) in enumerate(cores):
        out[b, hh * 2048:(hh + 1) * 2048] = np.asarray(r3[ci]["out"])
    return out
```

```python
import math
import contextlib
import numpy as np
import ml_dtypes
import concourse.bass as bass
import concourse.mybir as mybir
from concourse.bass_utils import run_bass_kernel_spmd

F32 = mybir.dt.float32
BF16 = mybir.dt.bfloat16
AF = mybir.ActivationFunctionType
ALU = mybir.AluOpType

D = 2048
SEQ = 4096
NB = 4
DEPTH = 2
NCOL = 4104
NT = SEQ // 128
RMS_EPS = 1e-6
SCALE = 0.125
NEG = -30000.0

SEM_EPOCH = 24000
N_LANES = 8


class T:
    __slots__ = ("name", "lw", "rd", "excl")

    def __init__(self, name="", excl=False):
        self.name = name
        self.lw = None
        self.rd = []
        self.excl = excl


def TP():
    return T(excl=True)


class Ctx:
    def __init__(self, nc, es):
        self.nc = nc
        self.es = es
        self.engs = {"pe": nc.tensor, "act": nc.scalar, "dve": nc.vector,
                     "pool": nc.gpsimd, "sp": nc.sync}
        self.sems = {}
        self.cur = {}
        self.epoch = {e: 0 for e in self.engs}
        self.seen = {e: {} for e in self.engs}
        for e in self.engs:
            self._new_epoch(e)
        self.lanes = {}
        self.lane_rr = {}
        self.n_wait = 0
        self.n_ins = 0
        self.uid = 0

    def _mksem(self, key):
        h = self.es.enter_context(self.nc.semaphore(key))
        self.sems[key] = h
        return h

    def _new_epoch(self, e):
        key = f"s_{e}_{self.epoch[e]}"
        self.epoch[e] += 1
        self._mksem(key)
        self.cur[e] = [key, 0]

    def _lanes(self, q):
        if q not in self.lanes:
            self.lanes[q] = []
            for i in range(N_LANES):
                key = f"l_{q}_{i}"
                self._mksem(key)
                self.lanes[q].append([key, 0])
            self.lane_rr[q] = 0
        return self.lanes[q]

    def _wait(self, e, ev):
        if ev is None:
            return
        key, val = ev
        if e == "pe" and key.startswith("s_pe_"):
            return
        if self.seen[e].get(key, 0) >= val:
            return
        self.engs[e].wait_ge(self.sems[key], val)
        self.seen[e][key] = val
        self.n_wait += 1

    def _deps(self, e, reads, writes):
        for t in reads:
            self._wait(e, t.lw)
            if t.excl:
                for ev in t.rd:
                    if ev[0].split("_")[1] != e:
                        self._wait(e, ev)
        for t in writes:
            self._wait(e, t.lw)
            for ev in t.rd:
                self._wait(e, ev)

    def _record(self, ev, reads, writes):
        for t in reads:
            t.rd = [r for r in t.rd if r[0] != ev[0]] + [ev]
        for t in writes:
            t.lw = ev
            t.rd = []

    def op(self, e, fn, reads=(), writes=()):
        self._deps(e, reads, writes)
        ins = fn(self.engs[e])
        c = self.cur[e]
        c[1] += 1
        ins.then_inc(self.sems[c[0]], 1)
        ev = (c[0], c[1])
        self._record(ev, reads, writes)
        self.n_ins += 1
        if c[1] >= SEM_EPOCH:
            self._new_epoch(e)
        return ev

    def dma(self, q, out, in_, reads=(), writes=(), **kw):
        lanes = self._lanes(q)
        i = self.lane_rr[q]
        self.lane_rr[q] = (i + 1) % N_LANES
        lane = lanes[i]
        if lane[1] > 0:
            self._wait(q, (lane[0], lane[1]))
        self._deps(q, reads, writes)
        ins = self.engs[q].dma_start(out=out, in_=in_, **kw)
        lane[1] += 16
        ins.then_inc(self.sems[lane[0]], 16)
        ev = (lane[0], lane[1])
        self._record(ev, reads, writes)
        self.n_ins += 1
        return ev

    def collective(self, fn, reads=(), writes=()):
        if "cc" not in self.sems:
            self._mksem("cc")
            self.cc_count = 0
        self._deps("pool", reads, writes)
        ins = fn(self.engs["pool"])
        self.cc_count += 1
        ins.then_inc(self.sems["cc"])
        ev = ("cc", self.cc_count)
        self._record(ev, reads, writes)
        self.n_ins += 1
        return ev

    def all_events(self):
        evs = []
        for e in self.engs:
            for ep in range(self.epoch[e]):
                key = f"s_{e}_{ep}"
                val = self.cur[e][1] if key == self.cur[e][0] else SEM_EPOCH
                if val > 0:
                    evs.append((key, val))
        for q, lanes in self.lanes.items():
            for lane in lanes:
                if lane[1] > 0:
                    evs.append((lane[0], lane[1]))
        if "cc" in self.sems and self.cc_count > 0:
            evs.append(("cc", self.cc_count))
        return evs

    def barrier(self):
        evs = self.all_events()
        for e in self.engs:
            for ev in evs:
                if ev[0] == self.cur[e][0]:
                    continue
                self._wait(e, ev)

    def finish(self, e="sp"):
        for ev in self.all_events():
            if ev[0] == self.cur[e][0]:
                continue
            self._wait(e, ev)


class Ring:
    def __init__(self, items):
        self.items = items
        self.i = 0

    def next(self):
        it = self.items[self.i]
        self.i = (self.i + 1) % len(self.items)
        return it


def _alt(i):
    return "act" if (i % 2) else "dve"


def evac_copy(c, eng, out, in_, reads, writes):
    if eng == "act":
        return c.op("act", lambda e: e.copy(out=out, in_=in_), reads=reads, writes=writes)
    return c.op(eng, lambda e: e.tensor_copy(out=out, in_=in_), reads=reads, writes=writes)


def evac_scale(c, eng, out, in_, sc, reads, writes):
    if eng == "act":
        return c.op("act", lambda e: e.mul(out=out, in_=in_, mul=sc), reads=reads, writes=writes)
    return c.op("dve", lambda e: e.tensor_scalar(out=out, in0=in_, scalar1=sc, scalar2=None, op0=ALU.mult),
                reads=reads, writes=writes)


def rsqrt_chain(c, out, in_, scale, epsb, tmp, reads, writes, tmpT):
    c.op("act", lambda e: e.activation(out=tmp, in_=in_, func=AF.Ln, bias=epsb, scale=scale),
         reads=reads, writes=[tmpT])
    c.op("act", lambda e: e.activation(out=out, in_=tmp, func=AF.Exp, scale=-0.5),
         reads=[tmpT], writes=writes)


def stage_inproj(c, nc, G, l, x_src, Tx_src):
    Wbf = G["Wbf"][l]
    TW = G["TWbf"][l]
    with contextlib.ExitStack() as es:
        sb = lambda n, s, d: es.enter_context(nc.sbuf_tensor(f"{n}_L{l}", s, d))
        ps = lambda n, s, d: es.enter_context(nc.psum_tensor(f"{n}_L{l}", s, d))
        xt = [(sb(f"ip_x{i}", [128, D], F32), T()) for i in range(3)]
        junk = sb("ip_junk", [128, D], BF16)
        Tjunk = T()
        st = sb("ip_st", [128, NT, 4], F32)
        Tst = [T() for _ in range(NT)]
        xn = [(sb(f"ip_xn{i}", [128, 4, D], BF16), T()) for i in range(2)]
        hT = [(sb(f"ip_hT{i}", [128, 16, 512], BF16), T()) for i in range(2)]
        wb = Ring([(sb(f"ip_w{i}", [128, 16, 512], BF16), T()) for i in range(3)])
        wl = [(sb(f"ip_wl{i}", [128, 16, 8], BF16), T()) for i in range(2)]
        sg_b = Ring([(sb(f"ip_sb{i}", [128, 512], BF16), T()) for i in range(4)])
        sg_f = Ring([(sb(f"ip_sf{i}", [128, 512], F32), T()) for i in range(4)])
        trp = Ring([(ps(f"ip_tr{i}", [128, 1024], BF16)[:, 0:512], TP()) for i in range(2)])
        mmp = Ring([(ps(f"ip_mm{i}", [128, 512], F32), TP()) for i in range(6)])
        gpk, Tgpk = G["gpk"][l]
        identb, Tidb = G["identb"]
        epsb = G["cst"][0][:, 0:1]
        Tcst = G["cst"][1]

        c.op("pool", lambda e: e.memset(st[:], 0.0), writes=Tst)

        def load_x(blk):
            for j in range(4):
                t = blk * 4 + j
                xa, Txa = xt[t % 3]
                c.dma("sp", xa[:], x_src[t * 128:(t + 1) * 128, :], reads=[Tx_src], writes=[Txa])

        def load_w(cb):
            if cb < 8:
                w, Tw = wb.next()
                c.dma("sp", w[:].rearrange("p k c -> p (k c)"), Wbf[:, cb * 8192:(cb + 1) * 8192],
                      reads=[TW[cb]], writes=[Tw])
            else:
                w, Tw = wl[load_w.n8 % 2]
                load_w.n8 += 1
                c.dma("sp", w[:].rearrange("p k c -> p (k c)"), Wbf[:, 65536:16 * NCOL], reads=[TW[8]], writes=[Tw])
            return w, Tw
        load_w.n8 = 0

        nev = 0

        def prologue(blk):
            xnb, Txn = xn[blk % 2]
            for j in range(4):
                t = blk * 4 + j
                xa, Txa = xt[t % 3]
                c.dma("sp", xa[:], x_src[t * 128:(t + 1) * 128, :], reads=[Tx_src], writes=[Txa])
                c.op("act", lambda e: e.activation(out=junk[:], in_=xa[:], func=AF.Square,
                                                   accum_out=st[:, t, 0:1]),
                     reads=[Txa], writes=[Tjunk, Tst[t]])
                rsqrt_chain(c, st[:, t, 2:3], st[:, t, 0:1], 1.0 / D, epsb, st[:, t, 1:2],
                            [Tst[t], Tcst], [Tst[t]], Tst[t])
                c.op("dve", lambda e: e.tensor_scalar(out=xnb[:, j, :], in0=xa[:], scalar1=st[:, t, 2:3],
                                                      scalar2=None, op0=ALU.mult),
                     reads=[Txa, Tst[t]], writes=[Txn])

        def transposes(blk):
            xnb, Txn = xn[blk % 2]
            hTb, ThT = hT[blk % 2]
            for k in range(16):
                p, Tp = trp.next()
                for j in range(4):
                    c.op("pe", lambda e: e.transpose(p[:, j * 128:(j + 1) * 128],
                                                     xnb[:, j, k * 128:(k + 1) * 128], identb[:]),
                         reads=[Txn, Tidb], writes=[Tp])
                evac_scale(c, _alt(k), hTb[:, k, :], p[:], gpk[:, k:k + 1], [Tp, Tgpk], [ThT])

        prologue(0)
        transposes(0)
        for blk in range(8):
            hTb, ThT = hT[blk % 2]
            if blk + 1 < 8:
                prologue(blk + 1)
            nxt = load_w(0)
            for cb in range(9):
                w, Tw = nxt
                if cb + 1 < 9:
                    nxt = load_w(cb + 1)
                if cb < 5:
                    for m in range(4):
                        p, Tp = mmp.next()
                        for k in range(16):
                            c.op("pe", lambda e: e.matmul(p[:], lhsT=w[:, k, m * 128:(m + 1) * 128],
                                                          rhs=hTb[:, k, :], start=(k == 0), stop=(k == 15)),
                                 reads=[Tw, ThT], writes=[Tp])
                        if cb < 2:
                            s, Ts = sg_b.next()
                            dst = G["QKT"][0][cb * 4 + m, :, blk * 512:(blk + 1) * 512]
                            Tdst = G["QKT"][1]
                        else:
                            s, Ts = sg_f.next()
                            dst = G["GQKV"][0][(cb - 2) * 4 + m, :, blk * 512:(blk + 1) * 512]
                            Tdst = G["GQKV"][1]
                        evac_copy(c, _alt(nev), s[:], p[:], [Tp], [Ts])
                        nev += 1
                        c.dma("pool", dst, s[:], reads=[Ts], writes=[Tdst])
                elif cb < 8:
                    for j in range(4):
                        t = blk * 4 + j
                        p, Tp = mmp.next()
                        for k in range(16):
                            c.op("pe", lambda e: e.matmul(p[:], lhsT=hTb[:, k, j * 128:(j + 1) * 128],
                                                          rhs=w[:, k, :], start=(k == 0), stop=(k == 15)),
                                 reads=[Tw, ThT], writes=[Tp])
                        if cb == 5:
                            s, Ts = sg_b.next()
                            dst, Tdst = G["DAV"][0][t * 128:(t + 1) * 128, :], G["DAV"][1]
                        elif cb == 6:
                            s, Ts = sg_f.next()
                            dst, Tdst = G["DAG"][0][t * 128:(t + 1) * 128, :], G["DAG"][1]
                        else:
                            s, Ts = sg_f.next()
                            dst, Tdst = G["GZ"][0][t * 128:(t + 1) * 128, :], G["GZ"][1]
                        evac_copy(c, _alt(nev), s[:], p[:], [Tp], [Ts])
                        nev += 1
                        c.dma("pool", dst, s[:], reads=[Ts], writes=[Tdst])
                else:
                    for j in range(4):
                        t = blk * 4 + j
                        p, Tp = mmp.next()
                        for k in range(16):
                            c.op("pe", lambda e: e.matmul(p[:, 0:8], lhsT=hTb[:, k, j * 128:(j + 1) * 128],
                                                          rhs=w[:, k, :], start=(k == 0), stop=(k == 15)),
                                 reads=[Tw, ThT], writes=[Tp])
                        s, Ts = sg_f.next()
                        evac_copy(c, _alt(nev), s[:, 0:8], p[:, 0:8], [Tp], [Ts])
                        nev += 1
                        c.dma("pool", G["GBA"][0][t * 128:(t + 1) * 128, :], s[:, 0:8], reads=[Ts],
                              writes=[G["GBA"][1]])
            if blk + 1 < 8:
                transposes(blk + 1)
        c.barrier()


def stage_outproj(c, nc, G, l, x_src, Tx_src, mixf, Tmixf, dst, Tdst, tok0, ntile, final):
    Wo = G["Wobf"][l]
    TWo = G["TWobf"][l]
    with contextlib.ExitStack() as es:
        sb = lambda n, s, d: es.enter_context(nc.sbuf_tensor(f"{n}_L{l}", s, d))
        ps = lambda n, s, d: es.enter_context(nc.psum_tensor(f"{n}_L{l}", s, d))
        wo = sb("op_w", [128, 16, D], BF16)
        Two = T()
        mt = [(sb(f"op_m{i}", [128, D], BF16), T()) for i in range(2)]
        mT = [(sb(f"op_mT{i}", [128, 16, 128], BF16), T()) for i in range(2)]
        xt = [(sb(f"op_x{i}", [128, D], F32), T()) for i in range(2)]
        ot = [(sb(f"op_o{i}", [128, D], F32), T()) for i in range(2)]
        junk = sb("op_junk", [128, D], BF16)
        Tjunk = T()
        st = sb("op_st", [128, NT, 4], F32)
        Tst = [T() for _ in range(NT)]
        fnw = sb("op_fnw", [128, D], F32)
        Tfnw = T()
        trp = Ring([(ps(f"op_tr{i}", [128, 1024], BF16)[:, 0:512], TP()) for i in range(2)])
        mmp = Ring([(ps(f"op_mm{i}", [128, 512], F32), TP()) for i in range(6)])
        identb, Tidb = G["identb"]
        epsb = G["cst"][0][:, 0:1]
        Tcst = G["cst"][1]
        for k4 in range(4):
            c.dma("sp", wo[:, k4 * 4:(k4 + 1) * 4, :], Wo[:, k4 * 4:(k4 + 1) * 4, :], reads=TWo[k4 * 4:(k4 + 1) * 4], writes=[Two])
        if final:
            c.dma("sp", fnw[:], G["fnw_d"][0:1, :].to_broadcast([128, D]), writes=[Tfnw])
            c.op("pool", lambda e: e.memset(st[:], 0.0), writes=Tst)
        nev = 0
        for i in range(ntile):
            tok = tok0 + i * 128
            m, Tm = mt[i % 2]
            mTt, TmT = mT[i % 2]
            xa, Txa = xt[i % 2]
            o, To = ot[i % 2]
            for r in range(2):
                src_ap = mixf(r, tok) if callable(mixf) else mixf[r, tok:tok + 128, :]
                c.dma("sp", m[:, r * 1024:(r + 1) * 1024], src_ap, reads=[Tmixf], writes=[Tm])
            c.dma("sp", xa[:], x_src[tok:tok + 128, :], reads=[Tx_src], writes=[Txa])
            for k4 in range(4):
                p, Tp = trp.next()
                for j in range(4):
                    k = k4 * 4 + j
                    c.op("pe", lambda e: e.transpose(p[:, j * 128:(j + 1) * 128], m[:, k * 128:(k + 1) * 128],
                                                     identb[:]),
                         reads=[Tm, Tidb], writes=[Tp])
                evac_copy(c, _alt(k4), mTt[:, k4 * 4:(k4 + 1) * 4, :],
                          p[:].rearrange("p (j t) -> p j t", j=4), [Tp], [TmT])
            for nb in range(4):
                p, Tp = mmp.next()
                for k in range(16):
                    c.op("pe", lambda e: e.matmul(p[:], lhsT=mTt[:, k, :], rhs=wo[:, k, nb * 512:(nb + 1) * 512],
                                                  start=(k == 0), stop=(k == 15)),
                         reads=[TmT, Two], writes=[Tp])
                c.op("dve", lambda e: e.tensor_tensor(out=o[:, nb * 512:(nb + 1) * 512], in0=p[:],
                                                      in1=xa[:, nb * 512:(nb + 1) * 512], op=ALU.add),
                     reads=[Tp, Txa], writes=[To])
            if not final:
                c.dma("pool", dst[tok:tok + 128, :], o[:], reads=[To], writes=[Tdst])
            else:
                c.op("act", lambda e: e.activation(out=junk[:], in_=o[:], func=AF.Square,
                                                   accum_out=st[:, i, 0:1]),
                     reads=[To], writes=[Tjunk, Tst[i]])
                rsqrt_chain(c, st[:, i, 2:3], st[:, i, 0:1], 1.0 / D, epsb, st[:, i, 1:2],
                            [Tst[i], Tcst], [Tst[i]], Tst[i])
                c.op("dve", lambda e: e.scalar_tensor_tensor(out=xa[:], in0=o[:], scalar=st[:, i, 2:3],
                                                             in1=fnw[:], op0=ALU.mult, op1=ALU.mult),
                     reads=[To, Tst[i], Tfnw], writes=[Txa])
                c.dma("pool", dst[i * 128:(i + 1) * 128, :], xa[:], reads=[Txa], writes=[Tdst])
        c.barrier()


def stage_da(c, nc, G, l):
    lam_init = 0.8 - 0.6 * math.exp(-0.3 * l)
    QKT, TQKT = G["QKT"]
    DAV, TDAV = G["DAV"]
    DAG, TDAG = G["DAG"]
    MIXH, TMIXH = G["MIXH"]
    DAVr = DAV.rearrange("(kb p) c -> p kb c", p=128)
    DAGr = DAG.rearrange("(j p) c -> p j c", p=128)
    MIXr = MIXH.rearrange("(j p) c -> p j c", p=128)
    with contextlib.ExitStack() as es:
        sb = lambda n, s, d: es.enter_context(nc.sbuf_tensor(f"{n}_L{l}", s, d))
        ps = lambda n, s, d: es.enter_context(nc.psum_tensor(f"{n}_L{l}", s, d))
        KT = [(sb(f"da_kt{i}", [128, SEQ], BF16), T()) for i in range(2)]
        QT = [(sb(f"da_qt{i}", [128, SEQ], BF16), T()) for i in range(2)]
        V = [(sb(f"da_v{i}", [128, NT, 129], BF16), T()) for i in range(2)]
        braw = sb("da_braw", [128, 5, 512], F32)
        Tbraw = T()
        mpat = sb("da_mpat", [128, 5, 512], F32)
        Tmpat = T()
        biasT = [(sb(f"da_bias{i}", [128, 5, 512], BF16), T()) for i in range(2)]
        ering = Ring([(sb(f"da_e{i}", [128, 512], BF16), T()) for i in range(4)])
        om = [(sb(f"da_om{i}", [128, 4, 128], F32), T()) for i in range(2)]
        rec = [(sb(f"da_rec{i}", [128, 4], F32), T()) for i in range(2)]
        dlt = sb("da_dlt", [128, 4, 128], F32)
        Tdlt = T()
        junk = sb("da_junk", [128, 128], BF16)
        Tjunk = T()
        ss = sb("da_ss", [128, 8, 4], F32)
        Tss = T()
        gate = [(sb(f"da_g{i}", [128, 4, 128], F32), T()) for i in range(2)]
        gm = sb("da_gm", [128, 4, 128], F32)
        Tgm = T()
        tmp = sb("da_tmp", [128, 4, 128], F32)
        Ttmp = T()
        fin = [(sb(f"da_fin{i}", [128, 4, 128], BF16), T()) for i in range(2)]
        wsub = sb("da_wsub", [128, 4, 128], F32)
        Twsub = T()
        lamv = sb("da_lamv", [128, 4, 64], F32)
        Tlamv = T()
        lamp = sb("da_lamp", [128, 2, 64], F32)
        lams = sb("da_lams", [128, 8], F32)
        Tlams = T()
        sring = Ring([(ps(f"da_s{i}", [128, 512], F32), TP()) for i in range(3)])
        Oacc = [[(ps(f"da_o{m}{b}", [128, 512], F32)[:, 0:258].rearrange("p (a b) -> p a b", a=2), TP())
                 for b in range(2)] for m in range(2)]
        identb, Tidb = G["identb"]
        cst, Tcst = G["cst"]
        epsb = cst[:, 0:1]
        zerob = cst[:, 1:2]
        cfar, Tcfar = G["cfar"]

        c.dma("sp", lamv[:], G["lam_d"][l:l + 1, :, :].to_broadcast([128, 4, 64]), writes=[Tlamv])
        c.op("pool", lambda e: e.memset(lams[:], 0.0), writes=[Tlams])
        c.op("dve", lambda e: e.tensor_tensor(out=lamp[:, 0, :], in0=lamv[:, 0, :], in1=lamv[:, 1, :], op=ALU.mult),
             reads=[Tlamv], writes=[Tlamv])
        c.op("dve", lambda e: e.tensor_tensor(out=lamp[:, 1, :], in0=lamv[:, 2, :], in1=lamv[:, 3, :], op=ALU.mult),
             reads=[Tlamv], writes=[Tlamv])
        for i in range(2):
            c.op("act", lambda e: e.activation(out=lamv[:, i, :], in_=lamp[:, i, :], func=AF.Identity,
                                               accum_out=lams[:, i:i + 1]),
                 reads=[Tlamv], writes=[Tlamv, Tlams])
        c.op("act", lambda e: e.activation(out=lams[:, 2:4], in_=lams[:, 0:2], func=AF.Exp),
             reads=[Tlams], writes=[Tlams])
        c.op("dve", lambda e: e.tensor_tensor(out=lams[:, 4:5], in0=lams[:, 3:4], in1=lams[:, 2:3], op=ALU.subtract),
             reads=[Tlams], writes=[Tlams])
        c.op("dve", lambda e: e.tensor_scalar(out=lams[:, 5:6], in0=lams[:, 4:5], scalar1=-lam_init, scalar2=None,
                                              op0=ALU.add),
             reads=[Tlams], writes=[Tlams])
        neglam = lams[:, 5:6]
        c.dma("sp", wsub[:], G["subw_d"][l:l + 1, :].unsqueeze(1).to_broadcast([128, 4, 128]), writes=[Twsub])
        c.op("dve", lambda e: e.tensor_scalar(out=wsub[:], in0=wsub[:], scalar1=1.0 - lam_init, scalar2=None,
                                              op0=ALU.mult),
             reads=[Twsub], writes=[Twsub])
        c.dma("sp", mpat[:], G["mpat_d"].rearrange("a p q -> p a q"), writes=[Tmpat])
        for i in range(2):
            c.op("pool", lambda e: e.memset(V[i][0][:, :, 128:129], 1.0), writes=[V[i][1]])

        def load_head(hl):
            kt, Tkt = KT[hl % 2]
            qt, Tqt = QT[hl % 2]
            v, Tv = V[hl % 2]
            bT, TbT = biasT[hl % 2]
            c.dma("sp", kt[:], QKT[4 + hl, :, :], reads=[TQKT], writes=[Tkt])
            c.dma("sp", qt[:], QKT[hl, :, :], reads=[TQKT], writes=[Tqt])
            for a in range(4):
                c.dma("sp", v[:, a * 8:(a + 1) * 8, 0:128], DAVr[:, a * 8:(a + 1) * 8, hl * 128:(hl + 1) * 128],
                      reads=[TDAV], writes=[Tv])
            c.dma("sp", braw[:], G["braw_d"][hl].rearrange("a p q -> p a q"), writes=[Tbraw])
            c.op("dve", lambda e: e.scalar_tensor_tensor(out=bT[:], in0=braw[:], scalar=1.0 / SCALE, in1=mpat[:],
                                                         op0=ALU.mult, op1=ALU.add),
                 reads=[Tbraw, Tmpat], writes=[TbT])

        load_head(0)
        it = 0

        def emit_qk(item):
            hl, qb, m, kb = item
            kt, Tkt = KT[hl % 2]
            qt, Tqt = QT[hl % 2]
            bT, TbT = biasT[hl % 2]
            r0 = 64 * m
            sp_, Tsp = sring.next()
            delta = kb * 128 - qb * 512
            special = delta >= -128
            j0 = max(0, kb - 4 * qb)
            c.op("pe", lambda e: e.matmul(sp_[:, j0 * 128:512], lhsT=kt[r0:r0 + 64, kb * 128:(kb + 1) * 128],
                                          rhs=qt[r0:r0 + 64, qb * 512 + j0 * 128:(qb + 1) * 512],
                                          start=True, stop=not special),
                 reads=[Tkt, Tqt], writes=[Tsp])
            if special:
                pat = (delta + 128) // 128
                c.op("pe", lambda e: e.matmul(sp_[:, j0 * 128:512], lhsT=identb[:], rhs=bT[:, pat, j0 * 128:512],
                                              start=False, stop=True),
                     reads=[Tidb, TbT], writes=[Tsp])
            return (sp_, Tsp, special, j0)

        def emit_rest(item, qkres):
            hl, qb, m, kb = item
            v, Tv = V[hl % 2]
            sp_, Tsp, special, j0 = qkres
            E, TE = ering.next()
            bias_ap = zerob if special else cfar[:, hl:hl + 1]
            c.op("act", lambda e: e.activation(out=E[:, j0 * 128:512], in_=sp_[:, j0 * 128:512], func=AF.Exp,
                                               bias=bias_ap, scale=SCALE),
                 reads=[Tsp, Tcst, Tcfar], writes=[TE])
            for j in range(j0, 4):
                acc, Tacc = Oacc[m][j // 2]
                c.op("pe", lambda e: e.matmul(acc[:, j % 2, :], lhsT=E[:, j * 128:(j + 1) * 128],
                                              rhs=v[:, kb, :], start=(kb == 0 and j % 2 == 0),
                                              stop=(kb == qb * 4 + j)),
                     reads=[TE, Tv], writes=[Tacc])

        def finish_map(m):
            o_m, Tom = om[m]
            rc, Trc = rec[m]
            for b in range(2):
                acc, Tacc = Oacc[m][b]
                c.op("dve", lambda e: e.reciprocal(out=rc[:, 2 * b:2 * b + 2], in_=acc[:, :, 128]),
                     reads=[Tacc], writes=[Trc])
                c.op("dve", lambda e: e.tensor_tensor(
                    out=o_m[:, 2 * b:2 * b + 2, :], in0=acc[:, :, 0:128],
                    in1=rc[:, 2 * b:2 * b + 2].unsqueeze(2).to_broadcast([128, 2, 128]), op=ALU.mult),
                     reads=[Tacc, Trc], writes=[Tom])

        def finish_qb(hl, qb, g, Tg, f, Tf):
            c.op("dve", lambda e: e.scalar_tensor_tensor(out=dlt[:], in0=om[1][0][:], scalar=neglam,
                                                         in1=om[0][0][:], op0=ALU.mult, op1=ALU.add),
                 reads=[om[0][1], om[1][1], Tlams], writes=[Tdlt])
            c.op("pool", lambda e: e.memset(ss[:, 0:4, 0], 0.0), writes=[Tss])
            for j in range(4):
                c.op("act", lambda e: e.activation(out=junk[:], in_=dlt[:, j, :], func=AF.Square,
                                                   accum_out=ss[:, j, 0:1]),
                     reads=[Tdlt], writes=[Tjunk, Tss])
            rsqrt_chain(c, ss[:, 4:8, 0], ss[:, 0:4, 0], 1.0 / 128, epsb, ss[:, 0:4, 1],
                        [Tss, Tcst], [Tss], Tss)
            c.op("act", lambda e: e.activation(out=gm[:], in_=g[:], func=AF.Silu), reads=[Tg], writes=[Tgm])
            c.op("pool", lambda e: e.tensor_tensor(out=gm[:], in0=gm[:], in1=wsub[:], op=ALU.mult),
                 reads=[Tgm, Twsub], writes=[Tgm])
            c.op("dve", lambda e: e.tensor_tensor(out=tmp[:], in0=dlt[:],
                                                  in1=ss[:, 4:8, 0:1].to_broadcast([128, 4, 128]), op=ALU.mult),
                 reads=[Tdlt, Tss], writes=[Ttmp])
            c.op("dve", lambda e: e.tensor_tensor(out=f[:], in0=tmp[:], in1=gm[:], op=ALU.mult),
                 reads=[Ttmp, Tgm], writes=[Tf])
            c.dma("pool", MIXr[:, qb * 4:(qb + 1) * 4, hl * 128:(hl + 1) * 128], f[:], reads=[Tf], writes=[TMIXH])

        items = [(hl, qb, m, kb) for hl in range(4) for qb in range(8) for m in range(2)
                 for kb in range(4 * qb + 4)]
        qkres = emit_qk(items[0])
        cur_g = None
        deferred = []
        for idx, item in enumerate(items):
            hl, qb, m, kb = item
            for dfr in deferred:
                dfr[0] -= 1
            while deferred and deferred[0][0] <= 0:
                deferred.pop(0)[1]()
            if kb == 0 and m == 0:
                if qb == 0 and hl + 1 < 4:
                    load_head(hl + 1)
                cur_g = (gate[it % 2], fin[it % 2])
                it += 1
                if G["pending"]:
                    G["pending"].pop(0)()
                c.dma("sp", cur_g[0][0][:], DAGr[:, qb * 4:(qb + 1) * 4, hl * 128:(hl + 1) * 128],
                      reads=[TDAG], writes=[cur_g[0][1]])
            nxt = emit_qk(items[idx + 1]) if idx + 1 < len(items) else None
            emit_rest(item, qkres)
            qkres = nxt
            if kb == 4 * qb + 3:
                finish_map(m)
                if m == 1:
                    deferred.append([3, (lambda a=(hl, qb, cur_g[0][0], cur_g[0][1], cur_g[1][0], cur_g[1][1]):
                                         finish_qb(*a))])
        while deferred:
            deferred.pop(0)[1]()
        while G["pending"]:
            G["pending"].pop(0)()
        c.barrier()


class _Stop(Exception):
    pass


def _chk(level):
    import os
    if int(os.environ.get("GDN_STOP", "99")) == level:
        _chk.c.barrier()
        raise _Stop()


def stage_gdn(c, nc, G, l):
    with contextlib.ExitStack() as es:
        try:
            _stage_gdn(c, nc, G, l, es)
        except _Stop:
            pass


def _stage_gdn(c, nc, G, l, es):
    _chk.c = c
    GQKV, TGQKV = G["GQKV"]
    GZ, TGZ = G["GZ"]
    GBA, TGBA = G["GBA"]
    MIXH, TMIXH = G["MIXH"]
    if True:
        sb = lambda n, s, d: es.enter_context(nc.sbuf_tensor(f"{n}_L{l}", s, d))
        ps = lambda n, s, d: es.enter_context(nc.psum_tensor(f"{n}_L{l}", s, d))
        cst, Tcst = G["cst"]
        epsb = cst[:, 0:1]
        oneb = cst[:, 2:3]
        identb, Tidb = G["identb"]
        identf, Tidf = G["identf"]
        Uf, TUf = G["Uf"]
        onesf, Tonesf = G["onesf"]
        negonesf, Tnegonesf = G["negonesf"]
        onesb, Tonesb = G["onesb"]
        negmask, Tnegmask = G["negmask"]
        strict, Tstrict = G["strict"]
        cw, Tcw = G["cw"][l]
        hpar, Thpar = G["hpar"][l]

        pring = Ring([(ps(f"gd_p{i}", [128, 4, 128], F32), TP()) for i in range(3)])
        sbanks = [ps(f"gd_scan{i}", [128, 4, 128], F32) for i in range(3)]
        Tsb = [TP() for _ in range(3)]
        ws_ps = [(sbanks[0][:, h, :], Tsb[0]) for h in range(4)]
        O_ps = [(sbanks[1][:, h, :], Tsb[1]) for h in range(4)]
        Sd_ps = [(sbanks[2][:, h, :], Tsb[2]) for h in range(4)]
        trbank = ps("gd_tr", [128, 8, 128], BF16)
        Ttr = TP()
        ssq_ps = ps("gd_ssq", [128, 512], F32)
        Tssq = TP()

        ba = sb("gd_ba", [128, NT, 8], F32)
        Tba = T()
        sc = {}
        for nm in ("beta", "negb", "xa", "nx", "mn", "ex", "lg", "mx", "g", "gc", "eg", "egl", "ekd", "dd"):
            sc[nm] = sb("gd_sc_" + nm, [128, NT, 4], F32)
        Tsc = T()
        nea = sb("gd_nea", [128, 4], F32)
        GBAr = GBA.rearrange("(n p) j -> p n j", p=128)
        for a in range(8):
            c.dma("sp", ba[:, a * 4:(a + 1) * 4, :], GBAr[:, a * 4:(a + 1) * 4, :], reads=[TGBA], writes=[Tba])
        c.op("act", lambda e: e.activation(out=sc["beta"][:], in_=ba[:, :, 0:4], func=AF.Sigmoid),
             reads=[Tba], writes=[Tsc])
        c.op("dve", lambda e: e.tensor_scalar(out=sc["negb"][:], in0=sc["beta"][:], scalar1=-1.0, scalar2=None,
                                              op0=ALU.mult), reads=[Tsc], writes=[Tsc])
        c.op("dve", lambda e: e.tensor_tensor(out=sc["xa"][:], in0=ba[:, :, 4:8],
                                              in1=hpar[:, 4:8].unsqueeze(1).to_broadcast([128, NT, 4]), op=ALU.add),
             reads=[Tba, Thpar], writes=[Tsc])
        c.op("dve", lambda e: e.tensor_scalar(out=sc["nx"][:], in0=sc["xa"][:], scalar1=-1.0, scalar2=None,
                                              op0=ALU.mult), reads=[Tsc], writes=[Tsc])
        c.op("dve", lambda e: e.tensor_tensor(out=sc["mn"][:], in0=sc["xa"][:], in1=sc["nx"][:], op=ALU.min),
             reads=[Tsc], writes=[Tsc])
        c.op("act", lambda e: e.activation(out=sc["ex"][:], in_=sc["mn"][:], func=AF.Exp), reads=[Tsc], writes=[Tsc])
        c.op("act", lambda e: e.activation(out=sc["lg"][:], in_=sc["ex"][:], func=AF.Ln, bias=oneb),
             reads=[Tsc, Tcst], writes=[Tsc])
        c.op("dve", lambda e: e.tensor_scalar(out=sc["mx"][:], in0=sc["xa"][:], scalar1=0.0, scalar2=None,
                                              op0=ALU.max), reads=[Tsc], writes=[Tsc])
        c.op("dve", lambda e: e.tensor_tensor(out=sc["lg"][:], in0=sc["lg"][:], in1=sc["mx"][:], op=ALU.add),
             reads=[Tsc], writes=[Tsc])
        c.op("act", lambda e: e.activation(out=nea[:], in_=hpar[:, 0:4], func=AF.Exp), reads=[Thpar], writes=[Tsc])
        c.op("dve", lambda e: e.tensor_scalar(out=nea[:], in0=nea[:], scalar1=-1.0, scalar2=None, op0=ALU.mult),
             reads=[Tsc], writes=[Tsc])
        c.op("dve", lambda e: e.tensor_tensor(out=sc["g"][:], in0=sc["lg"][:],
                                              in1=nea[:].unsqueeze(1).to_broadcast([128, NT, 4]), op=ALU.mult),
             reads=[Tsc], writes=[Tsc])
        gflat = sc["g"][:].rearrange("p n h -> p (n h)")
        bkA, TpA = pring.next()
        bkB, TpB = pring.next()
        pA = bkA[:].rearrange("p a b -> p (a b)")[:, 0:128]
        pB = bkB[:].rearrange("p a b -> p (a b)")[:, 0:128]
        c.op("pe", lambda e: e.matmul(pA, lhsT=Uf[:], rhs=gflat, start=True, stop=True),
             reads=[TUf, Tsc], writes=[TpA])
        c.op("pe", lambda e: e.matmul(pB, lhsT=onesf[:], rhs=gflat, start=True, stop=True),
             reads=[Tonesf, Tsc], writes=[TpB])
        fl = lambda nm: sc[nm][:].rearrange("p n h -> p (n h)")
        c.op("dve", lambda e: e.tensor_copy(out=fl("gc"), in_=pA), reads=[TpA], writes=[Tsc])
        c.op("act", lambda e: e.activation(out=fl("eg"), in_=pA, func=AF.Exp), reads=[TpA], writes=[Tsc])
        c.op("act", lambda e: e.activation(out=fl("egl"), in_=pB, func=AF.Exp), reads=[TpB], writes=[Tsc])
        c.op("dve", lambda e: e.tensor_tensor(out=fl("dd"), in0=pB, in1=fl("gc"), op=ALU.subtract),
             reads=[TpB, Tsc], writes=[Tsc])
        c.op("act", lambda e: e.activation(out=fl("ekd"), in_=fl("dd"), func=AF.Exp), reads=[Tsc], writes=[Tsc])

        _chk(1)
        Xr = Ring([(sb(f"gd_X{i}", [128, 515], F32), T()) for i in range(3)])
        yr = Ring([(sb(f"gd_y{i}", [128, 512], F32), T()) for i in range(2)])
        sr = Ring([(sb(f"gd_s{i}", [128, 512], F32), T()) for i in range(2)])
        sqr = Ring([(sb(f"gd_sq{i}", [128, 512], BF16), T()) for i in range(2)])
        rnr = Ring([(sb(f"gd_rn{i}", [128, 512], F32), T()) for i in range(2)])
        lnr = Ring([(sb(f"gd_ln{i}", [128, 512], F32), T()) for i in range(2)])
        qkvT = [[(sb(f"gd_qkv{p}_{t}", [128, 4, 512], BF16), T()) for t in range(3)] for p in range(2)]
        zt = [(sb(f"gd_z{i}", [128, 512], F32), T()) for i in range(2)]
        gzp = [(sb(f"gd_gz{i}", [128, 512], F32), T()) for i in range(2)]
        mixs = [(sb(f"gd_mix{i}", [128, 512], BF16), T()) for i in range(2)]
        gnw4 = sb("gd_gnw4", [128, 4, 128], F32)
        Tgnw4 = T()
        c.dma("sp", gnw4[:], G["gnw_d"][l:l + 1, :].unsqueeze(1).to_broadcast([128, 4, 128]), writes=[Tgnw4])
        S32 = [(sb(f"gd_S32_{h}", [128, 128], F32), T()) for h in range(4)]
        Sb = [(sb(f"gd_Sb_{h}", [128, 128], BF16), T()) for h in range(4)]
        for h in range(4):
            c.op("pool", lambda e: e.memset(S32[h][0][:], 0.0), writes=[S32[h][1]])
            c.op("pool", lambda e: e.memset(Sb[h][0][:], 0.0), writes=[Sb[h][1]])
        ost = sb("gd_ost", [128, NT, 4, 4], F32)
        Tost = [[T() for _ in range(4)] for _ in range(NT)]
        c.op("pool", lambda e: e.memset(ost[:], 0.0), writes=[t for row in Tost for t in row])
        junk = sb("gd_junk", [128, 128], BF16)
        Tjunk = T()

        def mk(nm, dt):
            return [[(sb(f"gd_{nm}_{p}_{h}", [128, 128], dt), T()) for h in range(4)] for p in range(2)]
        B_kg, B_kd, B_vt = mk("kg", BF16), mk("kd", BF16), mk("vt", BF16)
        B_Gm, B_egbc, B_dTi, B_dTs = mk("Gm", F32), mk("egbc", F32), mk("dTi", F32), mk("dTs", F32)
        B_N, B_NT, B_X = mk("N", BF16), mk("NT", BF16), mk("X", BF16)
        B_P = [mk("P0", BF16), mk("P1", BF16)]
        B_PT = [mk("PT0", BF16), mk("PT1", BF16)]
        B_qk, B_qg, B_u, B_w, B_vn = mk("qk", BF16), mk("qg", BF16), mk("u", F32), mk("w", BF16), mk("vn", BF16)

        for blk in range(8):
            par = blk % 2
            qT, kT, vT = qkvT[par]
            for r in range(12):
                t, hl = r // 4, r % 4
                X, TX = Xr.next()
                if blk == 0:
                    c.op("pool", lambda e: e.memset(X[:, 0:3], 0.0), writes=[TX])
                    c.dma("sp", X[:, 3:515], GQKV[r, :, 0:512], reads=[TGQKV], writes=[TX])
                else:
                    c.dma("sp", X[:], GQKV[r, :, blk * 512 - 3:blk * 512 + 512], reads=[TGQKV], writes=[TX])
                y, Ty = yr.next()
                c.op("dve", lambda e: e.tensor_scalar(out=y[:], in0=X[:, 0:512], scalar1=cw[:, r, 0:1], scalar2=None,
                                                      op0=ALU.mult), reads=[TX, Tcw], writes=[Ty])
                for j in range(1, 4):
                    c.op("dve", lambda e: e.scalar_tensor_tensor(out=y[:], in0=X[:, j:j + 512],
                                                                 scalar=cw[:, r, j:j + 1], in1=y[:],
                                                                 op0=ALU.mult, op1=ALU.add),
                         reads=[TX, Tcw, Ty], writes=[Ty])
                if t == 2:
                    c.op("act", lambda e: e.activation(out=vT[0][:, hl, :], in_=y[:], func=AF.Silu),
                         reads=[Ty], writes=[vT[1]])
                    continue
                s, Ts = sr.next()
                c.op("act", lambda e: e.activation(out=s[:], in_=y[:], func=AF.Silu), reads=[Ty], writes=[Ts])
                sq, Tsq = sqr.next()
                c.op("pool", lambda e: e.tensor_tensor(out=sq[:], in0=s[:], in1=s[:], op=ALU.mult),
                     reads=[Ts], writes=[Tsq])
                c.op("pe", lambda e: e.matmul(ssq_ps[:], lhsT=onesb[:], rhs=sq[:], start=True, stop=True),
                     reads=[Tonesb, Tsq], writes=[Tssq])
                rn, Trn = rnr.next()
                ln_, Tln = lnr.next()
                rsqrt_chain(c, rn[:], ssq_ps[:], 1.0, epsb, ln_[:], [Tssq, Tcst], [Trn], Tln)
                dstT = qT if t == 0 else kT
                scl = (128.0 ** -0.5) if t == 0 else 1.0
                c.op("dve", lambda e: e.scalar_tensor_tensor(out=dstT[0][:, hl, :], in0=s[:], scalar=scl, in1=rn[:],
                                                             op0=ALU.mult, op1=ALU.mult),
                     reads=[Ts, Trn], writes=[dstT[1]])

            _chk(2)
            for cc in range(4):
                n = blk * 4 + cc
                cp = n % 2
                cs = slice(cc * 128, (cc + 1) * 128)
                z, Tz = zt[cp]
                gz, Tgz = gzp[cp]
                mx, Tmx = mixs[cp]
                c.dma("sp", z[:], GZ[n * 128:(n + 1) * 128, :], reads=[TGZ], writes=[Tz])
                c.op("act", lambda e: e.activation(out=gz[:], in_=z[:], func=AF.Silu), reads=[Tz], writes=[Tgz])
                c.op("pool", lambda e: e.tensor_tensor(out=gz[:], in0=gz[:], in1=gnw4[:].rearrange("p a b -> p (a b)"),
                                                       op=ALU.mult), reads=[Tgz, Tgnw4], writes=[Tgz])
                col = lambda nm, h: sc[nm][:, n, h:h + 1]
                _chk(25)
                for h in range(4):
                    c.op("pe", lambda e: e.transpose(trbank[:, 2 * h, :], kT[0][:, h, cs], identb[:]),
                         reads=[kT[1], Tidb], writes=[Ttr])
                    c.op("pe", lambda e: e.transpose(trbank[:, 2 * h + 1, :], vT[0][:, h, cs], identb[:]),
                         reads=[vT[1], Tidb], writes=[Ttr])
                _chk(26)
                for h in range(4):
                    kg, Tkg = B_kg[cp][h]
                    kd, Tkd = B_kd[cp][h]
                    vt, Tvt = B_vt[cp][h]
                    p1, Tp1 = trbank[:, 2 * h, :], Ttr
                    p2, Tp2 = trbank[:, 2 * h + 1, :], Ttr
                    evac_scale(c, "act", kg[:], p1, col("eg", h), [Tp1, Tsc], [Tkg])
                    evac_scale(c, "dve", kd[:], p1, col("ekd", h), [Tp1, Tsc], [Tkd])
                    evac_copy(c, "act", vt[:], p2, [Tp2], [Tvt])
                _chk(3)
                bka, Tpa = pring.next()
                bkb, Tpb = pring.next()
                for h in range(4):
                    Gm, TGm = B_Gm[cp][h]
                    c.op("dve", lambda e: e.tensor_scalar(out=Gm[:], in0=Uf[:], scalar1=col("g", h), scalar2=None,
                                                          op0=ALU.mult), reads=[TUf, Tsc], writes=[TGm])
                    pa = bka[:, h, :]
                    pb = bkb[:, h, :]
                    c.op("pe", lambda e: e.matmul(pa, lhsT=onesf[:], rhs=Gm[:], start=True, stop=True),
                         reads=[Tonesf, TGm], writes=[Tpa])
                    c.op("pe", lambda e: e.matmul(pb, lhsT=onesf[:], rhs=Gm[:], start=True, stop=False),
                         reads=[Tonesf, TGm], writes=[Tpb])
                    c.op("pe", lambda e: e.matmul(pb, lhsT=Gm[:], rhs=negonesf[:], start=False, stop=False),
                         reads=[Tnegonesf, TGm], writes=[Tpb])
                    c.op("pe", lambda e: e.matmul(pb, lhsT=identf[:], rhs=negmask[:], start=False, stop=True),
                         reads=[Tidf, Tnegmask], writes=[Tpb])
                for h in range(4):
                    egbc, Tegbc = B_egbc[cp][h]
                    dTi, TdTi = B_dTi[cp][h]
                    dTs, TdTs = B_dTs[cp][h]
                    pa = bka[:, h, :]
                    pb = bkb[:, h, :]
                    c.op("act", lambda e: e.activation(out=egbc[:], in_=pa, func=AF.Exp), reads=[Tpa], writes=[Tegbc])
                    c.op("act", lambda e: e.activation(out=dTi[:], in_=pb, func=AF.Exp), reads=[Tpb], writes=[TdTi])
                    c.op("pool", lambda e: e.tensor_tensor(out=dTs[:], in0=dTi[:], in1=strict[:], op=ALU.mult),
                         reads=[TdTi, Tstrict], writes=[TdTs])
                _chk(4)
                bkk, Tpk = pring.next()
                bkq, Tpq = pring.next()
                for h in range(4):
                    pk = bkk[:, h, :]
                    pq = bkq[:, h, :]
                    c.op("pe", lambda e: e.matmul(pk, lhsT=kT[0][:, h, cs], rhs=kT[0][:, h, cs], start=True, stop=True),
                         reads=[kT[1]], writes=[Tpk])
                    c.op("pe", lambda e: e.matmul(pq, lhsT=kT[0][:, h, cs], rhs=qT[0][:, h, cs], start=True, stop=True),
                         reads=[kT[1], qT[1]], writes=[Tpq])
                for h in range(4):
                    N_, TN = B_N[cp][h]
                    qk, Tqk = B_qk[cp][h]
                    qg, Tqg = B_qg[cp][h]
                    dTi, TdTi = B_dTi[cp][h]
                    dTs, TdTs = B_dTs[cp][h]
                    egbc, Tegbc = B_egbc[cp][h]
                    pk = bkk[:, h, :]
                    pq = bkq[:, h, :]
                    c.op("dve", lambda e: e.scalar_tensor_tensor(out=N_[:], in0=pk, scalar=col("beta", h), in1=dTs[:],
                                                                 op0=ALU.mult, op1=ALU.mult),
                         reads=[Tpk, Tsc, TdTs], writes=[TN])
                    c.op("dve", lambda e: e.tensor_tensor(out=qk[:], in0=pq, in1=dTi[:], op=ALU.mult),
                         reads=[Tpq, TdTi], writes=[Tqk])
                    c.op("pool", lambda e: e.tensor_tensor(out=qg[:], in0=qT[0][:, h, cs], in1=egbc[:], op=ALU.mult),
                         reads=[qT[1], Tegbc], writes=[Tqg])
                _chk(5)
                bkt, Tpt = pring.next()
                for h in range(4):
                    N_, TN = B_N[cp][h]
                    X_, TX_ = B_X[cp][h]
                    c.op("pool", lambda e: e.tensor_tensor(out=X_[:], in0=identb[:], in1=N_[:], op=ALU.subtract),
                         reads=[Tidb, TN], writes=[TX_])
                    c.op("pe", lambda e: e.matmul(bkt[:, h, :], lhsT=N_[:], rhs=identb[:], start=True, stop=True),
                         reads=[TN, Tidb], writes=[Tpt])
                for h in range(4):
                    NT_, TNT = B_NT[cp][h]
                    evac_copy(c, _alt(h), NT_[:], bkt[:, h, :], [Tpt], [TNT])
                for k in range(1, 7):
                    bkt, Tpt = pring.next()
                    for h in range(4):
                        Pp, TPp = (B_N[cp][h] if k == 1 else B_P[(k - 1) % 2][cp][h])
                        PTp, TPTp = (B_NT[cp][h] if k == 1 else B_PT[(k - 1) % 2][cp][h])
                        c.op("pe", lambda e: e.matmul(bkt[:, h, :], lhsT=Pp[:], rhs=PTp[:], start=True, stop=True),
                             reads=[TPp, TPTp], writes=[Tpt])
                    if k < 6:
                        bkp, Tpp = pring.next()
                        for h in range(4):
                            Pp, TPp = (B_N[cp][h] if k == 1 else B_P[(k - 1) % 2][cp][h])
                            PTp, TPTp = (B_NT[cp][h] if k == 1 else B_PT[(k - 1) % 2][cp][h])
                            c.op("pe", lambda e: e.matmul(bkp[:, h, :], lhsT=PTp[:], rhs=Pp[:], start=True, stop=True),
                                 reads=[TPp, TPTp], writes=[Tpp])
                    for h in range(4):
                        PTn, TPTn = B_PT[k % 2][cp][h]
                        evac_copy(c, "act", PTn[:], bkt[:, h, :], [Tpt], [TPTn])
                    bkx, Tpx = pring.next()
                    for h in range(4):
                        PTn, TPTn = B_PT[k % 2][cp][h]
                        X_, TX_ = B_X[cp][h]
                        c.op("pe", lambda e: e.matmul(bkx[:, h, :], lhsT=PTn[:], rhs=X_[:], start=True, stop=True),
                             reads=[TPTn, TX_], writes=[Tpx])
                    if k < 6:
                        for h in range(4):
                            Pn, TPn = B_P[k % 2][cp][h]
                            evac_copy(c, "dve", Pn[:], bkp[:, h, :], [Tpp], [TPn])
                    for h in range(4):
                        X_, TX_ = B_X[cp][h]
                        c.op("dve", lambda e: e.tensor_tensor(out=X_[:], in0=bkx[:, h, :], in1=X_[:], op=ALU.add),
                             reads=[Tpx, TX_], writes=[TX_])
                _chk(6)
                bku, Tpu = pring.next()
                bkw, Tpw = pring.next()
                for h in range(4):
                    X_, TX_ = B_X[cp][h]
                    c.op("pe", lambda e: e.matmul(bku[:, h, :], lhsT=X_[:], rhs=B_vt[cp][h][0][:], start=True, stop=True),
                         reads=[TX_, B_vt[cp][h][1]], writes=[Tpu])
                    c.op("pe", lambda e: e.matmul(bkw[:, h, :], lhsT=B_kg[cp][h][0][:], rhs=X_[:], start=True, stop=True),
                         reads=[TX_, B_kg[cp][h][1]], writes=[Tpw])
                for h in range(4):
                    u, Tu = B_u[cp][h]
                    w, Tw = B_w[cp][h]
                    evac_scale(c, "dve", u[:], bku[:, h, :], col("beta", h), [Tpu, Tsc], [Tu])
                    evac_copy(c, "act", w[:], bkw[:, h, :], [Tpw], [Tw])
                _chk(7)
                for h in range(4):
                    c.op("pe", lambda e: e.matmul(ws_ps[h][0], lhsT=B_w[cp][h][0][:], rhs=Sb[h][0][:],
                                                  start=True, stop=True),
                         reads=[B_w[cp][h][1], Sb[h][1]], writes=[ws_ps[h][1]])
                for h in range(4):
                    vn, Tvn = B_vn[cp][h]
                    c.op("dve", lambda e: e.scalar_tensor_tensor(out=vn[:], in0=ws_ps[h][0], scalar=col("negb", h),
                                                                 in1=B_u[cp][h][0][:], op0=ALU.mult, op1=ALU.add),
                         reads=[ws_ps[h][1], Tsc, B_u[cp][h][1]], writes=[Tvn])
                for h in range(4):
                    vn, Tvn = B_vn[cp][h]
                    c.op("pe", lambda e: e.matmul(Sd_ps[h][0], lhsT=B_kd[cp][h][0][:], rhs=vn[:],
                                                  start=True, stop=True),
                         reads=[B_kd[cp][h][1], Tvn], writes=[Sd_ps[h][1]])
                for h in range(4):
                    vn, Tvn = B_vn[cp][h]
                    c.op("pe", lambda e: e.matmul(O_ps[h][0], lhsT=B_qg[cp][h][0][:], rhs=Sb[h][0][:],
                                                  start=True, stop=False),
                         reads=[B_qg[cp][h][1], Sb[h][1]], writes=[O_ps[h][1]])
                    c.op("pe", lambda e: e.matmul(O_ps[h][0], lhsT=B_qk[cp][h][0][:], rhs=vn[:],
                                                  start=False, stop=True),
                         reads=[B_qk[cp][h][1], Tvn], writes=[O_ps[h][1]])
                for h in range(4):
                    c.op("dve", lambda e: e.scalar_tensor_tensor(out=Sb[h][0][:], in0=S32[h][0][:],
                                                                 scalar=col("egl", h), in1=Sd_ps[h][0],
                                                                 op0=ALU.mult, op1=ALU.add),
                         reads=[S32[h][1], Tsc, Sd_ps[h][1]], writes=[Sb[h][1]])
                    c.op("dve", lambda e: e.scalar_tensor_tensor(out=S32[h][0][:], in0=S32[h][0][:],
                                                                 scalar=col("egl", h), in1=Sd_ps[h][0],
                                                                 op0=ALU.mult, op1=ALU.add),
                         reads=[S32[h][1], Tsc, Sd_ps[h][1]], writes=[S32[h][1]])
                for h in range(4):
                    To = Tost[n][h]
                    c.op("act", lambda e: e.activation(out=junk[:], in_=O_ps[h][0], func=AF.Square,
                                                       accum_out=ost[:, n, h, 0:1]),
                         reads=[O_ps[h][1]], writes=[Tjunk, To])
                    rsqrt_chain(c, ost[:, n, h, 2:3], ost[:, n, h, 0:1], 1.0 / 128, epsb, ost[:, n, h, 1:2],
                                [To, Tcst], [To], To)
                    c.op("dve", lambda e: e.scalar_tensor_tensor(out=mx[:, h * 128:(h + 1) * 128], in0=O_ps[h][0],
                                                                 scalar=ost[:, n, h, 2:3],
                                                                 in1=gz[:, h * 128:(h + 1) * 128],
                                                                 op0=ALU.mult, op1=ALU.mult),
                         reads=[O_ps[h][1], To, Tgz], writes=[Tmx])
                c.dma("pool", MIXH[n * 128:(n + 1) * 128, 512:1024], mx[:], reads=[Tmx], writes=[TMIXH])
                _chk(8)
        c.barrier()


def build_program(mode, layers=(0, 1), dbg_stages=None):
    nc = bass.Bass("TRN2", target_bir_lowering=False)
    dram = lambda n, s, d, kind: nc.dram_tensor(n, s, d, kind=kind).ap()
    IN, OUT, INT = "ExternalInput", "ExternalOutput", "Internal"
    SK = OUT if mode == "dbg" else INT
    G = {}
    need_in = {"fused": (0, 1), "p1": (0,), "p2": (1,), "p3": (), "dbg": tuple(layers)}[mode]
    need_out = {"fused": (0, 1), "p1": (), "p2": (0,), "p3": (1,), "dbg": tuple(layers)}[mode]
    with contextlib.ExitStack() as es:
        c = Ctx(nc, es)
        sb = lambda n, s, d: es.enter_context(nc.sbuf_tensor(n, s, d))
        win = {l: dram(f"win{l}", [128, 16 * NCOL], F32, IN) for l in need_in}
        wout = {l: dram(f"wout{l}", [128, 16, D], F32, IN) for l in need_out}
        G["Wbf"] = {l: dram(f"wbf{l}", [128, 16 * NCOL], BF16, INT) for l in need_in}
        G["TWbf"] = {l: [T() for _ in range(9)] for l in need_in}
        G["Wobf"] = {l: dram(f"wobf{l}", [128, 16, D], BF16, INT) for l in need_out}
        G["TWobf"] = {l: [T() for _ in range(16)] for l in need_out}
        gpk_d = dram("gpk_d", [DEPTH, 128, 16], F32, IN)
        cw_d = dram("cw_d", [DEPTH, 128, 12, 4], F32, IN)
        hpar_d = dram("hpar_d", [DEPTH, 128, 8], F32, IN)
        G["gnw_d"] = dram("gnw_d", [DEPTH, 128], F32, IN)
        G["subw_d"] = dram("subw_d", [DEPTH, 128], F32, IN)
        G["lam_d"] = dram("lam_d", [DEPTH, 4, 64], F32, IN)
        G["fnw_d"] = dram("fnw_d", [1, D], F32, IN)
        G["braw_d"] = dram("braw_d", [4, 5, 128, 512], F32, IN)
        G["mpat_d"] = dram("mpat_d", [5, 128, 512], F32, IN)
        cfar_d = dram("cfar_d", [128, 4], F32, IN)
        cmat_d = dram("cmat_d", [6, 128, 128], F32, IN)
        if mode == "dbg" and "inproj" not in dbg_stages:
            SK = IN
        G["QKT"] = (dram("s_qkt", [8, 128, SEQ], BF16, SK), T())
        G["GQKV"] = (dram("s_gqkv", [12, 128, SEQ], F32, SK), T())
        G["DAV"] = (dram("s_dav", [SEQ, 512], BF16, SK), T())
        G["DAG"] = (dram("s_dag", [SEQ, 512], F32, SK), T())
        G["GZ"] = (dram("s_gz", [SEQ, 512], F32, SK), T())
        G["GBA"] = (dram("s_gba", [SEQ, 8], F32, SK), T())
        cst = sb("cst", [128, 4], F32)
        Tcst = T()
        c.op("pool", lambda e: e.memset(cst[:, 0:1], RMS_EPS), writes=[Tcst])
        c.op("pool", lambda e: e.memset(cst[:, 1:2], 0.0), writes=[Tcst])
        c.op("pool", lambda e: e.memset(cst[:, 2:3], 1.0), writes=[Tcst])
        G["cst"] = (cst, Tcst)
        cm = sb("cmat", [128, 6, 128], F32)
        Tcm = T()
        c.dma("sp", cm[:], cmat_d.rearrange("a p q -> p a q"), writes=[Tcm])
        G["identf"] = (cm[:, 0, :], Tcm)
        G["Uf"] = (cm[:, 1, :], Tcm)
        G["onesf"] = (cm[:, 2, :], Tcm)
        G["negonesf"] = (cm[:, 3, :], Tcm)
        G["negmask"] = (cm[:, 4, :], Tcm)
        G["strict"] = (cm[:, 5, :], Tcm)
        identb = sb("identb", [128, 128], BF16)
        onesb = sb("onesb", [128, 128], BF16)
        Tib, Tob = T(), T()
        c.op("dve", lambda e: e.tensor_copy(out=identb[:], in_=cm[:, 0, :]), reads=[Tcm], writes=[Tib])
        c.op("dve", lambda e: e.tensor_copy(out=onesb[:], in_=cm[:, 2, :]), reads=[Tcm], writes=[Tob])
        G["identb"] = (identb, Tib)
        G["onesb"] = (onesb, Tob)
        cfar = sb("cfar", [128, 4], F32)
        Tcfar = T()
        c.dma("sp", cfar[:], cfar_d, writes=[Tcfar])
        G["cfar"] = (cfar, Tcfar)
        G["gpk"], G["cw"], G["hpar"] = {}, {}, {}
        for l in range(DEPTH):
            t1 = sb(f"gpk{l}", [128, 16], F32)
            t2 = sb(f"cw{l}", [128, 12, 4], F32)
            t3 = sb(f"hpar{l}", [128, 8], F32)
            T1, T2, T3 = T(), T(), T()
            c.dma("sp", t1[:], gpk_d[l], writes=[T1])
            c.dma("sp", t2[:], cw_d[l], writes=[T2])
            c.dma("sp", t3[:], hpar_d[l], writes=[T3])
            G["gpk"][l], G["cw"][l], G["hpar"][l] = (t1, T1), (t2, T2), (t3, T3)
        def cast_in(l, cb):
            a, b = (cb * 8192, (cb + 1) * 8192) if cb < 8 else (65536, 16 * NCOL)
            return lambda: c.dma("pool", G["Wbf"][l][:, a:b], win[l][:, a:b], writes=[G["TWbf"][l][cb]])

        def cast_out(l, k):
            return lambda: c.dma("pool", G["Wobf"][l][:, k, :], wout[l][:, k, :], writes=[G["TWobf"][l][k]])
        G["pending"] = []
        if mode == "fused":
            for k in range(9):
                cast_in(0, k)()
        else:
            for l in need_in:
                for k in range(9):
                    cast_in(l, k)()
            for l in need_out:
                for k in range(16):
                    cast_out(l, k)()

        if mode == "dbg":
            l = layers[0]
            x_in = dram("x_in", [SEQ, D], F32, IN)
            G["MIXH"] = (dram("mixh", [SEQ, 1024], BF16, OUT), T())
            mixf = dram("mixf_in", [2, SEQ, 1024], BF16, IN)
            x1 = dram("x1", [SEQ, D], F32, OUT)
            if "inproj" in dbg_stages:
                stage_inproj(c, nc, G, l, x_in, T())
            if "da" in dbg_stages:
                stage_da(c, nc, G, l)
            if "gdn" in dbg_stages:
                stage_gdn(c, nc, G, l)
            if "outproj" in dbg_stages:
                stage_outproj(c, nc, G, l, x_in, T(), mixf, T(), x1, T(), 0, NT, False)
        elif mode == "p1":
            x_in = dram("x_in", [SEQ, D], F32, IN)
            G["MIXH"] = (dram("mixh", [SEQ, 1024], BF16, OUT), T())
            stage_inproj(c, nc, G, 0, x_in, T())
            stage_da(c, nc, G, 0)
            stage_gdn(c, nc, G, 0)
        elif mode == "p2":
            x_in = dram("x_in", [SEQ, D], F32, IN)
            mixf = dram("mixf_in", [2, SEQ, 1024], BF16, IN)
            x1 = dram("x1", [SEQ, D], F32, OUT)
            Tx1 = T()
            G["MIXH"] = (dram("mixh", [SEQ, 1024], BF16, OUT), T())
            stage_outproj(c, nc, G, 0, x_in, T(), mixf, T(), x1, Tx1, 0, NT, False)
            stage_inproj(c, nc, G, 1, x1, Tx1)
            stage_da(c, nc, G, 1)
            stage_gdn(c, nc, G, 1)
        elif mode == "p3":
            x_in = dram("x_in", [SEQ // 2, D], F32, IN)
            mixf = dram("mixf_in", [2, SEQ // 2, 1024], BF16, IN)
            out = dram("out", [SEQ // 2, D], F32, OUT)
            stage_outproj(c, nc, G, 1, x_in, T(), mixf, T(), out, T(), 0, NT // 2, True)
        elif mode == "fused":
            x_in = dram("x_in", [SEQ, D], F32, IN)
            out = dram("out", [SEQ, D], F32, OUT)
            G["MIXH"] = (dram("s_mixh", [SEQ, 1024], BF16, INT), T())
            mixf4, Tmixf = dram("s_mixf", [4, 2, 1024, 1024], BF16, INT), T()
            mixf = lambda r, tok: mixf4[tok // 1024, r, tok % 1024:tok % 1024 + 128, :]
            x1, Tx1 = dram("s_x1", [SEQ, D], F32, INT), T()
            Tx0 = T()
            groups = [[2 * i, 2 * i + 1] for i in range(NB)]
            for l in range(DEPTH):
                src, Tsrc = (x_in, Tx0) if l == 0 else (x1, Tx1)
                stage_inproj(c, nc, G, l, src, Tsrc)
                G["pending"] += [cast_out(l, k) for k in range(16)]
                if l + 1 < DEPTH:
                    G["pending"] += [cast_in(l + 1, k) for k in range(9)]
                stage_da(c, nc, G, l)
                stage_gdn(c, nc, G, l)
                for ch in range(4):
                    c.collective(lambda e: e.collective_compute(
                        "AllGather", ALU.bypass, replica_groups=groups,
                        ins=[G["MIXH"][0][ch * 1024:(ch + 1) * 1024, :]],
                        outs=[mixf4[ch].rearrange("r p c -> (r p) c")]),
                        reads=[G["MIXH"][1]], writes=[Tmixf])
                if l == 0:
                    stage_outproj(c, nc, G, l, x_in, Tx0, mixf, Tmixf, x1, Tx1, 0, NT, False)
                else:
                    stage_outproj(c, nc, G, l, x1, Tx1, mixf, Tmixf, out, T(), 0, NT, True)
        c.finish("sp")
        print(f"[build {mode}] instructions={c.n_ins} waits={c.n_wait} sems={len(c.sems)}")
    return nc


def _bucket_np(dist):
    n = np.maximum(dist, 0)
    max_exact = 16
    large = max_exact + (np.log(np.maximum(n, max_exact).astype(np.float32) / np.float32(max_exact))
                         / np.float32(math.log(128 / max_exact)) * np.float32(32 - max_exact)).astype(np.int32)
    large = np.minimum(large, 31)
    return np.where(n < max_exact, n, large)


def _consts():
    j = np.arange(128)[:, None]
    i = np.arange(128)[None, :]
    cm = np.zeros((6, 128, 128), np.float32)
    cm[0] = np.eye(128)
    cm[1] = (j <= i)
    cm[2] = 1.0
    cm[3] = -1.0
    cm[4] = np.where(i < j, NEG, 0.0)
    cm[5] = (i > j)
    k = np.arange(128)[:, None]
    q = np.arange(512)[None, :]
    dists = [q - k - (a * 128 - 128) for a in range(5)]
    mpat = np.stack([np.where(d < 0, NEG, 0.0) for d in dists]).astype(np.float32)
    bidx = np.stack([_bucket_np(d) for d in dists])
    return cm, mpat, bidx


def prep_core_inputs(inp, b, hh):
    cm, mpat, bidx = _consts()
    H = np.arange(4 * hh, 4 * hh + 4)
    cols = np.concatenate([
        hh * 512 + np.arange(512), 1024 + hh * 512 + np.arange(512),
        4096 + hh * 512 + np.arange(512), 5120 + hh * 512 + np.arange(512), 6144 + hh * 512 + np.arange(512),
        2048 + hh * 512 + np.arange(512), 3072 + hh * 512 + np.arange(512), 7168 + hh * 512 + np.arange(512),
        8192 + hh * 4 + np.arange(4), 8200 + hh * 4 + np.arange(4)])
    rows = np.concatenate([np.concatenate([r * 512 + np.arange(512), 1024 + r * 512 + np.arange(512)])
                           for r in range(2)])
    m = {}
    for l in range(DEPTH):
        wpk = inp["w_in"][l][:, cols].reshape(16, 128, NCOL).transpose(1, 0, 2)
        m[f"win{l}"] = np.ascontiguousarray(np.concatenate(
            [wpk[:, :, cb * 512:(cb + 1) * 512].reshape(128, 8192) for cb in range(8)]
            + [wpk[:, :, 4096:NCOL].reshape(128, 128)], axis=1))
        m[f"wout{l}"] = np.ascontiguousarray(
            inp["w_out"][l][rows, :].reshape(16, 128, D).transpose(1, 0, 2))
    m["gpk_d"] = np.ascontiguousarray(inp["norm_w"].reshape(DEPTH, 16, 128).transpose(0, 2, 1))
    ch = np.stack([t * 1024 + (4 * hh + hl) * 128 + np.arange(128) for t in range(3) for hl in range(4)])
    m["cw_d"] = np.ascontiguousarray(inp["conv_w"][:, :, ch].transpose(0, 3, 2, 1))
    hp = np.concatenate([inp["a_log"][:, H], inp["dt_bias"][:, H]], axis=1)
    m["hpar_d"] = np.ascontiguousarray(np.broadcast_to(hp[:, None, :], (DEPTH, 128, 8)))
    m["gnw_d"] = np.ascontiguousarray(inp["gdn_norm_w"])
    m["subw_d"] = np.ascontiguousarray(inp["da_subln_w"])
    m["lam_d"] = np.ascontiguousarray(np.stack([inp["lambda_q1"], inp["lambda_k1"],
                                                inp["lambda_q2"], inp["lambda_k2"]], axis=1))
    m["fnw_d"] = np.ascontiguousarray(inp["final_norm_w"].reshape(1, D))
    rb = inp["rel_bias"]
    m["braw_d"] = np.ascontiguousarray(np.stack([rb[:, h][bidx] for h in H]).astype(np.float32))
    m["mpat_d"] = mpat
    m["cfar_d"] = np.ascontiguousarray(np.broadcast_to(rb[31, H][None, :], (128, 4)))
    m["cmat_d"] = cm
    return {k: np.asarray(v, dtype=np.float32) if v.dtype != np.float32 else v for k, v in m.items()}


_PROGS = {}


def _prog(mode):
    if mode not in _PROGS:
        _PROGS[mode] = build_program(mode)
    return _PROGS[mode]


_COMMON = ["gpk_d", "cw_d", "hpar_d", "gnw_d", "subw_d", "lam_d", "fnw_d", "braw_d", "mpat_d", "cfar_d", "cmat_d"]
FUSED = True


def kernel(**inputs):
    inp = {k: np.asarray(v) for k, v in inputs.items()}
    x = np.ascontiguousarray(inp["x"], dtype=np.float32)
    cores = [(b, hh) for b in range(NB) for hh in range(2)]
    prep = [prep_core_inputs(inp, b, hh) for (b, hh) in cores]
    ids = list(range(8))
    out = np.empty((NB, SEQ, D), np.float32)
    if FUSED:
        maps = []
        for ci, (b, hh) in enumerate(cores):
            m = {k: prep[ci][k] for k in _COMMON}
            for l in range(DEPTH):
                m[f"win{l}"] = prep[ci][f"win{l}"]
                m[f"wout{l}"] = prep[ci][f"wout{l}"]
            m["x_in"] = x[b]
            maps.append(m)
        res = run_bass_kernel_spmd(_prog("fused"), maps, core_ids=ids).results
        for ci, (b, hh) in enumerate(cores):
            out[b, hh * 2048:(hh + 1) * 2048] = np.asarray(res[ci]["out"])[hh * 2048:(hh + 1) * 2048]
        return out
    maps = []
    for ci, (b, hh) in enumerate(cores):
        m = {k: prep[ci][k] for k in _COMMON}
        m["win0"] = prep[ci]["win0"]
        m["x_in"] = x[b]
        maps.append(m)
    r1 = run_bass_kernel_spmd(_prog("p1"), maps, core_ids=ids).results
    mixf0 = [np.stack([np.asarray(r1[2 * b]["mixh"]), np.asarray(r1[2 * b + 1]["mixh"])]) for b in range(NB)]
    maps = []
    for ci, (b, hh) in enumerate(cores):
        m = {k: prep[ci][k] for k in _COMMON}
        m["win1"] = prep[ci]["win1"]
        m["wout0"] = prep[ci]["wout0"]
        m["x_in"] = x[b]
        m["mixf_in"] = mixf0[b]
        maps.append(m)
    r2 = run_bass_kernel_spmd(_prog("p2"), maps, core_ids=ids).results
    mixf1 = [np.stack([np.asarray(r2[2 * b]["mixh"]), np.asarray(r2[2 * b + 1]["mixh"])]) for b in range(NB)]
    maps = []
    for ci, (b, hh) in enumerate(cores):
        m = {k: prep[ci][k] for k in _COMMON}
        m["wout1"] = prep[ci]["wout1"]
        m["x_in"] = np.ascontiguousarray(np.asarray(r2[ci]["x1"])[hh * 2048:(hh + 1) * 2048])
        m["mixf_in"] = np.ascontiguousarray(mixf1[b][:, hh * 2048:(hh + 1) * 2048, :])
        maps.append(m)
    r3 = run_bass_kernel_spmd(_prog("p3"), maps, core_ids=ids).results
    for ci, (b, hh) in enumerate(cores):
        out[b, hh * 2048:(hh + 1) * 2048] = np.asarray(r3[ci]["out"])
    return out
```

```python
import math
import contextlib
import numpy as np
import ml_dtypes
import concourse.bass as bass
import concourse.mybir as mybir
from concourse.bass_utils import run_bass_kernel_spmd

F32 = mybir.dt.float32
BF16 = mybir.dt.bfloat16
AF = mybir.ActivationFunctionType
ALU = mybir.AluOpType

D = 2048
SEQ = 4096
NB = 4
DEPTH = 2
NCOL = 4104
NT = SEQ // 128
RMS_EPS = 1e-6
SCALE = 0.125
NEG = -30000.0

SEM_EPOCH = 24000
N_LANES = 8


class T:
    __slots__ = ("name", "lw", "rd", "excl")

    def __init__(self, name="", excl=False):
        self.name = name
        self.lw = None
        self.rd = []
        self.excl = excl


def TP():
    return T(excl=True)


class Ctx:
    def __init__(self, nc, es):
        self.nc = nc
        self.es = es
        self.engs = {"pe": nc.tensor, "act": nc.scalar, "dve": nc.vector,
                     "pool": nc.gpsimd, "sp": nc.sync}
        self.sems = {}
        self.cur = {}
        self.epoch = {e: 0 for e in self.engs}
        self.seen = {e: {} for e in self.engs}
        for e in self.engs:
            self._new_epoch(e)
        self.lanes = {}
        self.lane_rr = {}
        self.n_wait = 0
        self.n_ins = 0
        self.uid = 0

    def _mksem(self, key):
        h = self.es.enter_context(self.nc.semaphore(key))
        self.sems[key] = h
        return h

    def _new_epoch(self, e):
        key = f"s_{e}_{self.epoch[e]}"
        self.epoch[e] += 1
        self._mksem(key)
        self.cur[e] = [key, 0]

    def _lanes(self, q):
        if q not in self.lanes:
            self.lanes[q] = []
            for i in range(N_LANES):
                key = f"l_{q}_{i}"
                self._mksem(key)
                self.lanes[q].append([key, 0])
            self.lane_rr[q] = 0
        return self.lanes[q]

    def _wait(self, e, ev):
        if ev is None:
            return
        key, val = ev
        if e == "pe" and key.startswith("s_pe_"):
            return
        if self.seen[e].get(key, 0) >= val:
            return
        self.engs[e].wait_ge(self.sems[key], val)
        self.seen[e][key] = val
        self.n_wait += 1

    def _deps(self, e, reads, writes):
        for t in reads:
            self._wait(e, t.lw)
            if t.excl:
                for ev in t.rd:
                    if ev[0].split("_")[1] != e:
                        self._wait(e, ev)
        for t in writes:
            self._wait(e, t.lw)
            for ev in t.rd:
                self._wait(e, ev)

    def _record(self, ev, reads, writes):
        for t in reads:
            t.rd = [r for r in t.rd if r[0] != ev[0]] + [ev]
        for t in writes:
            t.lw = ev
            t.rd = []

    def op(self, e, fn, reads=(), writes=()):
        self._deps(e, reads, writes)
        ins = fn(self.engs[e])
        c = self.cur[e]
        c[1] += 1
        ins.then_inc(self.sems[c[0]], 1)
        ev = (c[0], c[1])
        self._record(ev, reads, writes)
        self.n_ins += 1
        if c[1] >= SEM_EPOCH:
            self._new_epoch(e)
        return ev

    def dma(self, q, out, in_, reads=(), writes=(), **kw):
        lanes = self._lanes(q)
        i = self.lane_rr[q]
        self.lane_rr[q] = (i + 1) % N_LANES
        lane = lanes[i]
        if lane[1] > 0:
            self._wait(q, (lane[0], lane[1]))
        self._deps(q, reads, writes)
        ins = self.engs[q].dma_start(out=out, in_=in_, **kw)
        lane[1] += 16
        ins.then_inc(self.sems[lane[0]], 16)
        ev = (lane[0], lane[1])
        self._record(ev, reads, writes)
        self.n_ins += 1
        return ev

    def collective(self, fn, reads=(), writes=()):
        if "cc" not in self.sems:
            self._mksem("cc")
            self.cc_count = 0
        self._deps("pool", reads, writes)
        ins = fn(self.engs["pool"])
        self.cc_count += 1
        ins.then_inc(self.sems["cc"])
        ev = ("cc", self.cc_count)
        self._record(ev, reads, writes)
        self.n_ins += 1
        return ev

    def all_events(self):
        evs = []
        for e in self.engs:
            for ep in range(self.epoch[e]):
                key = f"s_{e}_{ep}"
                val = self.cur[e][1] if key == self.cur[e][0] else SEM_EPOCH
                if val > 0:
                    evs.append((key, val))
        for q, lanes in self.lanes.items():
            for lane in lanes:
                if lane[1] > 0:
                    evs.append((lane[0], lane[1]))
        if "cc" in self.sems and self.cc_count > 0:
            evs.append(("cc", self.cc_count))
        return evs

    def barrier(self):
        evs = self.all_events()
        for e in self.engs:
            for ev in evs:
                if ev[0] == self.cur[e][0]:
                    continue
                self._wait(e, ev)

    def finish(self, e="sp"):
        for ev in self.all_events():
            if ev[0] == self.cur[e][0]:
                continue
            self._wait(e, ev)


class Ring:
    def __init__(self, items):
        self.items = items
        self.i = 0

    def next(self):
        it = self.items[self.i]
        self.i = (self.i + 1) % len(self.items)
        return it


def _alt(i):
    return "act" if (i % 2) else "dve"


def evac_copy(c, eng, out, in_, reads, writes):
    if eng == "act":
        return c.op("act", lambda e: e.copy(out=out, in_=in_), reads=reads, writes=writes)
    return c.op(eng, lambda e: e.tensor_copy(out=out, in_=in_), reads=reads, writes=writes)


def evac_scale(c, eng, out, in_, sc, reads, writes):
    if eng == "act":
        return c.op("act", lambda e: e.mul(out=out, in_=in_, mul=sc), reads=reads, writes=writes)
    return c.op("dve", lambda e: e.tensor_scalar(out=out, in0=in_, scalar1=sc, scalar2=None, op0=ALU.mult),
                reads=reads, writes=writes)


def rsqrt_chain(c, out, in_, scale, epsb, tmp, reads, writes, tmpT):
    c.op("act", lambda e: e.activation(out=tmp, in_=in_, func=AF.Ln, bias=epsb, scale=scale),
         reads=reads, writes=[tmpT])
    c.op("act", lambda e: e.activation(out=out, in_=tmp, func=AF.Exp, scale=-0.5),
         reads=[tmpT], writes=writes)


def stage_inproj(c, nc, G, l, x_src, Tx_src):
    Wbf = G["Wbf"][l]
    TW = G["TWbf"][l]
    with contextlib.ExitStack() as es:
        sb = lambda n, s, d: es.enter_context(nc.sbuf_tensor(f"{n}_L{l}", s, d))
        ps = lambda n, s, d: es.enter_context(nc.psum_tensor(f"{n}_L{l}", s, d))
        xt = [(sb(f"ip_x{i}", [128, D], F32), T()) for i in range(3)]
        junk = sb("ip_junk", [128, D], BF16)
        Tjunk = T()
        st = sb("ip_st", [128, NT, 4], F32)
        Tst = [T() for _ in range(NT)]
        xn = [(sb(f"ip_xn{i}", [128, 4, D], BF16), T()) for i in range(2)]
        hT = [(sb(f"ip_hT{i}", [128, 16, 512], BF16), T()) for i in range(2)]
        wb = Ring([(sb(f"ip_w{i}", [128, 16, 512], BF16), T()) for i in range(3)])
        wl = [(sb(f"ip_wl{i}", [128, 16, 8], BF16), T()) for i in range(2)]
        sg_b = Ring([(sb(f"ip_sb{i}", [128, 512], BF16), T()) for i in range(4)])
        sg_f = Ring([(sb(f"ip_sf{i}", [128, 512], F32), T()) for i in range(4)])
        trp = Ring([(ps(f"ip_tr{i}", [128, 1024], BF16)[:, 0:512], TP()) for i in range(2)])
        mmp = Ring([(ps(f"ip_mm{i}", [128, 512], F32), TP()) for i in range(6)])
        gpk, Tgpk = G["gpk"][l]
        identb, Tidb = G["identb"]
        epsb = G["cst"][0][:, 0:1]
        Tcst = G["cst"][1]

        c.op("pool", lambda e: e.memset(st[:], 0.0), writes=Tst)

        def load_x(blk):
            for j in range(4):
                t = blk * 4 + j
                xa, Txa = xt[t % 3]
                c.dma("sp", xa[:], x_src[t * 128:(t + 1) * 128, :], reads=[Tx_src], writes=[Txa])

        def load_w(cb):
            if cb < 8:
                w, Tw = wb.next()
                c.dma("sp", w[:].rearrange("p k c -> p (k c)"), Wbf[:, cb * 8192:(cb + 1) * 8192],
                      reads=[TW[cb]], writes=[Tw])
            else:
                w, Tw = wl[load_w.n8 % 2]
                load_w.n8 += 1
                c.dma("sp", w[:].rearrange("p k c -> p (k c)"), Wbf[:, 65536:16 * NCOL], reads=[TW[8]], writes=[Tw])
            return w, Tw
        load_w.n8 = 0

        nev = 0

        def prologue(blk):
            xnb, Txn = xn[blk % 2]
            for j in range(4):
                t = blk * 4 + j
                xa, Txa = xt[t % 3]
                c.dma("sp", xa[:], x_src[t * 128:(t + 1) * 128, :], reads=[Tx_src], writes=[Txa])
                c.op("act", lambda e: e.activation(out=junk[:], in_=xa[:], func=AF.Square,
                                                   accum_out=st[:, t, 0:1]),
                     reads=[Txa], writes=[Tjunk, Tst[t]])
                rsqrt_chain(c, st[:, t, 2:3], st[:, t, 0:1], 1.0 / D, epsb, st[:, t, 1:2],
                            [Tst[t], Tcst], [Tst[t]], Tst[t])
                c.op("dve", lambda e: e.tensor_scalar(out=xnb[:, j, :], in0=xa[:], scalar1=st[:, t, 2:3],
                                                      scalar2=None, op0=ALU.mult),
                     reads=[Txa, Tst[t]], writes=[Txn])

        def transposes(blk):
            xnb, Txn = xn[blk % 2]
            hTb, ThT = hT[blk % 2]
            for k in range(16):
                p, Tp = trp.next()
                for j in range(4):
                    c.op("pe", lambda e: e.transpose(p[:, j * 128:(j + 1) * 128],
                                                     xnb[:, j, k * 128:(k + 1) * 128], identb[:]),
                         reads=[Txn, Tidb], writes=[Tp])
                evac_scale(c, _alt(k), hTb[:, k, :], p[:], gpk[:, k:k + 1], [Tp, Tgpk], [ThT])

        prologue(0)
        transposes(0)
        for blk in range(8):
            hTb, ThT = hT[blk % 2]
            if blk + 1 < 8:
                prologue(blk + 1)
            nxt = load_w(0)
            for cb in range(9):
                w, Tw = nxt
                if cb + 1 < 9:
                    nxt = load_w(cb + 1)
                if cb < 5:
                    for m in range(4):
                        p, Tp = mmp.next()
                        for k in range(16):
                            c.op("pe", lambda e: e.matmul(p[:], lhsT=w[:, k, m * 128:(m + 1) * 128],
                                                          rhs=hTb[:, k, :], start=(k == 0), stop=(k == 15)),
                                 reads=[Tw, ThT], writes=[Tp])
                        if cb < 2:
                            s, Ts = sg_b.next()
                            dst = G["QKT"][0][cb * 4 + m, :, blk * 512:(blk + 1) * 512]
                            Tdst = G["QKT"][1]
                        else:
                            s, Ts = sg_f.next()
                            dst = G["GQKV"][0][(cb - 2) * 4 + m, :, blk * 512:(blk + 1) * 512]
                            Tdst = G["GQKV"][1]
                        evac_copy(c, _alt(nev), s[:], p[:], [Tp], [Ts])
                        nev += 1
                        c.dma("pool", dst, s[:], reads=[Ts], writes=[Tdst])
                elif cb < 8:
                    for j in range(4):
                        t = blk * 4 + j
                        p, Tp = mmp.next()
                        for k in range(16):
                            c.op("pe", lambda e: e.matmul(p[:], lhsT=hTb[:, k, j * 128:(j + 1) * 128],
                                                          rhs=w[:, k, :], start=(k == 0), stop=(k == 15)),
                                 reads=[Tw, ThT], writes=[Tp])
                        if cb == 5:
                            s, Ts = sg_b.next()
                            dst, Tdst = G["DAV"][0][t * 128:(t + 1) * 128, :], G["DAV"][1]
                        elif cb == 6:
                            s, Ts = sg_f.next()
                            dst, Tdst = G["DAG"][0][t * 128:(t + 1) * 128, :], G["DAG"][1]
                        else:
                            s, Ts = sg_f.next()
                            dst, Tdst = G["GZ"][0][t * 128:(t + 1) * 128, :], G["GZ"][1]
                        evac_copy(c, _alt(nev), s[:], p[:], [Tp], [Ts])
                        nev += 1
                        c.dma("pool", dst, s[:], reads=[Ts], writes=[Tdst])
                else:
                    for j in range(4):
                        t = blk * 4 + j
                        p, Tp = mmp.next()
                        for k in range(16):
                            c.op("pe", lambda e: e.matmul(p[:, 0:8], lhsT=hTb[:, k, j * 128:(j + 1) * 128],
                                                          rhs=w[:, k, :], start=(k == 0), stop=(k == 15)),
                                 reads=[Tw, ThT], writes=[Tp])
                        s, Ts = sg_f.next()
                        evac_copy(c, _alt(nev), s[:, 0:8], p[:, 0:8], [Tp], [Ts])
                        nev += 1
                        c.dma("pool", G["GBA"][0][t * 128:(t + 1) * 128, :], s[:, 0:8], reads=[Ts],
                              writes=[G["GBA"][1]])
            if blk + 1 < 8:
                transposes(blk + 1)
        c.barrier()


def stage_outproj(c, nc, G, l, x_src, Tx_src, mixf, Tmixf, dst, Tdst, tok0, ntile, final):
    Wo = G["Wobf"][l]
    TWo = G["TWobf"][l]
    with contextlib.ExitStack() as es:
        sb = lambda n, s, d: es.enter_context(nc.sbuf_tensor(f"{n}_L{l}", s, d))
        ps = lambda n, s, d: es.enter_context(nc.psum_tensor(f"{n}_L{l}", s, d))
        wo = sb("op_w", [128, 16, D], BF16)
        Two = T()
        mt = [(sb(f"op_m{i}", [128, D], BF16), T()) for i in range(2)]
        mT = [(sb(f"op_mT{i}", [128, 16, 128], BF16), T()) for i in range(2)]
        xt = [(sb(f"op_x{i}", [128, D], F32), T()) for i in range(2)]
        ot = [(sb(f"op_o{i}", [128, D], F32), T()) for i in range(2)]
        junk = sb("op_junk", [128, D], BF16)
        Tjunk = T()
        st = sb("op_st", [128, NT, 4], F32)
        Tst = [T() for _ in range(NT)]
        fnw = sb("op_fnw", [128, D], F32)
        Tfnw = T()
        trp = Ring([(ps(f"op_tr{i}", [128, 1024], BF16)[:, 0:512], TP()) for i in range(2)])
        mmp = Ring([(ps(f"op_mm{i}", [128, 512], F32), TP()) for i in range(6)])
        identb, Tidb = G["identb"]
        epsb = G["cst"][0][:, 0:1]
        Tcst = G["cst"][1]
        for k4 in range(4):
            c.dma("sp", wo[:, k4 * 4:(k4 + 1) * 4, :], Wo[:, k4 * 4:(k4 + 1) * 4, :], reads=TWo[k4 * 4:(k4 + 1) * 4], writes=[Two])
        if final:
            c.dma("sp", fnw[:], G["fnw_d"][0:1, :].to_broadcast([128, D]), writes=[Tfnw])
            c.op("pool", lambda e: e.memset(st[:], 0.0), writes=Tst)
        nev = 0
        for i in range(ntile):
            tok = tok0 + i * 128
            m, Tm = mt[i % 2]
            mTt, TmT = mT[i % 2]
            xa, Txa = xt[i % 2]
            o, To = ot[i % 2]
            for r in range(2):
                src_ap = mixf(r, tok) if callable(mixf) else mixf[r, tok:tok + 128, :]
                c.dma("sp", m[:, r * 1024:(r + 1) * 1024], src_ap, reads=[Tmixf], writes=[Tm])
            c.dma("sp", xa[:], x_src[tok:tok + 128, :], reads=[Tx_src], writes=[Txa])
            for k4 in range(4):
                p, Tp = trp.next()
                for j in range(4):
                    k = k4 * 4 + j
                    c.op("pe", lambda e: e.transpose(p[:, j * 128:(j + 1) * 128], m[:, k * 128:(k + 1) * 128],
                                                     identb[:]),
                         reads=[Tm, Tidb], writes=[Tp])
                evac_copy(c, _alt(k4), mTt[:, k4 * 4:(k4 + 1) * 4, :],
                          p[:].rearrange("p (j t) -> p j t", j=4), [Tp], [TmT])
            for nb in range(4):
                p, Tp = mmp.next()
                for k in range(16):
                    c.op("pe", lambda e: e.matmul(p[:], lhsT=mTt[:, k, :], rhs=wo[:, k, nb * 512:(nb + 1) * 512],
                                                  start=(k == 0), stop=(k == 15)),
                         reads=[TmT, Two], writes=[Tp])
                c.op("dve", lambda e: e.tensor_tensor(out=o[:, nb * 512:(nb + 1) * 512], in0=p[:],
                                                      in1=xa[:, nb * 512:(nb + 1) * 512], op=ALU.add),
                     reads=[Tp, Txa], writes=[To])
            if not final:
                c.dma("pool", dst[tok:tok + 128, :], o[:], reads=[To], writes=[Tdst])
            else:
                c.op("act", lambda e: e.activation(out=junk[:], in_=o[:], func=AF.Square,
                                                   accum_out=st[:, i, 0:1]),
                     reads=[To], writes=[Tjunk, Tst[i]])
                rsqrt_chain(c, st[:, i, 2:3], st[:, i, 0:1], 1.0 / D, epsb, st[:, i, 1:2],
                            [Tst[i], Tcst], [Tst[i]], Tst[i])
                c.op("dve", lambda e: e.scalar_tensor_tensor(out=xa[:], in0=o[:], scalar=st[:, i, 2:3],
                                                             in1=fnw[:], op0=ALU.mult, op1=ALU.mult),
                     reads=[To, Tst[i], Tfnw], writes=[Txa])
                c.dma("pool", dst[i * 128:(i + 1) * 128, :], xa[:], reads=[Txa], writes=[Tdst])
        c.barrier()


def stage_da(c, nc, G, l):
    lam_init = 0.8 - 0.6 * math.exp(-0.3 * l)
    QKT, TQKT = G["QKT"]
    DAV, TDAV = G["DAV"]
    DAG, TDAG = G["DAG"]
    MIXH, TMIXH = G["MIXH"]
    DAVr = DAV.rearrange("(kb p) c -> p kb c", p=128)
    DAGr = DAG.rearrange("(j p) c -> p j c", p=128)
    MIXr = MIXH.rearrange("(j p) c -> p j c", p=128)
    with contextlib.ExitStack() as es:
        sb = lambda n, s, d: es.enter_context(nc.sbuf_tensor(f"{n}_L{l}", s, d))
        ps = lambda n, s, d: es.enter_context(nc.psum_tensor(f"{n}_L{l}", s, d))
        KT = [(sb(f"da_kt{i}", [128, SEQ], BF16), T()) for i in range(2)]
        QT = [(sb(f"da_qt{i}", [128, SEQ], BF16), T()) for i in range(2)]
        V = [(sb(f"da_v{i}", [128, NT, 129], BF16), T()) for i in range(2)]
        braw = sb("da_braw", [128, 5, 512], F32)
        Tbraw = T()
        mpat = sb("da_mpat", [128, 5, 512], F32)
        Tmpat = T()
        biasT = [(sb(f"da_bias{i}", [128, 5, 512], BF16), T()) for i in range(2)]
        ering = Ring([(sb(f"da_e{i}", [128, 512], BF16), T()) for i in range(4)])
        om = [(sb(f"da_om{i}", [128, 4, 128], F32), T()) for i in range(2)]
        rec = [(sb(f"da_rec{i}", [128, 4], F32), T()) for i in range(2)]
        dlt = sb("da_dlt", [128, 4, 128], F32)
        Tdlt = T()
        junk = sb("da_junk", [128, 128], BF16)
        Tjunk = T()
        ss = sb("da_ss", [128, 8, 4], F32)
        Tss = T()
        gate = [(sb(f"da_g{i}", [128, 4, 128], F32), T()) for i in range(2)]
        gm = sb("da_gm", [128, 4, 128], F32)
        Tgm = T()
        tmp = sb("da_tmp", [128, 4, 128], F32)
        Ttmp = T()
        fin = [(sb(f"da_fin{i}", [128, 4, 128], BF16), T()) for i in range(2)]
        wsub = sb("da_wsub", [128, 4, 128], F32)
        Twsub = T()
        lamv = sb("da_lamv", [128, 4, 64], F32)
        Tlamv = T()
        lamp = sb("da_lamp", [128, 2, 64], F32)
        lams = sb("da_lams", [128, 8], F32)
        Tlams = T()
        sring = Ring([(ps(f"da_s{i}", [128, 512], F32), TP()) for i in range(3)])
        Oacc = [[(ps(f"da_o{m}{b}", [128, 512], F32)[:, 0:258].rearrange("p (a b) -> p a b", a=2), TP())
                 for b in range(2)] for m in range(2)]
        identb, Tidb = G["identb"]
        cst, Tcst = G["cst"]
        epsb = cst[:, 0:1]
        zerob = cst[:, 1:2]
        cfar, Tcfar = G["cfar"]

        c.dma("sp", lamv[:], G["lam_d"][l:l + 1, :, :].to_broadcast([128, 4, 64]), writes=[Tlamv])
        c.op("pool", lambda e: e.memset(lams[:], 0.0), writes=[Tlams])
        c.op("dve", lambda e: e.tensor_tensor(out=lamp[:, 0, :], in0=lamv[:, 0, :], in1=lamv[:, 1, :], op=ALU.mult),
             reads=[Tlamv], writes=[Tlamv])
        c.op("dve", lambda e: e.tensor_tensor(out=lamp[:, 1, :], in0=lamv[:, 2, :], in1=lamv[:, 3, :], op=ALU.mult),
             reads=[Tlamv], writes=[Tlamv])
        for i in range(2):
            c.op("act", lambda e: e.activation(out=lamv[:, i, :], in_=lamp[:, i, :], func=AF.Identity,
                                               accum_out=lams[:, i:i + 1]),
                 reads=[Tlamv], writes=[Tlamv, Tlams])
        c.op("act", lambda e: e.activation(out=lams[:, 2:4], in_=lams[:, 0:2], func=AF.Exp),
             reads=[Tlams], writes=[Tlams])
        c.op("dve", lambda e: e.tensor_tensor(out=lams[:, 4:5], in0=lams[:, 3:4], in1=lams[:, 2:3], op=ALU.subtract),
             reads=[Tlams], writes=[Tlams])
        c.op("dve", lambda e: e.tensor_scalar(out=lams[:, 5:6], in0=lams[:, 4:5], scalar1=-lam_init, scalar2=None,
                                              op0=ALU.add),
             reads=[Tlams], writes=[Tlams])
        neglam = lams[:, 5:6]
        c.dma("sp", wsub[:], G["subw_d"][l:l + 1, :].unsqueeze(1).to_broadcast([128, 4, 128]), writes=[Twsub])
        c.op("dve", lambda e: e.tensor_scalar(out=wsub[:], in0=wsub[:], scalar1=1.0 - lam_init, scalar2=None,
                                              op0=ALU.mult),
             reads=[Twsub], writes=[Twsub])
        c.dma("sp", mpat[:], G["mpat_d"].rearrange("a p q -> p a q"), writes=[Tmpat])
        for i in range(2):
            c.op("pool", lambda e: e.memset(V[i][0][:, :, 128:129], 1.0), writes=[V[i][1]])

        def load_head(hl):
            kt, Tkt = KT[hl % 2]
            qt, Tqt = QT[hl % 2]
            v, Tv = V[hl % 2]
            bT, TbT = biasT[hl % 2]
            c.dma("sp", kt[:], QKT[4 + hl, :, :], reads=[TQKT], writes=[Tkt])
            c.dma("sp", qt[:], QKT[hl, :, :], reads=[TQKT], writes=[Tqt])
            for a in range(4):
                c.dma("sp", v[:, a * 8:(a + 1) * 8, 0:128], DAVr[:, a * 8:(a + 1) * 8, hl * 128:(hl + 1) * 128],
                      reads=[TDAV], writes=[Tv])
            c.dma("sp", braw[:], G["braw_d"][hl].rearrange("a p q -> p a q"), writes=[Tbraw])
            c.op("dve", lambda e: e.scalar_tensor_tensor(out=bT[:], in0=braw[:], scalar=1.0 / SCALE, in1=mpat[:],
                                                         op0=ALU.mult, op1=ALU.add),
                 reads=[Tbraw, Tmpat], writes=[TbT])

        load_head(0)
        it = 0

        def emit_qk(item):
            hl, qb, m, kb = item
            kt, Tkt = KT[hl % 2]
            qt, Tqt = QT[hl % 2]
            bT, TbT = biasT[hl % 2]
            r0 = 64 * m
            sp_, Tsp = sring.next()
            delta = kb * 128 - qb * 512
            special = delta >= -128
            j0 = max(0, kb - 4 * qb)
            c.op("pe", lambda e: e.matmul(sp_[:, j0 * 128:512], lhsT=kt[r0:r0 + 64, kb * 128:(kb + 1) * 128],
                                          rhs=qt[r0:r0 + 64, qb * 512 + j0 * 128:(qb + 1) * 512],
                                          start=True, stop=not special),
                 reads=[Tkt, Tqt], writes=[Tsp])
            if special:
                pat = (delta + 128) // 128
                c.op("pe", lambda e: e.matmul(sp_[:, j0 * 128:512], lhsT=identb[:], rhs=bT[:, pat, j0 * 128:512],
                                              start=False, stop=True),
                     reads=[Tidb, TbT], writes=[Tsp])
            return (sp_, Tsp, special, j0)

        def emit_rest(item, qkres):
            hl, qb, m, kb = item
            v, Tv = V[hl % 2]
            sp_, Tsp, special, j0 = qkres
            E, TE = ering.next()
            bias_ap = zerob if special else cfar[:, hl:hl + 1]
            c.op("act", lambda e: e.activation(out=E[:, j0 * 128:512], in_=sp_[:, j0 * 128:512], func=AF.Exp,
                                               bias=bias_ap, scale=SCALE),
                 reads=[Tsp, Tcst, Tcfar], writes=[TE])
            for j in range(j0, 4):
                acc, Tacc = Oacc[m][j // 2]
                c.op("pe", lambda e: e.matmul(acc[:, j % 2, :], lhsT=E[:, j * 128:(j + 1) * 128],
                                              rhs=v[:, kb, :], start=(kb == 0 and j % 2 == 0),
                                              stop=(kb == qb * 4 + j)),
                     reads=[TE, Tv], writes=[Tacc])

        def finish_map(m):
            o_m, Tom = om[m]
            rc, Trc = rec[m]
            for b in range(2):
                acc, Tacc = Oacc[m][b]
                c.op("dve", lambda e: e.reciprocal(out=rc[:, 2 * b:2 * b + 2], in_=acc[:, :, 128]),
                     reads=[Tacc], writes=[Trc])
                c.op("dve", lambda e: e.tensor_tensor(
                    out=o_m[:, 2 * b:2 * b + 2, :], in0=acc[:, :, 0:128],
                    in1=rc[:, 2 * b:2 * b + 2].unsqueeze(2).to_broadcast([128, 2, 128]), op=ALU.mult),
                     reads=[Tacc, Trc], writes=[Tom])

        def finish_qb(hl, qb, g, Tg, f, Tf):
            c.op("dve", lambda e: e.scalar_tensor_tensor(out=dlt[:], in0=om[1][0][:], scalar=neglam,
                                                         in1=om[0][0][:], op0=ALU.mult, op1=ALU.add),
                 reads=[om[0][1], om[1][1], Tlams], writes=[Tdlt])
            c.op("pool", lambda e: e.memset(ss[:, 0:4, 0], 0.0), writes=[Tss])
            for j in range(4):
                c.op("act", lambda e: e.activation(out=junk[:], in_=dlt[:, j, :], func=AF.Square,
                                                   accum_out=ss[:, j, 0:1]),
                     reads=[Tdlt], writes=[Tjunk, Tss])
            rsqrt_chain(c, ss[:, 4:8, 0], ss[:, 0:4, 0], 1.0 / 128, epsb, ss[:, 0:4, 1],
                        [Tss, Tcst], [Tss], Tss)
            c.op("act", lambda e: e.activation(out=gm[:], in_=g[:], func=AF.Silu), reads=[Tg], writes=[Tgm])
            c.op("pool", lambda e: e.tensor_tensor(out=gm[:], in0=gm[:], in1=wsub[:], op=ALU.mult),
                 reads=[Tgm, Twsub], writes=[Tgm])
            c.op("dve", lambda e: e.tensor_tensor(out=tmp[:], in0=dlt[:],
                                                  in1=ss[:, 4:8, 0:1].to_broadcast([128, 4, 128]), op=ALU.mult),
                 reads=[Tdlt, Tss], writes=[Ttmp])
            c.op("dve", lambda e: e.tensor_tensor(out=f[:], in0=tmp[:], in1=gm[:], op=ALU.mult),
                 reads=[Ttmp, Tgm], writes=[Tf])
            c.dma("pool", MIXr[:, qb * 4:(qb + 1) * 4, hl * 128:(hl + 1) * 128], f[:], reads=[Tf], writes=[TMIXH])

        items = [(hl, qb, m, kb) for hl in range(4) for qb in range(8) for m in range(2)
                 for kb in range(4 * qb + 4)]
        qkres = emit_qk(items[0])
        cur_g = None
        deferred = []
        for idx, item in enumerate(items):
            hl, qb, m, kb = item
            for dfr in deferred:
                dfr[0] -= 1
            while deferred and deferred[0][0] <= 0:
                deferred.pop(0)[1]()
            if kb == 0 and m == 0:
                if qb == 0 and hl + 1 < 4:
                    load_head(hl + 1)
                cur_g = (gate[it % 2], fin[it % 2])
                it += 1
                if G["pending"]:
                    G["pending"].pop(0)()
                c.dma("sp", cur_g[0][0][:], DAGr[:, qb * 4:(qb + 1) * 4, hl * 128:(hl + 1) * 128],
                      reads=[TDAG], writes=[cur_g[0][1]])
            nxt = emit_qk(items[idx + 1]) if idx + 1 < len(items) else None
            emit_rest(item, qkres)
            qkres = nxt
            if kb == 4 * qb + 3:
                finish_map(m)
                if m == 1:
                    deferred.append([3, (lambda a=(hl, qb, cur_g[0][0], cur_g[0][1], cur_g[1][0], cur_g[1][1]):
                                         finish_qb(*a))])
        while deferred:
            deferred.pop(0)[1]()
        while G["pending"]:
            G["pending"].pop(0)()
        c.barrier()


class _Stop(Exception):
    pass


def _chk(level):
    import os
    if int(os.environ.get("GDN_STOP", "99")) == level:
        _chk.c.barrier()
        raise _Stop()


def stage_gdn(c, nc, G, l):
    with contextlib.ExitStack() as es:
        try:
            _stage_gdn(c, nc, G, l, es)
        except _Stop:
            pass


def _stage_gdn(c, nc, G, l, es):
    _chk.c = c
    GQKV, TGQKV = G["GQKV"]
    GZ, TGZ = G["GZ"]
    GBA, TGBA = G["GBA"]
    MIXH, TMIXH = G["MIXH"]
    if True:
        sb = lambda n, s, d: es.enter_context(nc.sbuf_tensor(f"{n}_L{l}", s, d))
        ps = lambda n, s, d: es.enter_context(nc.psum_tensor(f"{n}_L{l}", s, d))
        cst, Tcst = G["cst"]
        epsb = cst[:, 0:1]
        oneb = cst[:, 2:3]
        identb, Tidb = G["identb"]
        identf, Tidf = G["identf"]
        Uf, TUf = G["Uf"]
        onesf, Tonesf = G["onesf"]
        negonesf, Tnegonesf = G["negonesf"]
        onesb, Tonesb = G["onesb"]
        negmask, Tnegmask = G["negmask"]
        strict, Tstrict = G["strict"]
        cw, Tcw = G["cw"][l]
        hpar, Thpar = G["hpar"][l]

        pring = Ring([(ps(f"gd_p{i}", [128, 4, 128], F32), TP()) for i in range(4)])
        sbanks = [ps(f"gd_scan{i}", [128, 4, 128], F32) for i in range(3)]
        Tsb = [TP() for _ in range(3)]
        ws_ps = [(sbanks[0][:, h, :], Tsb[0]) for h in range(4)]
        O_ps = [(sbanks[1][:, h, :], Tsb[1]) for h in range(4)]
        Sd_ps = [(sbanks[2][:, h, :], Tsb[2]) for h in range(4)]
        trbank = ps("gd_tr", [128, 8, 128], BF16)
        Ttr = TP()

        ba = sb("gd_ba", [128, NT, 8], F32)
        Tba = T()
        sc = {}
        for nm in ("beta", "negb", "xa", "nx", "mn", "ex", "lg", "mx", "g", "gc", "eg", "egl", "ekd", "dd"):
            sc[nm] = sb("gd_sc_" + nm, [128, NT, 4], F32)
        Tsc = T()
        nea = sb("gd_nea", [128, 4], F32)
        GBAr = GBA.rearrange("(n p) j -> p n j", p=128)
        for a in range(8):
            c.dma("sp", ba[:, a * 4:(a + 1) * 4, :], GBAr[:, a * 4:(a + 1) * 4, :], reads=[TGBA], writes=[Tba])
        c.op("act", lambda e: e.activation(out=sc["beta"][:], in_=ba[:, :, 0:4], func=AF.Sigmoid),
             reads=[Tba], writes=[Tsc])
        c.op("dve", lambda e: e.tensor_scalar(out=sc["negb"][:], in0=sc["beta"][:], scalar1=-1.0, scalar2=None,
                                              op0=ALU.mult), reads=[Tsc], writes=[Tsc])
        c.op("dve", lambda e: e.tensor_tensor(out=sc["xa"][:], in0=ba[:, :, 4:8],
                                              in1=hpar[:, 4:8].unsqueeze(1).to_broadcast([128, NT, 4]), op=ALU.add),
             reads=[Tba, Thpar], writes=[Tsc])
        c.op("dve", lambda e: e.tensor_scalar(out=sc["nx"][:], in0=sc["xa"][:], scalar1=-1.0, scalar2=None,
                                              op0=ALU.mult), reads=[Tsc], writes=[Tsc])
        c.op("dve", lambda e: e.tensor_tensor(out=sc["mn"][:], in0=sc["xa"][:], in1=sc["nx"][:], op=ALU.min),
             reads=[Tsc], writes=[Tsc])
        c.op("act", lambda e: e.activation(out=sc["ex"][:], in_=sc["mn"][:], func=AF.Exp), reads=[Tsc], writes=[Tsc])
        c.op("act", lambda e: e.activation(out=sc["lg"][:], in_=sc["ex"][:], func=AF.Ln, bias=oneb),
             reads=[Tsc, Tcst], writes=[Tsc])
        c.op("dve", lambda e: e.tensor_scalar(out=sc["mx"][:], in0=sc["xa"][:], scalar1=0.0, scalar2=None,
                                              op0=ALU.max), reads=[Tsc], writes=[Tsc])
        c.op("dve", lambda e: e.tensor_tensor(out=sc["lg"][:], in0=sc["lg"][:], in1=sc["mx"][:], op=ALU.add),
             reads=[Tsc], writes=[Tsc])
        c.op("act", lambda e: e.activation(out=nea[:], in_=hpar[:, 0:4], func=AF.Exp), reads=[Thpar], writes=[Tsc])
        c.op("dve", lambda e: e.tensor_scalar(out=nea[:], in0=nea[:], scalar1=-1.0, scalar2=None, op0=ALU.mult),
             reads=[Tsc], writes=[Tsc])
        c.op("dve", lambda e: e.tensor_tensor(out=sc["g"][:], in0=sc["lg"][:],
                                              in1=nea[:].unsqueeze(1).to_broadcast([128, NT, 4]), op=ALU.mult),
             reads=[Tsc], writes=[Tsc])
        gflat = sc["g"][:].rearrange("p n h -> p (n h)")
        bkA, TpA = pring.next()
        bkB, TpB = pring.next()
        pA = bkA[:].rearrange("p a b -> p (a b)")[:, 0:128]
        pB = bkB[:].rearrange("p a b -> p (a b)")[:, 0:128]
        c.op("pe", lambda e: e.matmul(pA, lhsT=Uf[:], rhs=gflat, start=True, stop=True),
             reads=[TUf, Tsc], writes=[TpA])
        c.op("pe", lambda e: e.matmul(pB, lhsT=onesf[:], rhs=gflat, start=True, stop=True),
             reads=[Tonesf, Tsc], writes=[TpB])
        fl = lambda nm: sc[nm][:].rearrange("p n h -> p (n h)")
        c.op("dve", lambda e: e.tensor_copy(out=fl("gc"), in_=pA), reads=[TpA], writes=[Tsc])
        c.op("act", lambda e: e.activation(out=fl("eg"), in_=pA, func=AF.Exp), reads=[TpA], writes=[Tsc])
        c.op("act", lambda e: e.activation(out=fl("egl"), in_=pB, func=AF.Exp), reads=[TpB], writes=[Tsc])
        c.op("dve", lambda e: e.tensor_tensor(out=fl("dd"), in0=pB, in1=fl("gc"), op=ALU.subtract),
             reads=[TpB, Tsc], writes=[Tsc])
        c.op("act", lambda e: e.activation(out=fl("ekd"), in_=fl("dd"), func=AF.Exp), reads=[Tsc], writes=[Tsc])

        _chk(1)
        Xr = Ring([(sb(f"gd_X{i}", [128, 515], F32), T()) for i in range(3)])
        yr = Ring([(sb(f"gd_y{i}", [128, 512], F32), T()) for i in range(2)])
        sr = Ring([(sb(f"gd_s{i}", [128, 512], F32), T()) for i in range(2)])
        sqr = Ring([(sb(f"gd_sq{i}", [128, 512], BF16), T()) for i in range(2)])
        rnr = Ring([(sb(f"gd_rn{i}", [128, 512], F32), T()) for i in range(2)])
        lnr = Ring([(sb(f"gd_ln{i}", [128, 512], F32), T()) for i in range(2)])
        qkvT = [[(sb(f"gd_qkv{p}_{t}", [128, 4, 512], BF16), T()) for t in range(3)] for p in range(2)]
        zt = [(sb(f"gd_z{i}", [128, 512], F32), T()) for i in range(2)]
        gzp = [(sb(f"gd_gz{i}", [128, 512], F32), T()) for i in range(2)]
        mixs = [(sb(f"gd_mix{i}", [128, 512], BF16), T()) for i in range(2)]
        gnw4 = sb("gd_gnw4", [128, 4, 128], F32)
        Tgnw4 = T()
        c.dma("sp", gnw4[:], G["gnw_d"][l:l + 1, :].unsqueeze(1).to_broadcast([128, 4, 128]), writes=[Tgnw4])
        S32 = [(sb(f"gd_S32_{h}", [128, 128], F32), T()) for h in range(4)]
        Sb = [(sb(f"gd_Sb_{h}", [128, 128], BF16), T()) for h in range(4)]
        for h in range(4):
            c.op("pool", lambda e: e.memset(S32[h][0][:], 0.0), writes=[S32[h][1]])
            c.op("pool", lambda e: e.memset(Sb[h][0][:], 0.0), writes=[Sb[h][1]])
        ost = sb("gd_ost", [128, NT, 4, 4], F32)
        Tost = [[T() for _ in range(4)] for _ in range(NT)]
        c.op("pool", lambda e: e.memset(ost[:], 0.0), writes=[t for row in Tost for t in row])
        junk = sb("gd_junk", [128, 128], BF16)
        Tjunk = T()

        def mk(nm, dt):
            return [[(sb(f"gd_{nm}_{p}_{h}", [128, 128], dt), T()) for h in range(4)] for p in range(2)]
        B_kg, B_kd, B_vt = mk("kg", BF16), mk("kd", BF16), mk("vt", BF16)
        B_Gm, B_egbc, B_dTi, B_dTs = mk("Gm", F32), mk("egbc", F32), mk("dTi", F32), mk("dTs", F32)
        B_N, B_NT, B_X = mk("N", BF16), mk("NT", BF16), mk("X", BF16)
        B_P = [mk("P0", BF16), mk("P1", BF16)]
        B_PT = [mk("PT0", BF16), mk("PT1", BF16)]
        B_qk, B_qg, B_u, B_w, B_vn = mk("qk", BF16), mk("qg", BF16), mk("u", F32), mk("w", BF16), mk("vn", BF16)

        for blk in range(8):
            par = blk % 2
            qT, kT, vT = qkvT[par]
            for r in range(12):
                t, hl = r // 4, r % 4
                X, TX = Xr.next()
                if blk == 0:
                    c.op("pool", lambda e: e.memset(X[:, 0:3], 0.0), writes=[TX])
                    c.dma("sp", X[:, 3:515], GQKV[r, :, 0:512], reads=[TGQKV], writes=[TX])
                else:
                    c.dma("sp", X[:], GQKV[r, :, blk * 512 - 3:blk * 512 + 512], reads=[TGQKV], writes=[TX])
                y, Ty = yr.next()
                c.op("dve", lambda e: e.tensor_scalar(out=y[:], in0=X[:, 0:512], scalar1=cw[:, r, 0:1], scalar2=None,
                                                      op0=ALU.mult), reads=[TX, Tcw], writes=[Ty])
                for j in range(1, 4):
                    c.op("dve", lambda e: e.scalar_tensor_tensor(out=y[:], in0=X[:, j:j + 512],
                                                                 scalar=cw[:, r, j:j + 1], in1=y[:],
                                                                 op0=ALU.mult, op1=ALU.add),
                         reads=[TX, Tcw, Ty], writes=[Ty])
                if t == 2:
                    c.op("act", lambda e: e.activation(out=vT[0][:, hl, :], in_=y[:], func=AF.Silu),
                         reads=[Ty], writes=[vT[1]])
                    continue
                s, Ts = sr.next()
                c.op("act", lambda e: e.activation(out=s[:], in_=y[:], func=AF.Silu), reads=[Ty], writes=[Ts])
                sq, Tsq = sqr.next()
                c.op("pool", lambda e: e.tensor_tensor(out=sq[:], in0=s[:], in1=s[:], op=ALU.mult),
                     reads=[Ts], writes=[Tsq])
                bkss, Tssq = pring.next()
                ssq_ps = bkss[:].rearrange("p a b -> p (a b)")
                c.op("pe", lambda e: e.matmul(ssq_ps, lhsT=onesb[:], rhs=sq[:], start=True, stop=True),
                     reads=[Tonesb, Tsq], writes=[Tssq])
                rn, Trn = rnr.next()
                ln_, Tln = lnr.next()
                rsqrt_chain(c, rn[:], ssq_ps, 1.0, epsb, ln_[:], [Tssq, Tcst], [Trn], Tln)
                dstT = qT if t == 0 else kT
                scl = (128.0 ** -0.5) if t == 0 else 1.0
                c.op("dve", lambda e: e.scalar_tensor_tensor(out=dstT[0][:, hl, :], in0=s[:], scalar=scl, in1=rn[:],
                                                             op0=ALU.mult, op1=ALU.mult),
                     reads=[Ts, Trn], writes=[dstT[1]])

            _chk(2)
            for pair in range(2):
                CH = []
                for co in range(2):
                    cc = pair * 2 + co
                    n = blk * 4 + cc
                    CH.append((n, n % 2, slice(cc * 128, (cc + 1) * 128)))
                col = lambda nm, n, h: sc[nm][:, n, h:h + 1]
                for (n, cp, cs) in CH:
                    z, Tz = zt[cp]
                    gz, Tgz = gzp[cp]
                    c.dma("sp", z[:], GZ[n * 128:(n + 1) * 128, :], reads=[TGZ], writes=[Tz])
                    c.op("act", lambda e: e.activation(out=gz[:], in_=z[:], func=AF.Silu), reads=[Tz], writes=[Tgz])
                    c.op("pool", lambda e: e.tensor_tensor(out=gz[:], in0=gz[:],
                                                           in1=gnw4[:].rearrange("p a b -> p (a b)"), op=ALU.mult),
                         reads=[Tgz, Tgnw4], writes=[Tgz])
                _chk(25)
                for (n, cp, cs) in CH:
                    for h in range(4):
                        c.op("pe", lambda e: e.transpose(trbank[:, 2 * h, :], kT[0][:, h, cs], identb[:]),
                             reads=[kT[1], Tidb], writes=[Ttr])
                        c.op("pe", lambda e: e.transpose(trbank[:, 2 * h + 1, :], vT[0][:, h, cs], identb[:]),
                             reads=[vT[1], Tidb], writes=[Ttr])
                    for h in range(4):
                        kg, Tkg = B_kg[cp][h]
                        kd, Tkd = B_kd[cp][h]
                        vt, Tvt = B_vt[cp][h]
                        p1, p2 = trbank[:, 2 * h, :], trbank[:, 2 * h + 1, :]
                        evac_scale(c, "act", kg[:], p1, col("eg", n, h), [Ttr, Tsc], [Tkg])
                        evac_scale(c, "act", kd[:], p1, col("ekd", n, h), [Ttr, Tsc], [Tkd])
                        evac_copy(c, "act", vt[:], p2, [Ttr], [Tvt])
                _chk(3)
                bks = {}
                for (n, cp, cs) in CH:
                    bks[n] = (pring.next(), pring.next())
                    (bka, Tpa), (bkb, Tpb) = bks[n]
                    for h in range(4):
                        Gm, TGm = B_Gm[cp][h]
                        c.op("dve", lambda e: e.tensor_scalar(out=Gm[:], in0=Uf[:], scalar1=col("g", n, h),
                                                              scalar2=None, op0=ALU.mult),
                             reads=[TUf, Tsc], writes=[TGm])
                        pa = bka[:, h, :]
                        pb = bkb[:, h, :]
                        c.op("pe", lambda e: e.matmul(pa, lhsT=onesf[:], rhs=Gm[:], start=True, stop=True),
                             reads=[Tonesf, TGm], writes=[Tpa])
                        c.op("pe", lambda e: e.matmul(pb, lhsT=onesf[:], rhs=Gm[:], start=True, stop=False),
                             reads=[Tonesf, TGm], writes=[Tpb])
                        c.op("pe", lambda e: e.matmul(pb, lhsT=Gm[:], rhs=negonesf[:], start=False, stop=False),
                             reads=[Tnegonesf, TGm], writes=[Tpb])
                        c.op("pe", lambda e: e.matmul(pb, lhsT=identf[:], rhs=negmask[:], start=False, stop=True),
                             reads=[Tidf, Tnegmask], writes=[Tpb])
                for (n, cp, cs) in CH:
                    (bka, Tpa), (bkb, Tpb) = bks[n]
                    for h in range(4):
                        egbc, Tegbc = B_egbc[cp][h]
                        dTi, TdTi = B_dTi[cp][h]
                        dTs, TdTs = B_dTs[cp][h]
                        c.op("act", lambda e: e.activation(out=egbc[:], in_=bka[:, h, :], func=AF.Exp),
                             reads=[Tpa], writes=[Tegbc])
                        c.op("act", lambda e: e.activation(out=dTi[:], in_=bkb[:, h, :], func=AF.Exp),
                             reads=[Tpb], writes=[TdTi])
                        c.op("pool", lambda e: e.tensor_tensor(out=dTs[:], in0=dTi[:], in1=strict[:], op=ALU.mult),
                             reads=[TdTi, Tstrict], writes=[TdTs])
                _chk(4)
                for (n, cp, cs) in CH:
                    bks[n] = (pring.next(), pring.next())
                    (bkk, Tpk), (bkq, Tpq) = bks[n]
                    for h in range(4):
                        c.op("pe", lambda e: e.matmul(bkk[:, h, :], lhsT=kT[0][:, h, cs], rhs=kT[0][:, h, cs],
                                                      start=True, stop=True),
                             reads=[kT[1]], writes=[Tpk])
                        c.op("pe", lambda e: e.matmul(bkq[:, h, :], lhsT=kT[0][:, h, cs], rhs=qT[0][:, h, cs],
                                                      start=True, stop=True),
                             reads=[kT[1], qT[1]], writes=[Tpq])
                for (n, cp, cs) in CH:
                    (bkk, Tpk), (bkq, Tpq) = bks[n]
                    for h in range(4):
                        N_, TN = B_N[cp][h]
                        qk, Tqk = B_qk[cp][h]
                        qg, Tqg = B_qg[cp][h]
                        dTi, TdTi = B_dTi[cp][h]
                        dTs, TdTs = B_dTs[cp][h]
                        egbc, Tegbc = B_egbc[cp][h]
                        c.op("dve", lambda e: e.scalar_tensor_tensor(out=N_[:], in0=bkk[:, h, :],
                                                                     scalar=col("beta", n, h), in1=dTs[:],
                                                                     op0=ALU.mult, op1=ALU.mult),
                             reads=[Tpk, Tsc, TdTs], writes=[TN])
                        c.op("dve", lambda e: e.tensor_tensor(out=qk[:], in0=bkq[:, h, :], in1=dTi[:], op=ALU.mult),
                             reads=[Tpq, TdTi], writes=[Tqk])
                        c.op("pool", lambda e: e.tensor_tensor(out=qg[:], in0=qT[0][:, h, cs], in1=egbc[:],
                                                               op=ALU.mult),
                             reads=[qT[1], Tegbc], writes=[Tqg])
                _chk(5)
                for (n, cp, cs) in CH:
                    bkt, Tpt = pring.next()
                    bks[n] = (bkt, Tpt)
                    for h in range(4):
                        N_, TN = B_N[cp][h]
                        X_, TX_ = B_X[cp][h]
                        c.op("pool", lambda e: e.tensor_tensor(out=X_[:], in0=identb[:], in1=N_[:], op=ALU.subtract),
                             reads=[Tidb, TN], writes=[TX_])
                        c.op("pe", lambda e: e.matmul(bkt[:, h, :], lhsT=N_[:], rhs=identb[:], start=True, stop=True),
                             reads=[TN, Tidb], writes=[Tpt])
                for (n, cp, cs) in CH:
                    bkt, Tpt = bks[n]
                    for h in range(4):
                        NT_, TNT = B_NT[cp][h]
                        evac_copy(c, "act", NT_[:], bkt[:, h, :], [Tpt], [TNT])
                for k in range(1, 7):
                    bt, bp, bx = {}, {}, {}
                    for (n, cp, cs) in CH:
                        bt[n] = pring.next()
                        for h in range(4):
                            Pp, TPp = (B_N[cp][h] if k == 1 else B_P[(k - 1) % 2][cp][h])
                            PTp, TPTp = (B_NT[cp][h] if k == 1 else B_PT[(k - 1) % 2][cp][h])
                            c.op("pe", lambda e: e.matmul(bt[n][0][:, h, :], lhsT=Pp[:], rhs=PTp[:],
                                                          start=True, stop=True),
                                 reads=[TPp, TPTp], writes=[bt[n][1]])
                    for (n, cp, cs) in CH:
                        for h in range(4):
                            PTn, TPTn = B_PT[k % 2][cp][h]
                            evac_copy(c, "act", PTn[:], bt[n][0][:, h, :], [bt[n][1]], [TPTn])
                    if k < 6:
                        for (n, cp, cs) in CH:
                            bp[n] = pring.next()
                            for h in range(4):
                                Pp, TPp = (B_N[cp][h] if k == 1 else B_P[(k - 1) % 2][cp][h])
                                PTp, TPTp = (B_NT[cp][h] if k == 1 else B_PT[(k - 1) % 2][cp][h])
                                c.op("pe", lambda e: e.matmul(bp[n][0][:, h, :], lhsT=PTp[:], rhs=Pp[:],
                                                              start=True, stop=True),
                                     reads=[TPp, TPTp], writes=[bp[n][1]])
                        for (n, cp, cs) in CH:
                            for h in range(4):
                                Pn, TPn = B_P[k % 2][cp][h]
                                evac_copy(c, "dve", Pn[:], bp[n][0][:, h, :], [bp[n][1]], [TPn])
                    for (n, cp, cs) in CH:
                        bx[n] = pring.next()
                        for h in range(4):
                            PTn, TPTn = B_PT[k % 2][cp][h]
                            X_, TX_ = B_X[cp][h]
                            c.op("pe", lambda e: e.matmul(bx[n][0][:, h, :], lhsT=PTn[:], rhs=X_[:],
                                                          start=True, stop=True),
                                 reads=[TPTn, TX_], writes=[bx[n][1]])
                    for (n, cp, cs) in CH:
                        for h in range(4):
                            X_, TX_ = B_X[cp][h]
                            c.op("dve", lambda e: e.tensor_tensor(out=X_[:], in0=bx[n][0][:, h, :], in1=X_[:],
                                                                  op=ALU.add),
                                 reads=[bx[n][1], TX_], writes=[TX_])
                _chk(6)
                for (n, cp, cs) in CH:
                    bks[n] = (pring.next(), pring.next())
                    (bku, Tpu), (bkw, Tpw) = bks[n]
                    for h in range(4):
                        X_, TX_ = B_X[cp][h]
                        c.op("pe", lambda e: e.matmul(bku[:, h, :], lhsT=X_[:], rhs=B_vt[cp][h][0][:],
                                                      start=True, stop=True),
                             reads=[TX_, B_vt[cp][h][1]], writes=[Tpu])
                        c.op("pe", lambda e: e.matmul(bkw[:, h, :], lhsT=B_kg[cp][h][0][:], rhs=X_[:],
                                                      start=True, stop=True),
                             reads=[TX_, B_kg[cp][h][1]], writes=[Tpw])
                for (n, cp, cs) in CH:
                    (bku, Tpu), (bkw, Tpw) = bks[n]
                    for h in range(4):
                        u, Tu = B_u[cp][h]
                        w, Tw = B_w[cp][h]
                        evac_scale(c, "dve", u[:], bku[:, h, :], col("beta", n, h), [Tpu, Tsc], [Tu])
                        evac_copy(c, "act", w[:], bkw[:, h, :], [Tpw], [Tw])
                _chk(7)
                for (n, cp, cs) in CH:
                    gz, Tgz = gzp[cp]
                    mx, Tmx = mixs[cp]
                    for h in range(4):
                        c.op("pe", lambda e: e.matmul(ws_ps[h][0], lhsT=B_w[cp][h][0][:], rhs=Sb[h][0][:],
                                                      start=True, stop=True),
                             reads=[B_w[cp][h][1], Sb[h][1]], writes=[ws_ps[h][1]])
                    for h in range(4):
                        vn, Tvn = B_vn[cp][h]
                        c.op("dve", lambda e: e.scalar_tensor_tensor(out=vn[:], in0=ws_ps[h][0],
                                                                     scalar=col("negb", n, h),
                                                                     in1=B_u[cp][h][0][:], op0=ALU.mult, op1=ALU.add),
                             reads=[ws_ps[h][1], Tsc, B_u[cp][h][1]], writes=[Tvn])
                    for h in range(4):
                        vn, Tvn = B_vn[cp][h]
                        c.op("pe", lambda e: e.matmul(Sd_ps[h][0], lhsT=B_kd[cp][h][0][:], rhs=vn[:],
                                                      start=True, stop=True),
                             reads=[B_kd[cp][h][1], Tvn], writes=[Sd_ps[h][1]])
                    for h in range(4):
                        vn, Tvn = B_vn[cp][h]
                        c.op("pe", lambda e: e.matmul(O_ps[h][0], lhsT=B_qg[cp][h][0][:], rhs=Sb[h][0][:],
                                                      start=True, stop=False),
                             reads=[B_qg[cp][h][1], Sb[h][1]], writes=[O_ps[h][1]])
                        c.op("pe", lambda e: e.matmul(O_ps[h][0], lhsT=B_qk[cp][h][0][:], rhs=vn[:],
                                                      start=False, stop=True),
                             reads=[B_qk[cp][h][1], Tvn], writes=[O_ps[h][1]])
                    for h in range(4):
                        c.op("dve", lambda e: e.scalar_tensor_tensor(out=Sb[h][0][:], in0=S32[h][0][:],
                                                                     scalar=col("egl", n, h), in1=Sd_ps[h][0],
                                                                     op0=ALU.mult, op1=ALU.add),
                             reads=[S32[h][1], Tsc, Sd_ps[h][1]], writes=[Sb[h][1]])
                    for h in range(4):
                        c.op("dve", lambda e: e.scalar_tensor_tensor(out=S32[h][0][:], in0=S32[h][0][:],
                                                                     scalar=col("egl", n, h), in1=Sd_ps[h][0],
                                                                     op0=ALU.mult, op1=ALU.add),
                             reads=[S32[h][1], Tsc, Sd_ps[h][1]], writes=[S32[h][1]])
                    for h in range(4):
                        To = Tost[n][h]
                        c.op("act", lambda e: e.activation(out=junk[:], in_=O_ps[h][0], func=AF.Square,
                                                           accum_out=ost[:, n, h, 0:1]),
                             reads=[O_ps[h][1]], writes=[Tjunk, To])
                        rsqrt_chain(c, ost[:, n, h, 2:3], ost[:, n, h, 0:1], 1.0 / 128, epsb, ost[:, n, h, 1:2],
                                    [To, Tcst], [To], To)
                    for h in range(4):
                        To = Tost[n][h]
                        c.op("dve", lambda e: e.scalar_tensor_tensor(out=mx[:, h * 128:(h + 1) * 128],
                                                                     in0=O_ps[h][0], scalar=ost[:, n, h, 2:3],
                                                                     in1=gz[:, h * 128:(h + 1) * 128],
                                                                     op0=ALU.mult, op1=ALU.mult),
                             reads=[O_ps[h][1], To, Tgz], writes=[Tmx])
                    c.dma("pool", MIXH[n * 128:(n + 1) * 128, 512:1024], mx[:], reads=[Tmx], writes=[TMIXH])
                _chk(8)
        c.barrier()


def build_program(mode, layers=(0, 1), dbg_stages=None):
    nc = bass.Bass("TRN2", target_bir_lowering=False)
    dram = lambda n, s, d, kind: nc.dram_tensor(n, s, d, kind=kind).ap()
    IN, OUT, INT = "ExternalInput", "ExternalOutput", "Internal"
    SK = OUT if mode == "dbg" else INT
    G = {}
    need_in = {"fused": (0, 1), "p1": (0,), "p2": (1,), "p3": (), "dbg": tuple(layers)}[mode]
    need_out = {"fused": (0, 1), "p1": (), "p2": (0,), "p3": (1,), "dbg": tuple(layers)}[mode]
    with contextlib.ExitStack() as es:
        c = Ctx(nc, es)
        sb = lambda n, s, d: es.enter_context(nc.sbuf_tensor(n, s, d))
        win = {l: dram(f"win{l}", [128, 16 * NCOL], F32, IN) for l in need_in}
        wout = {l: dram(f"wout{l}", [128, 16, D], F32, IN) for l in need_out}
        G["Wbf"] = {l: dram(f"wbf{l}", [128, 16 * NCOL], BF16, INT) for l in need_in}
        G["TWbf"] = {l: [T() for _ in range(9)] for l in need_in}
        G["Wobf"] = {l: dram(f"wobf{l}", [128, 16, D], BF16, INT) for l in need_out}
        G["TWobf"] = {l: [T() for _ in range(16)] for l in need_out}
        gpk_d = dram("gpk_d", [DEPTH, 128, 16], F32, IN)
        cw_d = dram("cw_d", [DEPTH, 128, 12, 4], F32, IN)
        hpar_d = dram("hpar_d", [DEPTH, 128, 8], F32, IN)
        G["gnw_d"] = dram("gnw_d", [DEPTH, 128], F32, IN)
        G["subw_d"] = dram("subw_d", [DEPTH, 128], F32, IN)
        G["lam_d"] = dram("lam_d", [DEPTH, 4, 64], F32, IN)
        G["fnw_d"] = dram("fnw_d", [1, D], F32, IN)
        G["braw_d"] = dram("braw_d", [4, 5, 128, 512], F32, IN)
        G["mpat_d"] = dram("mpat_d", [5, 128, 512], F32, IN)
        cfar_d = dram("cfar_d", [128, 4], F32, IN)
        cmat_d = dram("cmat_d", [6, 128, 128], F32, IN)
        if mode == "dbg" and "inproj" not in dbg_stages:
            SK = IN
        G["QKT"] = (dram("s_qkt", [8, 128, SEQ], BF16, SK), T())
        G["GQKV"] = (dram("s_gqkv", [12, 128, SEQ], F32, SK), T())
        G["DAV"] = (dram("s_dav", [SEQ, 512], BF16, SK), T())
        G["DAG"] = (dram("s_dag", [SEQ, 512], F32, SK), T())
        G["GZ"] = (dram("s_gz", [SEQ, 512], F32, SK), T())
        G["GBA"] = (dram("s_gba", [SEQ, 8], F32, SK), T())
        cst = sb("cst", [128, 4], F32)
        Tcst = T()
        c.op("pool", lambda e: e.memset(cst[:, 0:1], RMS_EPS), writes=[Tcst])
        c.op("pool", lambda e: e.memset(cst[:, 1:2], 0.0), writes=[Tcst])
        c.op("pool", lambda e: e.memset(cst[:, 2:3], 1.0), writes=[Tcst])
        G["cst"] = (cst, Tcst)
        cm = sb("cmat", [128, 6, 128], F32)
        Tcm = T()
        c.dma("sp", cm[:], cmat_d.rearrange("a p q -> p a q"), writes=[Tcm])
        G["identf"] = (cm[:, 0, :], Tcm)
        G["Uf"] = (cm[:, 1, :], Tcm)
        G["onesf"] = (cm[:, 2, :], Tcm)
        G["negonesf"] = (cm[:, 3, :], Tcm)
        G["negmask"] = (cm[:, 4, :], Tcm)
        G["strict"] = (cm[:, 5, :], Tcm)
        identb = sb("identb", [128, 128], BF16)
        onesb = sb("onesb", [128, 128], BF16)
        Tib, Tob = T(), T()
        c.op("dve", lambda e: e.tensor_copy(out=identb[:], in_=cm[:, 0, :]), reads=[Tcm], writes=[Tib])
        c.op("dve", lambda e: e.tensor_copy(out=onesb[:], in_=cm[:, 2, :]), reads=[Tcm], writes=[Tob])
        G["identb"] = (identb, Tib)
        G["onesb"] = (onesb, Tob)
        cfar = sb("cfar", [128, 4], F32)
        Tcfar = T()
        c.dma("sp", cfar[:], cfar_d, writes=[Tcfar])
        G["cfar"] = (cfar, Tcfar)
        G["gpk"], G["cw"], G["hpar"] = {}, {}, {}
        for l in range(DEPTH):
            t1 = sb(f"gpk{l}", [128, 16], F32)
            t2 = sb(f"cw{l}", [128, 12, 4], F32)
            t3 = sb(f"hpar{l}", [128, 8], F32)
            T1, T2, T3 = T(), T(), T()
            c.dma("sp", t1[:], gpk_d[l], writes=[T1])
            c.dma("sp", t2[:], cw_d[l], writes=[T2])
            c.dma("sp", t3[:], hpar_d[l], writes=[T3])
            G["gpk"][l], G["cw"][l], G["hpar"][l] = (t1, T1), (t2, T2), (t3, T3)
        def cast_in(l, cb):
            a, b = (cb * 8192, (cb + 1) * 8192) if cb < 8 else (65536, 16 * NCOL)
            return lambda: c.dma("pool", G["Wbf"][l][:, a:b], win[l][:, a:b], writes=[G["TWbf"][l][cb]])

        def cast_out(l, k):
            return lambda: c.dma("pool", G["Wobf"][l][:, k, :], wout[l][:, k, :], writes=[G["TWobf"][l][k]])
        G["pending"] = []
        if mode == "fused":
            for k in range(9):
                cast_in(0, k)()
        else:
            for l in need_in:
                for k in range(9):
                    cast_in(l, k)()
            for l in need_out:
                for k in range(16):
                    cast_out(l, k)()

        if mode == "dbg":
            l = layers[0]
            x_in = dram("x_in", [SEQ, D], F32, IN)
            G["MIXH"] = (dram("mixh", [SEQ, 1024], BF16, OUT), T())
            mixf = dram("mixf_in", [2, SEQ, 1024], BF16, IN)
            x1 = dram("x1", [SEQ, D], F32, OUT)
            if "inproj" in dbg_stages:
                stage_inproj(c, nc, G, l, x_in, T())
            if "da" in dbg_stages:
                stage_da(c, nc, G, l)
            if "gdn" in dbg_stages:
                stage_gdn(c, nc, G, l)
            if "outproj" in dbg_stages:
                stage_outproj(c, nc, G, l, x_in, T(), mixf, T(), x1, T(), 0, NT, False)
        elif mode == "p1":
            x_in = dram("x_in", [SEQ, D], F32, IN)
            G["MIXH"] = (dram("mixh", [SEQ, 1024], BF16, OUT), T())
            stage_inproj(c, nc, G, 0, x_in, T())
            stage_da(c, nc, G, 0)
            stage_gdn(c, nc, G, 0)
        elif mode == "p2":
            x_in = dram("x_in", [SEQ, D], F32, IN)
            mixf = dram("mixf_in", [2, SEQ, 1024], BF16, IN)
            x1 = dram("x1", [SEQ, D], F32, OUT)
            Tx1 = T()
            G["MIXH"] = (dram("mixh", [SEQ, 1024], BF16, OUT), T())
            stage_outproj(c, nc, G, 0, x_in, T(), mixf, T(), x1, Tx1, 0, NT, False)
            stage_inproj(c, nc, G, 1, x1, Tx1)
            stage_da(c, nc, G, 1)
            stage_gdn(c, nc, G, 1)
        elif mode == "p3":
            x_in = dram("x_in", [SEQ // 2, D], F32, IN)
            mixf = dram("mixf_in", [2, SEQ // 2, 1024], BF16, IN)
            out = dram("out", [SEQ // 2, D], F32, OUT)
            stage_outproj(c, nc, G, 1, x_in, T(), mixf, T(), out, T(), 0, NT // 2, True)
        elif mode == "fused":
            x_in = dram("x_in", [SEQ, D], F32, IN)
            out = dram("out", [SEQ, D], F32, OUT)
            G["MIXH"] = (dram("s_mixh", [SEQ, 1024], BF16, INT), T())
            mixf4, Tmixf = dram("s_mixf", [4, 2, 1024, 1024], BF16, INT), T()
            mixf = lambda r, tok: mixf4[tok // 1024, r, tok % 1024:tok % 1024 + 128, :]
            x1, Tx1 = dram("s_x1", [SEQ, D], F32, INT), T()
            Tx0 = T()
            groups = [[2 * i, 2 * i + 1] for i in range(NB)]
            for l in range(DEPTH):
                src, Tsrc = (x_in, Tx0) if l == 0 else (x1, Tx1)
                stage_inproj(c, nc, G, l, src, Tsrc)
                G["pending"] += [cast_out(l, k) for k in range(16)]
                if l + 1 < DEPTH:
                    G["pending"] += [cast_in(l + 1, k) for k in range(9)]
                stage_da(c, nc, G, l)
                stage_gdn(c, nc, G, l)
                for ch in range(4):
                    c.collective(lambda e: e.collective_compute(
                        "AllGather", ALU.bypass, replica_groups=groups,
                        ins=[G["MIXH"][0][ch * 1024:(ch + 1) * 1024, :]],
                        outs=[mixf4[ch].rearrange("r p c -> (r p) c")]),
                        reads=[G["MIXH"][1]], writes=[Tmixf])
                if l == 0:
                    stage_outproj(c, nc, G, l, x_in, Tx0, mixf, Tmixf, x1, Tx1, 0, NT, False)
                else:
                    stage_outproj(c, nc, G, l, x1, Tx1, mixf, Tmixf, out, T(), 0, NT, True)
        c.finish("sp")
        print(f"[build {mode}] instructions={c.n_ins} waits={c.n_wait} sems={len(c.sems)}")
    return nc


def _bucket_np(dist):
    n = np.maximum(dist, 0)
    max_exact = 16
    large = max_exact + (np.log(np.maximum(n, max_exact).astype(np.float32) / np.float32(max_exact))
                         / np.float32(math.log(128 / max_exact)) * np.float32(32 - max_exact)).astype(np.int32)
    large = np.minimum(large, 31)
    return np.where(n < max_exact, n, large)


def _consts():
    j = np.arange(128)[:, None]
    i = np.arange(128)[None, :]
    cm = np.zeros((6, 128, 128), np.float32)
    cm[0] = np.eye(128)
    cm[1] = (j <= i)
    cm[2] = 1.0
    cm[3] = -1.0
    cm[4] = np.where(i < j, NEG, 0.0)
    cm[5] = (i > j)
    k = np.arange(128)[:, None]
    q = np.arange(512)[None, :]
    dists = [q - k - (a * 128 - 128) for a in range(5)]
    mpat = np.stack([np.where(d < 0, NEG, 0.0) for d in dists]).astype(np.float32)
    bidx = np.stack([_bucket_np(d) for d in dists])
    return cm, mpat, bidx


def prep_core_inputs(inp, b, hh):
    cm, mpat, bidx = _consts()
    H = np.arange(4 * hh, 4 * hh + 4)
    cols = np.concatenate([
        hh * 512 + np.arange(512), 1024 + hh * 512 + np.arange(512),
        4096 + hh * 512 + np.arange(512), 5120 + hh * 512 + np.arange(512), 6144 + hh * 512 + np.arange(512),
        2048 + hh * 512 + np.arange(512), 3072 + hh * 512 + np.arange(512), 7168 + hh * 512 + np.arange(512),
        8192 + hh * 4 + np.arange(4), 8200 + hh * 4 + np.arange(4)])
    rows = np.concatenate([np.concatenate([r * 512 + np.arange(512), 1024 + r * 512 + np.arange(512)])
                           for r in range(2)])
    m = {}
    for l in range(DEPTH):
        wpk = inp["w_in"][l][:, cols].reshape(16, 128, NCOL).transpose(1, 0, 2)
        m[f"win{l}"] = np.ascontiguousarray(np.concatenate(
            [wpk[:, :, cb * 512:(cb + 1) * 512].reshape(128, 8192) for cb in range(8)]
            + [wpk[:, :, 4096:NCOL].reshape(128, 128)], axis=1))
        m[f"wout{l}"] = np.ascontiguousarray(
            inp["w_out"][l][rows, :].reshape(16, 128, D).transpose(1, 0, 2))
    m["gpk_d"] = np.ascontiguousarray(inp["norm_w"].reshape(DEPTH, 16, 128).transpose(0, 2, 1))
    ch = np.stack([t * 1024 + (4 * hh + hl) * 128 + np.arange(128) for t in range(3) for hl in range(4)])
    m["cw_d"] = np.ascontiguousarray(inp["conv_w"][:, :, ch].transpose(0, 3, 2, 1))
    hp = np.concatenate([inp["a_log"][:, H], inp["dt_bias"][:, H]], axis=1)
    m["hpar_d"] = np.ascontiguousarray(np.broadcast_to(hp[:, None, :], (DEPTH, 128, 8)))
    m["gnw_d"] = np.ascontiguousarray(inp["gdn_norm_w"])
    m["subw_d"] = np.ascontiguousarray(inp["da_subln_w"])
    m["lam_d"] = np.ascontiguousarray(np.stack([inp["lambda_q1"], inp["lambda_k1"],
                                                inp["lambda_q2"], inp["lambda_k2"]], axis=1))
    m["fnw_d"] = np.ascontiguousarray(inp["final_norm_w"].reshape(1, D))
    rb = inp["rel_bias"]
    m["braw_d"] = np.ascontiguousarray(np.stack([rb[:, h][bidx] for h in H]).astype(np.float32))
    m["mpat_d"] = mpat
    m["cfar_d"] = np.ascontiguousarray(np.broadcast_to(rb[31, H][None, :], (128, 4)))
    m["cmat_d"] = cm
    return {k: np.asarray(v, dtype=np.float32) if v.dtype != np.float32 else v for k, v in m.items()}


_PROGS = {}


def _prog(mode):
    if mode not in _PROGS:
        _PROGS[mode] = build_program(mode)
    return _PROGS[mode]


_COMMON = ["gpk_d", "cw_d", "hpar_d", "gnw_d", "subw_d", "lam_d", "fnw_d", "braw_d", "mpat_d", "cfar_d", "cmat_d"]
FUSED = True


def kernel(**inputs):
    inp = {k: np.asarray(v) for k, v in inputs.items()}
    x = np.ascontiguousarray(inp["x"], dtype=np.float32)
    cores = [(b, hh) for b in range(NB) for hh in range(2)]
    prep = [prep_core_inputs(inp, b, hh) for (b, hh) in cores]
    ids = list(range(8))
    out = np.empty((NB, SEQ, D), np.float32)
    if FUSED:
        maps = []
        for ci, (b, hh) in enumerate(cores):
            m = {k: prep[ci][k] for k in _COMMON}
            for l in range(DEPTH):
                m[f"win{l}"] = prep[ci][f"win{l}"]
                m[f"wout{l}"] = prep[ci][f"wout{l}"]
            m["x_in"] = x[b]
            maps.append(m)
        res = run_bass_kernel_spmd(_prog("fused"), maps, core_ids=ids).results
        for ci, (b, hh) in enumerate(cores):
            out[b, hh * 2048:(hh + 1) * 2048] = np.asarray(res[ci]["out"])[hh * 2048:(hh + 1) * 2048]
        return out
    maps = []
    for ci, (b, hh) in enumerate(cores):
        m = {k: prep[ci][k] for k in _COMMON}
        m["win0"] = prep[ci]["win0"]
        m["x_in"] = x[b]
        maps.append(m)
    r1 = run_bass_kernel_spmd(_prog("p1"), maps, core_ids=ids).results
    mixf0 = [np.stack([np.asarray(r1[2 * b]["mixh"]), np.asarray(r1[2 * b + 1]["mixh"])]) for b in range(NB)]
    maps = []
    for ci, (b, hh) in enumerate(cores):
        m = {k: prep[ci][k] for k in _COMMON}
        m["win1"] = prep[ci]["win1"]
        m["wout0"] = prep[ci]["wout0"]
        m["x_in"] = x[b]
        m["mixf_in"] = mixf0[b]
        maps.append(m)
    r2 = run_bass_kernel_spmd(_prog("p2"), maps, core_ids=ids).results
    mixf1 = [np.stack([np.asarray(r2[2 * b]["mixh"]), np.asarray(r2[2 * b + 1]["mixh"])]) for b in range(NB)]
    maps = []
    for ci, (b, hh) in enumerate(cores):
        m = {k: prep[ci][k] for k in _COMMON}
        m["wout1"] = prep[ci]["wout1"]
        m["x_in"] = np.ascontiguousarray(np.asarray(r2[ci]["x1"])[hh * 2048:(hh + 1) * 2048])
        m["mixf_in"] = np.ascontiguousarray(mixf1[b][:, hh * 2048:(hh + 1) * 2048, :])
        maps.append(m)
    r3 = run_bass_kernel_spmd(_prog("p3"), maps, core_ids=ids).results
    for ci, (b, hh) in enumerate(cores):
        out[b, hh * 2048:(hh + 1) * 2048] = np.asarray(r3[ci]["out"])
    return out
```

```python
import math
import contextlib
import numpy as np
import ml_dtypes
import concourse.bass as bass
import concourse.mybir as mybir
from concourse.bass_utils import run_bass_kernel_spmd

F32 = mybir.dt.float32
BF16 = mybir.dt.bfloat16
AF = mybir.ActivationFunctionType
ALU = mybir.AluOpType

D = 2048
SEQ = 4096
NB = 4
DEPTH = 2
NCOL = 4104
NT = SEQ // 128
RMS_EPS = 1e-6
SCALE = 0.125
NEG = -30000.0

SEM_EPOCH = 24000
N_LANES = 8


class T:
    __slots__ = ("name", "lw", "rd", "excl")

    def __init__(self, name="", excl=False):
        self.name = name
        self.lw = None
        self.rd = []
        self.excl = excl


def TP():
    return T(excl=True)


class Ctx:
    def __init__(self, nc, es):
        self.nc = nc
        self.es = es
        self.engs = {"pe": nc.tensor, "act": nc.scalar, "dve": nc.vector,
                     "pool": nc.gpsimd, "sp": nc.sync}
        self.sems = {}
        self.cur = {}
        self.epoch = {e: 0 for e in self.engs}
        self.seen = {e: {} for e in self.engs}
        for e in self.engs:
            self._new_epoch(e)
        self.lanes = {}
        self.lane_rr = {}
        self.n_wait = 0
        self.n_ins = 0
        self.uid = 0

    def _mksem(self, key):
        h = self.es.enter_context(self.nc.semaphore(key))
        self.sems[key] = h
        return h

    def _new_epoch(self, e):
        key = f"s_{e}_{self.epoch[e]}"
        self.epoch[e] += 1
        self._mksem(key)
        self.cur[e] = [key, 0]

    def _lanes(self, q):
        if q not in self.lanes:
            self.lanes[q] = []
            for i in range(N_LANES):
                key = f"l_{q}_{i}"
                self._mksem(key)
                self.lanes[q].append([key, 0])
            self.lane_rr[q] = 0
        return self.lanes[q]

    def _wait(self, e, ev):
        if ev is None:
            return
        key, val = ev
        if e == "pe" and key.startswith("s_pe_"):
            return
        if self.seen[e].get(key, 0) >= val:
            return
        self.engs[e].wait_ge(self.sems[key], val)
        self.seen[e][key] = val
        self.n_wait += 1

    def _deps(self, e, reads, writes):
        for t in reads:
            self._wait(e, t.lw)
            if t.excl:
                for ev in t.rd:
                    if ev[0].split("_")[1] != e:
                        self._wait(e, ev)
        for t in writes:
            self._wait(e, t.lw)
            for ev in t.rd:
                self._wait(e, ev)

    def _record(self, ev, reads, writes):
        for t in reads:
            t.rd = [r for r in t.rd if r[0] != ev[0]] + [ev]
        for t in writes:
            t.lw = ev
            t.rd = []

    def op(self, e, fn, reads=(), writes=()):
        self._deps(e, reads, writes)
        ins = fn(self.engs[e])
        c = self.cur[e]
        c[1] += 1
        ins.then_inc(self.sems[c[0]], 1)
        ev = (c[0], c[1])
        self._record(ev, reads, writes)
        self.n_ins += 1
        if c[1] >= SEM_EPOCH:
            self._new_epoch(e)
        return ev

    def dma(self, q, out, in_, reads=(), writes=(), **kw):
        lanes = self._lanes(q)
        i = self.lane_rr[q]
        self.lane_rr[q] = (i + 1) % N_LANES
        lane = lanes[i]
        if lane[1] > 0:
            self._wait(q, (lane[0], lane[1]))
        self._deps(q, reads, writes)
        ins = self.engs[q].dma_start(out=out, in_=in_, **kw)
        lane[1] += 16
        ins.then_inc(self.sems[lane[0]], 16)
        ev = (lane[0], lane[1])
        self._record(ev, reads, writes)
        self.n_ins += 1
        return ev

    def collective(self, fn, reads=(), writes=()):
        if "cc" not in self.sems:
            self._mksem("cc")
            self.cc_count = 0
        self._deps("pool", reads, writes)
        ins = fn(self.engs["pool"])
        self.cc_count += 1
        ins.then_inc(self.sems["cc"])
        ev = ("cc", self.cc_count)
        self._record(ev, reads, writes)
        self.n_ins += 1
        return ev

    def all_events(self):
        evs = []
        for e in self.engs:
            for ep in range(self.epoch[e]):
                key = f"s_{e}_{ep}"
                val = self.cur[e][1] if key == self.cur[e][0] else SEM_EPOCH
                if val > 0:
                    evs.append((key, val))
        for q, lanes in self.lanes.items():
            for lane in lanes:
                if lane[1] > 0:
                    evs.append((lane[0], lane[1]))
        if "cc" in self.sems and self.cc_count > 0:
            evs.append(("cc", self.cc_count))
        return evs

    def barrier(self):
        evs = self.all_events()
        for e in self.engs:
            for ev in evs:
                if ev[0] == self.cur[e][0]:
                    continue
                self._wait(e, ev)

    def finish(self, e="sp"):
        for ev in self.all_events():
            if ev[0] == self.cur[e][0]:
                continue
            self._wait(e, ev)


class Ring:
    def __init__(self, items):
        self.items = items
        self.i = 0

    def next(self):
        it = self.items[self.i]
        self.i = (self.i + 1) % len(self.items)
        return it


def _alt(i):
    return "act" if (i % 2) else "dve"


def evac_copy(c, eng, out, in_, reads, writes):
    if eng == "act":
        return c.op("act", lambda e: e.copy(out=out, in_=in_), reads=reads, writes=writes)
    return c.op(eng, lambda e: e.tensor_copy(out=out, in_=in_), reads=reads, writes=writes)


def evac_scale(c, eng, out, in_, sc, reads, writes):
    if eng == "act":
        return c.op("act", lambda e: e.mul(out=out, in_=in_, mul=sc), reads=reads, writes=writes)
    return c.op("dve", lambda e: e.tensor_scalar(out=out, in0=in_, scalar1=sc, scalar2=None, op0=ALU.mult),
                reads=reads, writes=writes)


def rsqrt_chain(c, out, in_, scale, epsb, tmp, reads, writes, tmpT):
    c.op("act", lambda e: e.activation(out=tmp, in_=in_, func=AF.Ln, bias=epsb, scale=scale),
         reads=reads, writes=[tmpT])
    c.op("act", lambda e: e.activation(out=out, in_=tmp, func=AF.Exp, scale=-0.5),
         reads=[tmpT], writes=writes)


def stage_inproj(c, nc, G, l, x_src, Tx_src):
    Wbf = G["Wbf"][l]
    TW = G["TWbf"][l]
    with contextlib.ExitStack() as es:
        sb = lambda n, s, d: es.enter_context(nc.sbuf_tensor(f"{n}_L{l}", s, d))
        ps = lambda n, s, d: es.enter_context(nc.psum_tensor(f"{n}_L{l}", s, d))
        xt = [(sb(f"ip_x{i}", [128, D], F32), T()) for i in range(3)]
        junk = sb("ip_junk", [128, D], BF16)
        Tjunk = T()
        st = sb("ip_st", [128, NT, 4], F32)
        Tst = [T() for _ in range(NT)]
        xn = [(sb(f"ip_xn{i}", [128, 4, D], BF16), T()) for i in range(2)]
        hT = [(sb(f"ip_hT{i}", [128, 16, 512], BF16), T()) for i in range(2)]
        wb = Ring([(sb(f"ip_w{i}", [128, 16, 512], BF16), T()) for i in range(3)])
        wl = [(sb(f"ip_wl{i}", [128, 16, 8], BF16), T()) for i in range(2)]
        sg_b = Ring([(sb(f"ip_sb{i}", [128, 512], BF16), T()) for i in range(4)])
        sg_f = Ring([(sb(f"ip_sf{i}", [128, 512], F32), T()) for i in range(4)])
        trp = Ring([(ps(f"ip_tr{i}", [128, 1024], BF16)[:, 0:512], TP()) for i in range(2)])
        mmp = Ring([(ps(f"ip_mm{i}", [128, 512], F32), TP()) for i in range(6)])
        gpk, Tgpk = G["gpk"][l]
        identb, Tidb = G["identb"]
        epsb = G["cst"][0][:, 0:1]
        Tcst = G["cst"][1]

        c.op("pool", lambda e: e.memset(st[:], 0.0), writes=Tst)

        def load_x(blk):
            for j in range(4):
                t = blk * 4 + j
                xa, Txa = xt[t % 3]
                c.dma("sp", xa[:], x_src[t * 128:(t + 1) * 128, :], reads=[Tx_src], writes=[Txa])

        def load_w(cb):
            if cb < 8:
                w, Tw = wb.next()
                c.dma("sp", w[:].rearrange("p k c -> p (k c)"), Wbf[:, cb * 8192:(cb + 1) * 8192],
                      reads=[TW[cb]], writes=[Tw])
            else:
                w, Tw = wl[load_w.n8 % 2]
                load_w.n8 += 1
                c.dma("sp", w[:].rearrange("p k c -> p (k c)"), Wbf[:, 65536:16 * NCOL], reads=[TW[8]], writes=[Tw])
            return w, Tw
        load_w.n8 = 0

        nev = 0

        def prologue(blk):
            xnb, Txn = xn[blk % 2]
            for j in range(4):
                t = blk * 4 + j
                xa, Txa = xt[t % 3]
                c.dma("sp", xa[:], x_src[t * 128:(t + 1) * 128, :], reads=[Tx_src], writes=[Txa])
                c.op("act", lambda e: e.activation(out=junk[:], in_=xa[:], func=AF.Square,
                                                   accum_out=st[:, t, 0:1]),
                     reads=[Txa], writes=[Tjunk, Tst[t]])
                rsqrt_chain(c, st[:, t, 2:3], st[:, t, 0:1], 1.0 / D, epsb, st[:, t, 1:2],
                            [Tst[t], Tcst], [Tst[t]], Tst[t])
                c.op("dve", lambda e: e.tensor_scalar(out=xnb[:, j, :], in0=xa[:], scalar1=st[:, t, 2:3],
                                                      scalar2=None, op0=ALU.mult),
                     reads=[Txa, Tst[t]], writes=[Txn])

        def transposes(blk):
            xnb, Txn = xn[blk % 2]
            hTb, ThT = hT[blk % 2]
            for k in range(16):
                p, Tp = trp.next()
                for j in range(4):
                    c.op("pe", lambda e: e.transpose(p[:, j * 128:(j + 1) * 128],
                                                     xnb[:, j, k * 128:(k + 1) * 128], identb[:]),
                         reads=[Txn, Tidb], writes=[Tp])
                evac_scale(c, _alt(k), hTb[:, k, :], p[:], gpk[:, k:k + 1], [Tp, Tgpk], [ThT])

        prologue(0)
        transposes(0)
        for blk in range(8):
            hTb, ThT = hT[blk % 2]
            if blk + 1 < 8:
                prologue(blk + 1)
            nxt = load_w(0)
            for cb in range(9):
                w, Tw = nxt
                if cb + 1 < 9:
                    nxt = load_w(cb + 1)
                if cb < 5:
                    for m in range(4):
                        p, Tp = mmp.next()
                        for k in range(16):
                            c.op("pe", lambda e: e.matmul(p[:], lhsT=w[:, k, m * 128:(m + 1) * 128],
                                                          rhs=hTb[:, k, :], start=(k == 0), stop=(k == 15)),
                                 reads=[Tw, ThT], writes=[Tp])
                        if cb < 2:
                            s, Ts = sg_b.next()
                            dst = G["QKT"][0][cb * 4 + m, :, blk * 512:(blk + 1) * 512]
                            Tdst = G["QKT"][1]
                        else:
                            s, Ts = sg_f.next()
                            dst = G["GQKV"][0][(cb - 2) * 4 + m, :, blk * 512:(blk + 1) * 512]
                            Tdst = G["GQKV"][1]
                        evac_copy(c, _alt(nev), s[:], p[:], [Tp], [Ts])
                        nev += 1
                        c.dma("pool", dst, s[:], reads=[Ts], writes=[Tdst])
                elif cb < 8:
                    for j in range(4):
                        t = blk * 4 + j
                        p, Tp = mmp.next()
                        for k in range(16):
                            c.op("pe", lambda e: e.matmul(p[:], lhsT=hTb[:, k, j * 128:(j + 1) * 128],
                                                          rhs=w[:, k, :], start=(k == 0), stop=(k == 15)),
                                 reads=[Tw, ThT], writes=[Tp])
                        if cb == 5:
                            s, Ts = sg_b.next()
                            dst, Tdst = G["DAV"][0][t * 128:(t + 1) * 128, :], G["DAV"][1]
                        elif cb == 6:
                            s, Ts = sg_f.next()
                            dst, Tdst = G["DAG"][0][t * 128:(t + 1) * 128, :], G["DAG"][1]
                        else:
                            s, Ts = sg_f.next()
                            dst, Tdst = G["GZ"][0][t * 128:(t + 1) * 128, :], G["GZ"][1]
                        evac_copy(c, _alt(nev), s[:], p[:], [Tp], [Ts])
                        nev += 1
                        c.dma("pool", dst, s[:], reads=[Ts], writes=[Tdst])
                else:
                    for j in range(4):
                        t = blk * 4 + j
                        p, Tp = mmp.next()
                        for k in range(16):
                            c.op("pe", lambda e: e.matmul(p[:, 0:8], lhsT=hTb[:, k, j * 128:(j + 1) * 128],
                                                          rhs=w[:, k, :], start=(k == 0), stop=(k == 15)),
                                 reads=[Tw, ThT], writes=[Tp])
                        s, Ts = sg_f.next()
                        evac_copy(c, _alt(nev), s[:, 0:8], p[:, 0:8], [Tp], [Ts])
                        nev += 1
                        c.dma("pool", G["GBA"][0][t * 128:(t + 1) * 128, :], s[:, 0:8], reads=[Ts],
                              writes=[G["GBA"][1]])
            if blk + 1 < 8:
                transposes(blk + 1)
        c.barrier()


def stage_outproj(c, nc, G, l, x_src, Tx_src, mixf, Tmixf, dst, Tdst, tok0, ntile, final):
    Wo = G["Wobf"][l]
    TWo = G["TWobf"][l]
    with contextlib.ExitStack() as es:
        sb = lambda n, s, d: es.enter_context(nc.sbuf_tensor(f"{n}_L{l}", s, d))
        ps = lambda n, s, d: es.enter_context(nc.psum_tensor(f"{n}_L{l}", s, d))
        wo = sb("op_w", [128, 16, D], BF16)
        Two = T()
        mt = [(sb(f"op_m{i}", [128, D], BF16), T()) for i in range(2)]
        mT = [(sb(f"op_mT{i}", [128, 16, 128], BF16), T()) for i in range(2)]
        xt = [(sb(f"op_x{i}", [128, D], F32), T()) for i in range(2)]
        ot = [(sb(f"op_o{i}", [128, D], F32), T()) for i in range(2)]
        junk = sb("op_junk", [128, D], BF16)
        Tjunk = T()
        st = sb("op_st", [128, NT, 4], F32)
        Tst = [T() for _ in range(NT)]
        fnw = sb("op_fnw", [128, D], F32)
        Tfnw = T()
        trp = Ring([(ps(f"op_tr{i}", [128, 1024], BF16)[:, 0:512], TP()) for i in range(2)])
        mmp = Ring([(ps(f"op_mm{i}", [128, 512], F32), TP()) for i in range(6)])
        identb, Tidb = G["identb"]
        epsb = G["cst"][0][:, 0:1]
        Tcst = G["cst"][1]
        for k4 in range(4):
            c.dma("sp", wo[:, k4 * 4:(k4 + 1) * 4, :], Wo[:, k4 * 4:(k4 + 1) * 4, :], reads=TWo[k4 * 4:(k4 + 1) * 4], writes=[Two])
        if final:
            c.dma("sp", fnw[:], G["fnw_d"][0:1, :].to_broadcast([128, D]), writes=[Tfnw])
            c.op("pool", lambda e: e.memset(st[:], 0.0), writes=Tst)
        nev = 0
        for i in range(ntile):
            tok = tok0 + i * 128
            m, Tm = mt[i % 2]
            mTt, TmT = mT[i % 2]
            xa, Txa = xt[i % 2]
            o, To = ot[i % 2]
            for r in range(2):
                src_ap = mixf(r, tok) if callable(mixf) else mixf[r, tok:tok + 128, :]
                c.dma("sp", m[:, r * 1024:(r + 1) * 1024], src_ap, reads=[Tmixf], writes=[Tm])
            c.dma("sp", xa[:], x_src[tok:tok + 128, :], reads=[Tx_src], writes=[Txa])
            for k4 in range(4):
                p, Tp = trp.next()
                for j in range(4):
                    k = k4 * 4 + j
                    c.op("pe", lambda e: e.transpose(p[:, j * 128:(j + 1) * 128], m[:, k * 128:(k + 1) * 128],
                                                     identb[:]),
                         reads=[Tm, Tidb], writes=[Tp])
                evac_copy(c, _alt(k4), mTt[:, k4 * 4:(k4 + 1) * 4, :],
                          p[:].rearrange("p (j t) -> p j t", j=4), [Tp], [TmT])
            for nb in range(4):
                p, Tp = mmp.next()
                for k in range(16):
                    c.op("pe", lambda e: e.matmul(p[:], lhsT=mTt[:, k, :], rhs=wo[:, k, nb * 512:(nb + 1) * 512],
                                                  start=(k == 0), stop=(k == 15)),
                         reads=[TmT, Two], writes=[Tp])
                c.op("dve", lambda e: e.tensor_tensor(out=o[:, nb * 512:(nb + 1) * 512], in0=p[:],
                                                      in1=xa[:, nb * 512:(nb + 1) * 512], op=ALU.add),
                     reads=[Tp, Txa], writes=[To])
            if not final:
                c.dma("pool", dst[tok:tok + 128, :], o[:], reads=[To], writes=[Tdst])
            else:
                c.op("act", lambda e: e.activation(out=junk[:], in_=o[:], func=AF.Square,
                                                   accum_out=st[:, i, 0:1]),
                     reads=[To], writes=[Tjunk, Tst[i]])
                rsqrt_chain(c, st[:, i, 2:3], st[:, i, 0:1], 1.0 / D, epsb, st[:, i, 1:2],
                            [Tst[i], Tcst], [Tst[i]], Tst[i])
                c.op("dve", lambda e: e.scalar_tensor_tensor(out=xa[:], in0=o[:], scalar=st[:, i, 2:3],
                                                             in1=fnw[:], op0=ALU.mult, op1=ALU.mult),
                     reads=[To, Tst[i], Tfnw], writes=[Txa])
                c.dma("pool", dst[i * 128:(i + 1) * 128, :], xa[:], reads=[Txa], writes=[Tdst])
        c.barrier()


def stage_da(c, nc, G, l):
    lam_init = 0.8 - 0.6 * math.exp(-0.3 * l)
    QKT, TQKT = G["QKT"]
    DAV, TDAV = G["DAV"]
    DAG, TDAG = G["DAG"]
    MIXH, TMIXH = G["MIXH"]
    DAVr = DAV.rearrange("(kb p) c -> p kb c", p=128)
    DAGr = DAG.rearrange("(j p) c -> p j c", p=128)
    MIXr = MIXH.rearrange("(j p) c -> p j c", p=128)
    with contextlib.ExitStack() as es:
        sb = lambda n, s, d: es.enter_context(nc.sbuf_tensor(f"{n}_L{l}", s, d))
        ps = lambda n, s, d: es.enter_context(nc.psum_tensor(f"{n}_L{l}", s, d))
        KT = [(sb(f"da_kt{i}", [128, SEQ], BF16), T()) for i in range(2)]
        QT = [(sb(f"da_qt{i}", [128, SEQ], BF16), T()) for i in range(2)]
        V = [(sb(f"da_v{i}", [128, NT, 129], BF16), T()) for i in range(2)]
        braw = sb("da_braw", [128, 5, 512], F32)
        Tbraw = T()
        mpat = sb("da_mpat", [128, 5, 512], F32)
        Tmpat = T()
        biasT = [(sb(f"da_bias{i}", [128, 5, 512], BF16), T()) for i in range(2)]
        ering = Ring([(sb(f"da_e{i}", [128, 512], BF16), T()) for i in range(4)])
        om = [(sb(f"da_om{i}", [128, 4, 128], F32), T()) for i in range(2)]
        rec = [(sb(f"da_rec{i}", [128, 4], F32), T()) for i in range(2)]
        dlt = sb("da_dlt", [128, 4, 128], F32)
        Tdlt = T()
        junk = sb("da_junk", [128, 128], BF16)
        Tjunk = T()
        ss = sb("da_ss", [128, 8, 4], F32)
        Tss = T()
        gate = [(sb(f"da_g{i}", [128, 4, 128], F32), T()) for i in range(2)]
        gm = sb("da_gm", [128, 4, 128], F32)
        Tgm = T()
        tmp = sb("da_tmp", [128, 4, 128], F32)
        Ttmp = T()
        fin = [(sb(f"da_fin{i}", [128, 4, 128], BF16), T()) for i in range(2)]
        wsub = sb("da_wsub", [128, 4, 128], F32)
        Twsub = T()
        lamv = sb("da_lamv", [128, 4, 64], F32)
        Tlamv = T()
        lamp = sb("da_lamp", [128, 2, 64], F32)
        lams = sb("da_lams", [128, 8], F32)
        Tlams = T()
        sring = Ring([(ps(f"da_s{i}", [128, 512], F32), TP()) for i in range(3)])
        Oacc = [[(ps(f"da_o{m}{b}", [128, 512], F32)[:, 0:258].rearrange("p (a b) -> p a b", a=2), TP())
                 for b in range(2)] for m in range(2)]
        identb, Tidb = G["identb"]
        cst, Tcst = G["cst"]
        epsb = cst[:, 0:1]
        zerob = cst[:, 1:2]
        cfar, Tcfar = G["cfar"]

        c.dma("sp", lamv[:], G["lam_d"][l:l + 1, :, :].to_broadcast([128, 4, 64]), writes=[Tlamv])
        c.op("pool", lambda e: e.memset(lams[:], 0.0), writes=[Tlams])
        c.op("dve", lambda e: e.tensor_tensor(out=lamp[:, 0, :], in0=lamv[:, 0, :], in1=lamv[:, 1, :], op=ALU.mult),
             reads=[Tlamv], writes=[Tlamv])
        c.op("dve", lambda e: e.tensor_tensor(out=lamp[:, 1, :], in0=lamv[:, 2, :], in1=lamv[:, 3, :], op=ALU.mult),
             reads=[Tlamv], writes=[Tlamv])
        for i in range(2):
            c.op("act", lambda e: e.activation(out=lamv[:, i, :], in_=lamp[:, i, :], func=AF.Identity,
                                               accum_out=lams[:, i:i + 1]),
                 reads=[Tlamv], writes=[Tlamv, Tlams])
        c.op("act", lambda e: e.activation(out=lams[:, 2:4], in_=lams[:, 0:2], func=AF.Exp),
             reads=[Tlams], writes=[Tlams])
        c.op("dve", lambda e: e.tensor_tensor(out=lams[:, 4:5], in0=lams[:, 3:4], in1=lams[:, 2:3], op=ALU.subtract),
             reads=[Tlams], writes=[Tlams])
        c.op("dve", lambda e: e.tensor_scalar(out=lams[:, 5:6], in0=lams[:, 4:5], scalar1=-lam_init, scalar2=None,
                                              op0=ALU.add),
             reads=[Tlams], writes=[Tlams])
        neglam = lams[:, 5:6]
        c.dma("sp", wsub[:], G["subw_d"][l:l + 1, :].unsqueeze(1).to_broadcast([128, 4, 128]), writes=[Twsub])
        c.op("dve", lambda e: e.tensor_scalar(out=wsub[:], in0=wsub[:], scalar1=1.0 - lam_init, scalar2=None,
                                              op0=ALU.mult),
             reads=[Twsub], writes=[Twsub])
        c.dma("sp", mpat[:], G["mpat_d"].rearrange("a p q -> p a q"), writes=[Tmpat])
        for i in range(2):
            c.op("pool", lambda e: e.memset(V[i][0][:, :, 128:129], 1.0), writes=[V[i][1]])

        def load_head(hl):
            kt, Tkt = KT[hl % 2]
            qt, Tqt = QT[hl % 2]
            v, Tv = V[hl % 2]
            bT, TbT = biasT[hl % 2]
            c.dma("sp", kt[:], QKT[4 + hl, :, :], reads=[TQKT], writes=[Tkt])
            c.dma("sp", qt[:], QKT[hl, :, :], reads=[TQKT], writes=[Tqt])
            for a in range(4):
                c.dma("sp", v[:, a * 8:(a + 1) * 8, 0:128], DAVr[:, a * 8:(a + 1) * 8, hl * 128:(hl + 1) * 128],
                      reads=[TDAV], writes=[Tv])
            c.dma("sp", braw[:], G["braw_d"][hl].rearrange("a p q -> p a q"), writes=[Tbraw])
            c.op("dve", lambda e: e.scalar_tensor_tensor(out=bT[:], in0=braw[:], scalar=1.0 / SCALE, in1=mpat[:],
                                                         op0=ALU.mult, op1=ALU.add),
                 reads=[Tbraw, Tmpat], writes=[TbT])

        load_head(0)
        it = 0

        def emit_qk(item):
            hl, qb, m, kb = item
            kt, Tkt = KT[hl % 2]
            qt, Tqt = QT[hl % 2]
            bT, TbT = biasT[hl % 2]
            r0 = 64 * m
            sp_, Tsp = sring.next()
            delta = kb * 128 - qb * 512
            special = delta >= -128
            j0 = max(0, kb - 4 * qb)
            c.op("pe", lambda e: e.matmul(sp_[:, j0 * 128:512], lhsT=kt[r0:r0 + 64, kb * 128:(kb + 1) * 128],
                                          rhs=qt[r0:r0 + 64, qb * 512 + j0 * 128:(qb + 1) * 512],
                                          start=True, stop=not special),
                 reads=[Tkt, Tqt], writes=[Tsp])
            if special:
                pat = (delta + 128) // 128
                c.op("pe", lambda e: e.matmul(sp_[:, j0 * 128:512], lhsT=identb[:], rhs=bT[:, pat, j0 * 128:512],
                                              start=False, stop=True),
                     reads=[Tidb, TbT], writes=[Tsp])
            return (sp_, Tsp, special, j0)

        def emit_rest(item, qkres):
            hl, qb, m, kb = item
            v, Tv = V[hl % 2]
            sp_, Tsp, special, j0 = qkres
            E, TE = ering.next()
            bias_ap = zerob if special else cfar[:, hl:hl + 1]
            c.op("act", lambda e: e.activation(out=E[:, j0 * 128:512], in_=sp_[:, j0 * 128:512], func=AF.Exp,
                                               bias=bias_ap, scale=SCALE),
                 reads=[Tsp, Tcst, Tcfar], writes=[TE])
            for j in range(j0, 4):
                acc, Tacc = Oacc[m][j // 2]
                c.op("pe", lambda e: e.matmul(acc[:, j % 2, :], lhsT=E[:, j * 128:(j + 1) * 128],
                                              rhs=v[:, kb, :], start=(kb == 0 and j % 2 == 0),
                                              stop=(kb == qb * 4 + j)),
                     reads=[TE, Tv], writes=[Tacc])

        def finish_map(m):
            o_m, Tom = om[m]
            rc, Trc = rec[m]
            for b in range(2):
                acc, Tacc = Oacc[m][b]
                c.op("dve", lambda e: e.reciprocal(out=rc[:, 2 * b:2 * b + 2], in_=acc[:, :, 128]),
                     reads=[Tacc], writes=[Trc])
                c.op("dve", lambda e: e.tensor_tensor(
                    out=o_m[:, 2 * b:2 * b + 2, :], in0=acc[:, :, 0:128],
                    in1=rc[:, 2 * b:2 * b + 2].unsqueeze(2).to_broadcast([128, 2, 128]), op=ALU.mult),
                     reads=[Tacc, Trc], writes=[Tom])

        def finish_qb(hl, qb, g, Tg, f, Tf):
            c.op("dve", lambda e: e.scalar_tensor_tensor(out=dlt[:], in0=om[1][0][:], scalar=neglam,
                                                         in1=om[0][0][:], op0=ALU.mult, op1=ALU.add),
                 reads=[om[0][1], om[1][1], Tlams], writes=[Tdlt])
            c.op("pool", lambda e: e.memset(ss[:, 0:4, 0], 0.0), writes=[Tss])
            for j in range(4):
                c.op("act", lambda e: e.activation(out=junk[:], in_=dlt[:, j, :], func=AF.Square,
                                                   accum_out=ss[:, j, 0:1]),
                     reads=[Tdlt], writes=[Tjunk, Tss])
            rsqrt_chain(c, ss[:, 4:8, 0], ss[:, 0:4, 0], 1.0 / 128, epsb, ss[:, 0:4, 1],
                        [Tss, Tcst], [Tss], Tss)
            c.op("act", lambda e: e.activation(out=gm[:], in_=g[:], func=AF.Silu), reads=[Tg], writes=[Tgm])
            c.op("pool", lambda e: e.tensor_tensor(out=gm[:], in0=gm[:], in1=wsub[:], op=ALU.mult),
                 reads=[Tgm, Twsub], writes=[Tgm])
            c.op("dve", lambda e: e.tensor_tensor(out=tmp[:], in0=dlt[:],
                                                  in1=ss[:, 4:8, 0:1].to_broadcast([128, 4, 128]), op=ALU.mult),
                 reads=[Tdlt, Tss], writes=[Ttmp])
            c.op("dve", lambda e: e.tensor_tensor(out=f[:], in0=tmp[:], in1=gm[:], op=ALU.mult),
                 reads=[Ttmp, Tgm], writes=[Tf])
            c.dma("pool", MIXr[:, qb * 4:(qb + 1) * 4, hl * 128:(hl + 1) * 128], f[:], reads=[Tf], writes=[TMIXH])

        items = [(hl, qb, m, kb) for hl in range(4) for qb in range(8) for m in range(2)
                 for kb in range(4 * qb + 4)]
        qkres = emit_qk(items[0])
        cur_g = None
        deferred = []
        for idx, item in enumerate(items):
            hl, qb, m, kb = item
            for dfr in deferred:
                dfr[0] -= 1
            while deferred and deferred[0][0] <= 0:
                deferred.pop(0)[1]()
            if kb == 0 and m == 0:
                if qb == 0 and hl + 1 < 4:
                    load_head(hl + 1)
                cur_g = (gate[it % 2], fin[it % 2])
                it += 1
                if G["pending"]:
                    G["pending"].pop(0)()
                c.dma("sp", cur_g[0][0][:], DAGr[:, qb * 4:(qb + 1) * 4, hl * 128:(hl + 1) * 128],
                      reads=[TDAG], writes=[cur_g[0][1]])
            nxt = emit_qk(items[idx + 1]) if idx + 1 < len(items) else None
            emit_rest(item, qkres)
            qkres = nxt
            if kb == 4 * qb + 3:
                finish_map(m)
                if m == 1:
                    deferred.append([3, (lambda a=(hl, qb, cur_g[0][0], cur_g[0][1], cur_g[1][0], cur_g[1][1]):
                                         finish_qb(*a))])
        while deferred:
            deferred.pop(0)[1]()
        while G["pending"]:
            G["pending"].pop(0)()
        c.barrier()


class _Stop(Exception):
    pass


def _chk(level):
    import os
    if int(os.environ.get("GDN_STOP", "99")) == level:
        _chk.c.barrier()
        raise _Stop()


def stage_gdn(c, nc, G, l):
    with contextlib.ExitStack() as es:
        try:
            _stage_gdn(c, nc, G, l, es)
        except _Stop:
            pass


def _stage_gdn(c, nc, G, l, es):
    _chk.c = c
    GQKV, TGQKV = G["GQKV"]
    GZ, TGZ = G["GZ"]
    GBA, TGBA = G["GBA"]
    MIXH, TMIXH = G["MIXH"]
    if True:
        sb = lambda n, s, d: es.enter_context(nc.sbuf_tensor(f"{n}_L{l}", s, d))
        ps = lambda n, s, d: es.enter_context(nc.psum_tensor(f"{n}_L{l}", s, d))
        cst, Tcst = G["cst"]
        epsb = cst[:, 0:1]
        oneb = cst[:, 2:3]
        identb, Tidb = G["identb"]
        identf, Tidf = G["identf"]
        Uf, TUf = G["Uf"]
        onesf, Tonesf = G["onesf"]
        negonesf, Tnegonesf = G["negonesf"]
        onesb, Tonesb = G["onesb"]
        negmask, Tnegmask = G["negmask"]
        strict, Tstrict = G["strict"]
        cw, Tcw = G["cw"][l]
        hpar, Thpar = G["hpar"][l]

        pring = Ring([(ps(f"gd_p{i}", [128, 4, 128], F32), TP()) for i in range(4)])
        sbanks = [ps(f"gd_scan{i}", [128, 4, 128], F32) for i in range(3)]
        Tsb = [TP() for _ in range(3)]
        ws_ps = [(sbanks[0][:, h, :], Tsb[0]) for h in range(4)]
        O_ps = [(sbanks[1][:, h, :], Tsb[1]) for h in range(4)]
        Sd_ps = [(sbanks[2][:, h, :], Tsb[2]) for h in range(4)]
        trbank = ps("gd_tr", [128, 8, 128], BF16)
        Ttr = TP()

        ba = sb("gd_ba", [128, NT, 8], F32)
        Tba = T()
        sc = {}
        for nm in ("beta", "negb", "xa", "nx", "mn", "ex", "lg", "mx", "g", "gc", "eg", "egl", "ekd", "dd"):
            sc[nm] = sb("gd_sc_" + nm, [128, NT, 4], F32)
        Tsc = T()
        nea = sb("gd_nea", [128, 4], F32)
        GBAr = GBA.rearrange("(n p) j -> p n j", p=128)
        for a in range(8):
            c.dma("sp", ba[:, a * 4:(a + 1) * 4, :], GBAr[:, a * 4:(a + 1) * 4, :], reads=[TGBA], writes=[Tba])
        c.op("act", lambda e: e.activation(out=sc["beta"][:], in_=ba[:, :, 0:4], func=AF.Sigmoid),
             reads=[Tba], writes=[Tsc])
        c.op("dve", lambda e: e.tensor_scalar(out=sc["negb"][:], in0=sc["beta"][:], scalar1=-1.0, scalar2=None,
                                              op0=ALU.mult), reads=[Tsc], writes=[Tsc])
        c.op("dve", lambda e: e.tensor_tensor(out=sc["xa"][:], in0=ba[:, :, 4:8],
                                              in1=hpar[:, 4:8].unsqueeze(1).to_broadcast([128, NT, 4]), op=ALU.add),
             reads=[Tba, Thpar], writes=[Tsc])
        c.op("dve", lambda e: e.tensor_scalar(out=sc["nx"][:], in0=sc["xa"][:], scalar1=-1.0, scalar2=None,
                                              op0=ALU.mult), reads=[Tsc], writes=[Tsc])
        c.op("dve", lambda e: e.tensor_tensor(out=sc["mn"][:], in0=sc["xa"][:], in1=sc["nx"][:], op=ALU.min),
             reads=[Tsc], writes=[Tsc])
        c.op("act", lambda e: e.activation(out=sc["ex"][:], in_=sc["mn"][:], func=AF.Exp), reads=[Tsc], writes=[Tsc])
        c.op("act", lambda e: e.activation(out=sc["lg"][:], in_=sc["ex"][:], func=AF.Ln, bias=oneb),
             reads=[Tsc, Tcst], writes=[Tsc])
        c.op("dve", lambda e: e.tensor_scalar(out=sc["mx"][:], in0=sc["xa"][:], scalar1=0.0, scalar2=None,
                                              op0=ALU.max), reads=[Tsc], writes=[Tsc])
        c.op("dve", lambda e: e.tensor_tensor(out=sc["lg"][:], in0=sc["lg"][:], in1=sc["mx"][:], op=ALU.add),
             reads=[Tsc], writes=[Tsc])
        c.op("act", lambda e: e.activation(out=nea[:], in_=hpar[:, 0:4], func=AF.Exp), reads=[Thpar], writes=[Tsc])
        c.op("dve", lambda e: e.tensor_scalar(out=nea[:], in0=nea[:], scalar1=-1.0, scalar2=None, op0=ALU.mult),
             reads=[Tsc], writes=[Tsc])
        c.op("dve", lambda e: e.tensor_tensor(out=sc["g"][:], in0=sc["lg"][:],
                                              in1=nea[:].unsqueeze(1).to_broadcast([128, NT, 4]), op=ALU.mult),
             reads=[Tsc], writes=[Tsc])
        gflat = sc["g"][:].rearrange("p n h -> p (n h)")
        bkA, TpA = pring.next()
        bkB, TpB = pring.next()
        pA = bkA[:].rearrange("p a b -> p (a b)")[:, 0:128]
        pB = bkB[:].rearrange("p a b -> p (a b)")[:, 0:128]
        c.op("pe", lambda e: e.matmul(pA, lhsT=Uf[:], rhs=gflat, start=True, stop=True),
             reads=[TUf, Tsc], writes=[TpA])
        c.op("pe", lambda e: e.matmul(pB, lhsT=onesf[:], rhs=gflat, start=True, stop=True),
             reads=[Tonesf, Tsc], writes=[TpB])
        fl = lambda nm: sc[nm][:].rearrange("p n h -> p (n h)")
        c.op("dve", lambda e: e.tensor_copy(out=fl("gc"), in_=pA), reads=[TpA], writes=[Tsc])
        c.op("act", lambda e: e.activation(out=fl("eg"), in_=pA, func=AF.Exp), reads=[TpA], writes=[Tsc])
        c.op("act", lambda e: e.activation(out=fl("egl"), in_=pB, func=AF.Exp), reads=[TpB], writes=[Tsc])
        c.op("dve", lambda e: e.tensor_tensor(out=fl("dd"), in0=pB, in1=fl("gc"), op=ALU.subtract),
             reads=[TpB, Tsc], writes=[Tsc])
        c.op("act", lambda e: e.activation(out=fl("ekd"), in_=fl("dd"), func=AF.Exp), reads=[Tsc], writes=[Tsc])

        _chk(1)
        Xr = Ring([(sb(f"gd_X{i}", [128, 515], F32), T()) for i in range(3)])
        yr = Ring([(sb(f"gd_y{i}", [128, 512], F32), T()) for i in range(2)])
        sr = Ring([(sb(f"gd_s{i}", [128, 512], F32), T()) for i in range(8)])
        sqr = Ring([(sb(f"gd_sq{i}", [128, 512], BF16), T()) for i in range(2)])
        rnr = Ring([(sb(f"gd_rn{i}", [128, 512], F32), T()) for i in range(2)])
        lnr = Ring([(sb(f"gd_ln{i}", [128, 512], F32), T()) for i in range(2)])
        qkvT = [[(sb(f"gd_qkv{p}_{t}", [128, 4, 512], BF16), T()) for t in range(3)] for p in range(2)]
        zt = [(sb(f"gd_z{i}", [128, 512], F32), T()) for i in range(2)]
        gzp = [(sb(f"gd_gz{i}", [128, 512], F32), T()) for i in range(2)]
        mixs = [(sb(f"gd_mix{i}", [128, 512], BF16), T()) for i in range(2)]
        gnw4 = sb("gd_gnw4", [128, 4, 128], F32)
        Tgnw4 = T()
        c.dma("sp", gnw4[:], G["gnw_d"][l:l + 1, :].unsqueeze(1).to_broadcast([128, 4, 128]), writes=[Tgnw4])
        S32 = [(sb(f"gd_S32_{h}", [128, 128], F32), T()) for h in range(4)]
        Sb = [(sb(f"gd_Sb_{h}", [128, 128], BF16), T()) for h in range(4)]
        for h in range(4):
            c.op("pool", lambda e: e.memset(S32[h][0][:], 0.0), writes=[S32[h][1]])
            c.op("pool", lambda e: e.memset(Sb[h][0][:], 0.0), writes=[Sb[h][1]])
        ost = sb("gd_ost", [128, NT, 4, 4], F32)
        Tost = [[T() for _ in range(4)] for _ in range(NT)]
        c.op("pool", lambda e: e.memset(ost[:], 0.0), writes=[t for row in Tost for t in row])
        junk = sb("gd_junk", [128, 128], BF16)
        Tjunk = T()

        def mk(nm, dt):
            return [[(sb(f"gd_{nm}_{p}_{h}", [128, 128], dt), T()) for h in range(4)] for p in range(2)]
        B_kg, B_kd, B_vt = mk("kg", BF16), mk("kd", BF16), mk("vt", BF16)
        B_Gm, B_egbc, B_dTi, B_dTs = mk("Gm", F32), mk("egbc", F32), mk("dTi", F32), mk("dTs", F32)
        B_N, B_NT, B_X = mk("N", BF16), mk("NT", BF16), mk("X", BF16)
        B_P = [mk("P0", BF16), mk("P1", BF16)]
        B_PT = [mk("PT0", BF16), mk("PT1", BF16)]
        B_qk, B_qg, B_u, B_w, B_vn = mk("qk", BF16), mk("qg", BF16), mk("u", F32), mk("w", BF16), mk("vn", BF16)

        for blk in range(8):
            par = blk % 2
            qT, kT, vT = qkvT[par]
            srows = []
            for r in range(12):
                t, hl = r // 4, r % 4
                X, TX = Xr.next()
                if blk == 0:
                    c.op("pool", lambda e: e.memset(X[:, 0:3], 0.0), writes=[TX])
                    c.dma("sp", X[:, 3:515], GQKV[r, :, 0:512], reads=[TGQKV], writes=[TX])
                else:
                    c.dma("sp", X[:], GQKV[r, :, blk * 512 - 3:blk * 512 + 512], reads=[TGQKV], writes=[TX])
                y, Ty = yr.next()
                c.op("dve", lambda e: e.tensor_scalar(out=y[:], in0=X[:, 0:512], scalar1=cw[:, r, 0:1], scalar2=None,
                                                      op0=ALU.mult), reads=[TX, Tcw], writes=[Ty])
                for j in range(1, 4):
                    c.op("dve", lambda e: e.scalar_tensor_tensor(out=y[:], in0=X[:, j:j + 512],
                                                                 scalar=cw[:, r, j:j + 1], in1=y[:],
                                                                 op0=ALU.mult, op1=ALU.add),
                         reads=[TX, Tcw, Ty], writes=[Ty])
                if t == 2:
                    c.op("act", lambda e: e.activation(out=vT[0][:, hl, :], in_=y[:], func=AF.Silu),
                         reads=[Ty], writes=[vT[1]])
                    continue
                s, Ts = sr.next()
                c.op("act", lambda e: e.activation(out=s[:], in_=y[:], func=AF.Silu), reads=[Ty], writes=[Ts])
                srows.append((r, s, Ts))
            for (r, s, Ts) in srows:
                t, hl = r // 4, r % 4
                sq, Tsq = sqr.next()
                c.op("pool", lambda e: e.tensor_tensor(out=sq[:], in0=s[:], in1=s[:], op=ALU.mult),
                     reads=[Ts], writes=[Tsq])
                bkss, Tssq = pring.next()
                ssq_ps = bkss[:].rearrange("p a b -> p (a b)")
                c.op("pe", lambda e: e.matmul(ssq_ps, lhsT=onesb[:], rhs=sq[:], start=True, stop=True),
                     reads=[Tonesb, Tsq], writes=[Tssq])
                rn, Trn = rnr.next()
                ln_, Tln = lnr.next()
                rsqrt_chain(c, rn[:], ssq_ps, 1.0, epsb, ln_[:], [Tssq, Tcst], [Trn], Tln)
                dstT = qT if t == 0 else kT
                scl = (128.0 ** -0.5) if t == 0 else 1.0
                c.op("dve", lambda e: e.scalar_tensor_tensor(out=dstT[0][:, hl, :], in0=s[:], scalar=scl, in1=rn[:],
                                                             op0=ALU.mult, op1=ALU.mult),
                     reads=[Ts, Trn], writes=[dstT[1]])

            _chk(2)
            for pair in range(2):
                CH = []
                for co in range(2):
                    cc = pair * 2 + co
                    n = blk * 4 + cc
                    CH.append((n, n % 2, slice(cc * 128, (cc + 1) * 128)))
                col = lambda nm, n, h: sc[nm][:, n, h:h + 1]
                for (n, cp, cs) in CH:
                    z, Tz = zt[cp]
                    gz, Tgz = gzp[cp]
                    c.dma("sp", z[:], GZ[n * 128:(n + 1) * 128, :], reads=[TGZ], writes=[Tz])
                    c.op("act", lambda e: e.activation(out=gz[:], in_=z[:], func=AF.Silu), reads=[Tz], writes=[Tgz])
                    c.op("pool", lambda e: e.tensor_tensor(out=gz[:], in0=gz[:],
                                                           in1=gnw4[:].rearrange("p a b -> p (a b)"), op=ALU.mult),
                         reads=[Tgz, Tgnw4], writes=[Tgz])
                _chk(25)
                for (n, cp, cs) in CH:
                    for h in range(4):
                        c.op("pe", lambda e: e.transpose(trbank[:, 2 * h, :], kT[0][:, h, cs], identb[:]),
                             reads=[kT[1], Tidb], writes=[Ttr])
                        c.op("pe", lambda e: e.transpose(trbank[:, 2 * h + 1, :], vT[0][:, h, cs], identb[:]),
                             reads=[vT[1], Tidb], writes=[Ttr])
                    for h in range(4):
                        kg, Tkg = B_kg[cp][h]
                        kd, Tkd = B_kd[cp][h]
                        vt, Tvt = B_vt[cp][h]
                        p1, p2 = trbank[:, 2 * h, :], trbank[:, 2 * h + 1, :]
                        evac_scale(c, "act", kg[:], p1, col("eg", n, h), [Ttr, Tsc], [Tkg])
                        evac_scale(c, "act", kd[:], p1, col("ekd", n, h), [Ttr, Tsc], [Tkd])
                        evac_copy(c, "act", vt[:], p2, [Ttr], [Tvt])
                _chk(3)
                bks = {}
                for (n, cp, cs) in CH:
                    bks[n] = (pring.next(), pring.next())
                    (bka, Tpa), (bkb, Tpb) = bks[n]
                    for h in range(4):
                        Gm, TGm = B_Gm[cp][h]
                        c.op("dve", lambda e: e.tensor_scalar(out=Gm[:], in0=Uf[:], scalar1=col("g", n, h),
                                                              scalar2=None, op0=ALU.mult),
                             reads=[TUf, Tsc], writes=[TGm])
                        pa = bka[:, h, :]
                        pb = bkb[:, h, :]
                        c.op("pe", lambda e: e.matmul(pa, lhsT=onesf[:], rhs=Gm[:], start=True, stop=True),
                             reads=[Tonesf, TGm], writes=[Tpa])
                        c.op("pe", lambda e: e.matmul(pb, lhsT=onesf[:], rhs=Gm[:], start=True, stop=False),
                             reads=[Tonesf, TGm], writes=[Tpb])
                        c.op("pe", lambda e: e.matmul(pb, lhsT=Gm[:], rhs=negonesf[:], start=False, stop=False),
                             reads=[Tnegonesf, TGm], writes=[Tpb])
                        c.op("pe", lambda e: e.matmul(pb, lhsT=identf[:], rhs=negmask[:], start=False, stop=True),
                             reads=[Tidf, Tnegmask], writes=[Tpb])
                for (n, cp, cs) in CH:
                    (bka, Tpa), (bkb, Tpb) = bks[n]
                    for h in range(4):
                        egbc, Tegbc = B_egbc[cp][h]
                        dTi, TdTi = B_dTi[cp][h]
                        dTs, TdTs = B_dTs[cp][h]
                        c.op("act", lambda e: e.activation(out=egbc[:], in_=bka[:, h, :], func=AF.Exp),
                             reads=[Tpa], writes=[Tegbc])
                        c.op("act", lambda e: e.activation(out=dTi[:], in_=bkb[:, h, :], func=AF.Exp),
                             reads=[Tpb], writes=[TdTi])
                        c.op("pool", lambda e: e.tensor_tensor(out=dTs[:], in0=dTi[:], in1=strict[:], op=ALU.mult),
                             reads=[TdTi, Tstrict], writes=[TdTs])
                _chk(4)
                for (n, cp, cs) in CH:
                    bks[n] = (pring.next(), pring.next())
                    (bkk, Tpk), (bkq, Tpq) = bks[n]
                    for h in range(4):
                        c.op("pe", lambda e: e.matmul(bkk[:, h, :], lhsT=kT[0][:, h, cs], rhs=kT[0][:, h, cs],
                                                      start=True, stop=True),
                             reads=[kT[1]], writes=[Tpk])
                        c.op("pe", lambda e: e.matmul(bkq[:, h, :], lhsT=kT[0][:, h, cs], rhs=qT[0][:, h, cs],
                                                      start=True, stop=True),
                             reads=[kT[1], qT[1]], writes=[Tpq])
                for (n, cp, cs) in CH:
                    (bkk, Tpk), (bkq, Tpq) = bks[n]
                    for h in range(4):
                        N_, TN = B_N[cp][h]
                        qk, Tqk = B_qk[cp][h]
                        qg, Tqg = B_qg[cp][h]
                        dTi, TdTi = B_dTi[cp][h]
                        dTs, TdTs = B_dTs[cp][h]
                        egbc, Tegbc = B_egbc[cp][h]
                        c.op("dve", lambda e: e.scalar_tensor_tensor(out=N_[:], in0=bkk[:, h, :],
                                                                     scalar=col("beta", n, h), in1=dTs[:],
                                                                     op0=ALU.mult, op1=ALU.mult),
                             reads=[Tpk, Tsc, TdTs], writes=[TN])
                        c.op("dve", lambda e: e.tensor_tensor(out=qk[:], in0=bkq[:, h, :], in1=dTi[:], op=ALU.mult),
                             reads=[Tpq, TdTi], writes=[Tqk])
                        c.op("pool", lambda e: e.tensor_tensor(out=qg[:], in0=qT[0][:, h, cs], in1=egbc[:],
                                                               op=ALU.mult),
                             reads=[qT[1], Tegbc], writes=[Tqg])
                _chk(5)
                for (n, cp, cs) in CH:
                    bkt, Tpt = pring.next()
                    bks[n] = (bkt, Tpt)
                    for h in range(4):
                        N_, TN = B_N[cp][h]
                        X_, TX_ = B_X[cp][h]
                        c.op("pool", lambda e: e.tensor_tensor(out=X_[:], in0=identb[:], in1=N_[:], op=ALU.subtract),
                             reads=[Tidb, TN], writes=[TX_])
                        c.op("pe", lambda e: e.matmul(bkt[:, h, :], lhsT=N_[:], rhs=identb[:], start=True, stop=True),
                             reads=[TN, Tidb], writes=[Tpt])
                for (n, cp, cs) in CH:
                    bkt, Tpt = bks[n]
                    for h in range(4):
                        NT_, TNT = B_NT[cp][h]
                        evac_copy(c, "act", NT_[:], bkt[:, h, :], [Tpt], [TNT])
                for k in range(1, 7):
                    bt, bp, bx = {}, {}, {}
                    for (n, cp, cs) in CH:
                        bt[n] = pring.next()
                        for h in range(4):
                            Pp, TPp = (B_N[cp][h] if k == 1 else B_P[(k - 1) % 2][cp][h])
                            PTp, TPTp = (B_NT[cp][h] if k == 1 else B_PT[(k - 1) % 2][cp][h])
                            c.op("pe", lambda e: e.matmul(bt[n][0][:, h, :], lhsT=Pp[:], rhs=PTp[:],
                                                          start=True, stop=True),
                                 reads=[TPp, TPTp], writes=[bt[n][1]])
                    for (n, cp, cs) in CH:
                        for h in range(4):
                            PTn, TPTn = B_PT[k % 2][cp][h]
                            evac_copy(c, "act", PTn[:], bt[n][0][:, h, :], [bt[n][1]], [TPTn])
                    if k < 6:
                        for (n, cp, cs) in CH:
                            bp[n] = pring.next()
                            for h in range(4):
                                Pp, TPp = (B_N[cp][h] if k == 1 else B_P[(k - 1) % 2][cp][h])
                                PTp, TPTp = (B_NT[cp][h] if k == 1 else B_PT[(k - 1) % 2][cp][h])
                                c.op("pe", lambda e: e.matmul(bp[n][0][:, h, :], lhsT=PTp[:], rhs=Pp[:],
                                                              start=True, stop=True),
                                     reads=[TPp, TPTp], writes=[bp[n][1]])
                        for (n, cp, cs) in CH:
                            for h in range(4):
                                Pn, TPn = B_P[k % 2][cp][h]
                                evac_copy(c, "dve", Pn[:], bp[n][0][:, h, :], [bp[n][1]], [TPn])
                    for (n, cp, cs) in CH:
                        bx[n] = pring.next()
                        for h in range(4):
                            PTn, TPTn = B_PT[k % 2][cp][h]
                            X_, TX_ = B_X[cp][h]
                            c.op("pe", lambda e: e.matmul(bx[n][0][:, h, :], lhsT=PTn[:], rhs=X_[:],
                                                          start=True, stop=True),
                                 reads=[TPTn, TX_], writes=[bx[n][1]])
                    for (n, cp, cs) in CH:
                        for h in range(4):
                            X_, TX_ = B_X[cp][h]
                            c.op("dve", lambda e: e.tensor_tensor(out=X_[:], in0=bx[n][0][:, h, :], in1=X_[:],
                                                                  op=ALU.add),
                                 reads=[bx[n][1], TX_], writes=[TX_])
                _chk(6)
                for (n, cp, cs) in CH:
                    bks[n] = (pring.next(), pring.next())
                    (bku, Tpu), (bkw, Tpw) = bks[n]
                    for h in range(4):
                        X_, TX_ = B_X[cp][h]
                        c.op("pe", lambda e: e.matmul(bku[:, h, :], lhsT=X_[:], rhs=B_vt[cp][h][0][:],
                                                      start=True, stop=True),
                             reads=[TX_, B_vt[cp][h][1]], writes=[Tpu])
                        c.op("pe", lambda e: e.matmul(bkw[:, h, :], lhsT=B_kg[cp][h][0][:], rhs=X_[:],
                                                      start=True, stop=True),
                             reads=[TX_, B_kg[cp][h][1]], writes=[Tpw])
                for (n, cp, cs) in CH:
                    (bku, Tpu), (bkw, Tpw) = bks[n]
                    for h in range(4):
                        u, Tu = B_u[cp][h]
                        w, Tw = B_w[cp][h]
                        evac_scale(c, "dve", u[:], bku[:, h, :], col("beta", n, h), [Tpu, Tsc], [Tu])
                        evac_copy(c, "act", w[:], bkw[:, h, :], [Tpw], [Tw])
                _chk(7)
                for (n, cp, cs) in CH:
                    gz, Tgz = gzp[cp]
                    mx, Tmx = mixs[cp]
                    for h in range(4):
                        c.op("pe", lambda e: e.matmul(ws_ps[h][0], lhsT=B_w[cp][h][0][:], rhs=Sb[h][0][:],
                                                      start=True, stop=True),
                             reads=[B_w[cp][h][1], Sb[h][1]], writes=[ws_ps[h][1]])
                    for h in range(4):
                        vn, Tvn = B_vn[cp][h]
                        c.op("dve", lambda e: e.scalar_tensor_tensor(out=vn[:], in0=ws_ps[h][0],
                                                                     scalar=col("negb", n, h),
                                                                     in1=B_u[cp][h][0][:], op0=ALU.mult, op1=ALU.add),
                             reads=[ws_ps[h][1], Tsc, B_u[cp][h][1]], writes=[Tvn])
                    for h in range(4):
                        vn, Tvn = B_vn[cp][h]
                        c.op("pe", lambda e: e.matmul(Sd_ps[h][0], lhsT=B_kd[cp][h][0][:], rhs=vn[:],
                                                      start=True, stop=True),
                             reads=[B_kd[cp][h][1], Tvn], writes=[Sd_ps[h][1]])
                    for h in range(4):
                        vn, Tvn = B_vn[cp][h]
                        c.op("pe", lambda e: e.matmul(O_ps[h][0], lhsT=B_qg[cp][h][0][:], rhs=Sb[h][0][:],
                                                      start=True, stop=False),
                             reads=[B_qg[cp][h][1], Sb[h][1]], writes=[O_ps[h][1]])
                        c.op("pe", lambda e: e.matmul(O_ps[h][0], lhsT=B_qk[cp][h][0][:], rhs=vn[:],
                                                      start=False, stop=True),
                             reads=[B_qk[cp][h][1], Tvn], writes=[O_ps[h][1]])
                    for h in range(4):
                        c.op("dve", lambda e: e.scalar_tensor_tensor(out=Sb[h][0][:], in0=S32[h][0][:],
                                                                     scalar=col("egl", n, h), in1=Sd_ps[h][0],
                                                                     op0=ALU.mult, op1=ALU.add),
                             reads=[S32[h][1], Tsc, Sd_ps[h][1]], writes=[Sb[h][1]])
                    for h in range(4):
                        c.op("dve", lambda e: e.scalar_tensor_tensor(out=S32[h][0][:], in0=S32[h][0][:],
                                                                     scalar=col("egl", n, h), in1=Sd_ps[h][0],
                                                                     op0=ALU.mult, op1=ALU.add),
                             reads=[S32[h][1], Tsc, Sd_ps[h][1]], writes=[S32[h][1]])
                    for h in range(4):
                        To = Tost[n][h]
                        c.op("act", lambda e: e.activation(out=junk[:], in_=O_ps[h][0], func=AF.Square,
                                                           accum_out=ost[:, n, h, 0:1]),
                             reads=[O_ps[h][1]], writes=[Tjunk, To])
                        rsqrt_chain(c, ost[:, n, h, 2:3], ost[:, n, h, 0:1], 1.0 / 128, epsb, ost[:, n, h, 1:2],
                                    [To, Tcst], [To], To)
                    for h in range(4):
                        To = Tost[n][h]
                        c.op("dve", lambda e: e.scalar_tensor_tensor(out=mx[:, h * 128:(h + 1) * 128],
                                                                     in0=O_ps[h][0], scalar=ost[:, n, h, 2:3],
                                                                     in1=gz[:, h * 128:(h + 1) * 128],
                                                                     op0=ALU.mult, op1=ALU.mult),
                             reads=[O_ps[h][1], To, Tgz], writes=[Tmx])
                    c.dma("pool", MIXH[n * 128:(n + 1) * 128, 512:1024], mx[:], reads=[Tmx], writes=[TMIXH])
                _chk(8)
        c.barrier()


def build_program(mode, layers=(0, 1), dbg_stages=None):
    nc = bass.Bass("TRN2", target_bir_lowering=False)
    dram = lambda n, s, d, kind: nc.dram_tensor(n, s, d, kind=kind).ap()
    IN, OUT, INT = "ExternalInput", "ExternalOutput", "Internal"
    SK = OUT if mode == "dbg" else INT
    G = {}
    need_in = {"fused": (0, 1), "p1": (0,), "p2": (1,), "p3": (), "dbg": tuple(layers)}[mode]
    need_out = {"fused": (0, 1), "p1": (), "p2": (0,), "p3": (1,), "dbg": tuple(layers)}[mode]
    with contextlib.ExitStack() as es:
        c = Ctx(nc, es)
        sb = lambda n, s, d: es.enter_context(nc.sbuf_tensor(n, s, d))
        win = {l: dram(f"win{l}", [128, 16 * NCOL], F32, IN) for l in need_in}
        wout = {l: dram(f"wout{l}", [128, 16, D], F32, IN) for l in need_out}
        G["Wbf"] = {l: dram(f"wbf{l}", [128, 16 * NCOL], BF16, INT) for l in need_in}
        G["TWbf"] = {l: [T() for _ in range(9)] for l in need_in}
        G["Wobf"] = {l: dram(f"wobf{l}", [128, 16, D], BF16, INT) for l in need_out}
        G["TWobf"] = {l: [T() for _ in range(16)] for l in need_out}
        gpk_d = dram("gpk_d", [DEPTH, 128, 16], F32, IN)
        cw_d = dram("cw_d", [DEPTH, 128, 12, 4], F32, IN)
        hpar_d = dram("hpar_d", [DEPTH, 128, 8], F32, IN)
        G["gnw_d"] = dram("gnw_d", [DEPTH, 128], F32, IN)
        G["subw_d"] = dram("subw_d", [DEPTH, 128], F32, IN)
        G["lam_d"] = dram("lam_d", [DEPTH, 4, 64], F32, IN)
        G["fnw_d"] = dram("fnw_d", [1, D], F32, IN)
        G["braw_d"] = dram("braw_d", [4, 5, 128, 512], F32, IN)
        G["mpat_d"] = dram("mpat_d", [5, 128, 512], F32, IN)
        cfar_d = dram("cfar_d", [128, 4], F32, IN)
        cmat_d = dram("cmat_d", [6, 128, 128], F32, IN)
        if mode == "dbg" and "inproj" not in dbg_stages:
            SK = IN
        G["QKT"] = (dram("s_qkt", [8, 128, SEQ], BF16, SK), T())
        G["GQKV"] = (dram("s_gqkv", [12, 128, SEQ], F32, SK), T())
        G["DAV"] = (dram("s_dav", [SEQ, 512], BF16, SK), T())
        G["DAG"] = (dram("s_dag", [SEQ, 512], F32, SK), T())
        G["GZ"] = (dram("s_gz", [SEQ, 512], F32, SK), T())
        G["GBA"] = (dram("s_gba", [SEQ, 8], F32, SK), T())
        cst = sb("cst", [128, 4], F32)
        Tcst = T()
        c.op("pool", lambda e: e.memset(cst[:, 0:1], RMS_EPS), writes=[Tcst])
        c.op("pool", lambda e: e.memset(cst[:, 1:2], 0.0), writes=[Tcst])
        c.op("pool", lambda e: e.memset(cst[:, 2:3], 1.0), writes=[Tcst])
        G["cst"] = (cst, Tcst)
        cm = sb("cmat", [128, 6, 128], F32)
        Tcm = T()
        c.dma("sp", cm[:], cmat_d.rearrange("a p q -> p a q"), writes=[Tcm])
        G["identf"] = (cm[:, 0, :], Tcm)
        G["Uf"] = (cm[:, 1, :], Tcm)
        G["onesf"] = (cm[:, 2, :], Tcm)
        G["negonesf"] = (cm[:, 3, :], Tcm)
        G["negmask"] = (cm[:, 4, :], Tcm)
        G["strict"] = (cm[:, 5, :], Tcm)
        identb = sb("identb", [128, 128], BF16)
        onesb = sb("onesb", [128, 128], BF16)
        Tib, Tob = T(), T()
        c.op("dve", lambda e: e.tensor_copy(out=identb[:], in_=cm[:, 0, :]), reads=[Tcm], writes=[Tib])
        c.op("dve", lambda e: e.tensor_copy(out=onesb[:], in_=cm[:, 2, :]), reads=[Tcm], writes=[Tob])
        G["identb"] = (identb, Tib)
        G["onesb"] = (onesb, Tob)
        cfar = sb("cfar", [128, 4], F32)
        Tcfar = T()
        c.dma("sp", cfar[:], cfar_d, writes=[Tcfar])
        G["cfar"] = (cfar, Tcfar)
        G["gpk"], G["cw"], G["hpar"] = {}, {}, {}
        for l in range(DEPTH):
            t1 = sb(f"gpk{l}", [128, 16], F32)
            t2 = sb(f"cw{l}", [128, 12, 4], F32)
            t3 = sb(f"hpar{l}", [128, 8], F32)
            T1, T2, T3 = T(), T(), T()
            c.dma("sp", t1[:], gpk_d[l], writes=[T1])
            c.dma("sp", t2[:], cw_d[l], writes=[T2])
            c.dma("sp", t3[:], hpar_d[l], writes=[T3])
            G["gpk"][l], G["cw"][l], G["hpar"][l] = (t1, T1), (t2, T2), (t3, T3)
        def cast_in(l, cb):
            a, b = (cb * 8192, (cb + 1) * 8192) if cb < 8 else (65536, 16 * NCOL)
            return lambda: c.dma("pool", G["Wbf"][l][:, a:b], win[l][:, a:b], writes=[G["TWbf"][l][cb]])

        def cast_out(l, k):
            return lambda: c.dma("pool", G["Wobf"][l][:, k, :], wout[l][:, k, :], writes=[G["TWobf"][l][k]])
        G["pending"] = []
        if mode == "fused":
            for k in range(9):
                cast_in(0, k)()
        else:
            for l in need_in:
                for k in range(9):
                    cast_in(l, k)()
            for l in need_out:
                for k in range(16):
                    cast_out(l, k)()

        if mode == "dbg":
            l = layers[0]
            x_in = dram("x_in", [SEQ, D], F32, IN)
            G["MIXH"] = (dram("mixh", [SEQ, 1024], BF16, OUT), T())
            mixf = dram("mixf_in", [2, SEQ, 1024], BF16, IN)
            x1 = dram("x1", [SEQ, D], F32, OUT)
            if "inproj" in dbg_stages:
                stage_inproj(c, nc, G, l, x_in, T())
            if "da" in dbg_stages:
                stage_da(c, nc, G, l)
            if "gdn" in dbg_stages:
                stage_gdn(c, nc, G, l)
            if "outproj" in dbg_stages:
                stage_outproj(c, nc, G, l, x_in, T(), mixf, T(), x1, T(), 0, NT, False)
        elif mode == "p1":
            x_in = dram("x_in", [SEQ, D], F32, IN)
            G["MIXH"] = (dram("mixh", [SEQ, 1024], BF16, OUT), T())
            stage_inproj(c, nc, G, 0, x_in, T())
            stage_da(c, nc, G, 0)
            stage_gdn(c, nc, G, 0)
        elif mode == "p2":
            x_in = dram("x_in", [SEQ, D], F32, IN)
            mixf = dram("mixf_in", [2, SEQ, 1024], BF16, IN)
            x1 = dram("x1", [SEQ, D], F32, OUT)
            Tx1 = T()
            G["MIXH"] = (dram("mixh", [SEQ, 1024], BF16, OUT), T())
            stage_outproj(c, nc, G, 0, x_in, T(), mixf, T(), x1, Tx1, 0, NT, False)
            stage_inproj(c, nc, G, 1, x1, Tx1)
            stage_da(c, nc, G, 1)
            stage_gdn(c, nc, G, 1)
        elif mode == "p3":
            x_in = dram("x_in", [SEQ // 2, D], F32, IN)
            mixf = dram("mixf_in", [2, SEQ // 2, 1024], BF16, IN)
            out = dram("out", [SEQ // 2, D], F32, OUT)
            stage_outproj(c, nc, G, 1, x_in, T(), mixf, T(), out, T(), 0, NT // 2, True)
        elif mode == "fused":
            x_in = dram("x_in", [SEQ, D], F32, IN)
            out = dram("out", [SEQ, D], F32, OUT)
            G["MIXH"] = (dram("s_mixh", [SEQ, 1024], BF16, INT), T())
            mixf4, Tmixf = dram("s_mixf", [4, 2, 1024, 1024], BF16, INT), T()
            mixf = lambda r, tok: mixf4[tok // 1024, r, tok % 1024:tok % 1024 + 128, :]
            x1, Tx1 = dram("s_x1", [SEQ, D], F32, INT), T()
            Tx0 = T()
            groups = [[2 * i, 2 * i + 1] for i in range(NB)]
            for l in range(DEPTH):
                src, Tsrc = (x_in, Tx0) if l == 0 else (x1, Tx1)
                stage_inproj(c, nc, G, l, src, Tsrc)
                G["pending"] += [cast_out(l, k) for k in range(16)]
                if l + 1 < DEPTH:
                    G["pending"] += [cast_in(l + 1, k) for k in range(9)]
                stage_da(c, nc, G, l)
                stage_gdn(c, nc, G, l)
                for ch in range(4):
                    c.collective(lambda e: e.collective_compute(
                        "AllGather", ALU.bypass, replica_groups=groups,
                        ins=[G["MIXH"][0][ch * 1024:(ch + 1) * 1024, :]],
                        outs=[mixf4[ch].rearrange("r p c -> (r p) c")]),
                        reads=[G["MIXH"][1]], writes=[Tmixf])
                if l == 0:
                    stage_outproj(c, nc, G, l, x_in, Tx0, mixf, Tmixf, x1, Tx1, 0, NT, False)
                else:
                    stage_outproj(c, nc, G, l, x1, Tx1, mixf, Tmixf, out, T(), 0, NT, True)
        c.finish("sp")
        print(f"[build {mode}] instructions={c.n_ins} waits={c.n_wait} sems={len(c.sems)}")
    return nc


def _bucket_np(dist):
    n = np.maximum(dist, 0)
    max_exact = 16
    large = max_exact + (np.log(np.maximum(n, max_exact).astype(np.float32) / np.float32(max_exact))
                         / np.float32(math.log(128 / max_exact)) * np.float32(32 - max_exact)).astype(np.int32)
    large = np.minimum(large, 31)
    return np.where(n < max_exact, n, large)


def _consts():
    j = np.arange(128)[:, None]
    i = np.arange(128)[None, :]
    cm = np.zeros((6, 128, 128), np.float32)
    cm[0] = np.eye(128)
    cm[1] = (j <= i)
    cm[2] = 1.0
    cm[3] = -1.0
    cm[4] = np.where(i < j, NEG, 0.0)
    cm[5] = (i > j)
    k = np.arange(128)[:, None]
    q = np.arange(512)[None, :]
    dists = [q - k - (a * 128 - 128) for a in range(5)]
    mpat = np.stack([np.where(d < 0, NEG, 0.0) for d in dists]).astype(np.float32)
    bidx = np.stack([_bucket_np(d) for d in dists])
    return cm, mpat, bidx


def prep_core_inputs(inp, b, hh):
    cm, mpat, bidx = _consts()
    H = np.arange(4 * hh, 4 * hh + 4)
    cols = np.concatenate([
        hh * 512 + np.arange(512), 1024 + hh * 512 + np.arange(512),
        4096 + hh * 512 + np.arange(512), 5120 + hh * 512 + np.arange(512), 6144 + hh * 512 + np.arange(512),
        2048 + hh * 512 + np.arange(512), 3072 + hh * 512 + np.arange(512), 7168 + hh * 512 + np.arange(512),
        8192 + hh * 4 + np.arange(4), 8200 + hh * 4 + np.arange(4)])
    rows = np.concatenate([np.concatenate([r * 512 + np.arange(512), 1024 + r * 512 + np.arange(512)])
                           for r in range(2)])
    m = {}
    for l in range(DEPTH):
        wpk = inp["w_in"][l][:, cols].reshape(16, 128, NCOL).transpose(1, 0, 2)
        m[f"win{l}"] = np.ascontiguousarray(np.concatenate(
            [wpk[:, :, cb * 512:(cb + 1) * 512].reshape(128, 8192) for cb in range(8)]
            + [wpk[:, :, 4096:NCOL].reshape(128, 128)], axis=1))
        m[f"wout{l}"] = np.ascontiguousarray(
            inp["w_out"][l][rows, :].reshape(16, 128, D).transpose(1, 0, 2))
    m["gpk_d"] = np.ascontiguousarray(inp["norm_w"].reshape(DEPTH, 16, 128).transpose(0, 2, 1))
    ch = np.stack([t * 1024 + (4 * hh + hl) * 128 + np.arange(128) for t in range(3) for hl in range(4)])
    m["cw_d"] = np.ascontiguousarray(inp["conv_w"][:, :, ch].transpose(0, 3, 2, 1))
    hp = np.concatenate([inp["a_log"][:, H], inp["dt_bias"][:, H]], axis=1)
    m["hpar_d"] = np.ascontiguousarray(np.broadcast_to(hp[:, None, :], (DEPTH, 128, 8)))
    m["gnw_d"] = np.ascontiguousarray(inp["gdn_norm_w"])
    m["subw_d"] = np.ascontiguousarray(inp["da_subln_w"])
    m["lam_d"] = np.ascontiguousarray(np.stack([inp["lambda_q1"], inp["lambda_k1"],
                                                inp["lambda_q2"], inp["lambda_k2"]], axis=1))
    m["fnw_d"] = np.ascontiguousarray(inp["final_norm_w"].reshape(1, D))
    rb = inp["rel_bias"]
    m["braw_d"] = np.ascontiguousarray(np.stack([rb[:, h][bidx] for h in H]).astype(np.float32))
    m["mpat_d"] = mpat
    m["cfar_d"] = np.ascontiguousarray(np.broadcast_to(rb[31, H][None, :], (128, 4)))
    m["cmat_d"] = cm
    return {k: np.asarray(v, dtype=np.float32) if v.dtype != np.float32 else v for k, v in m.items()}


_PROGS = {}


def _prog(mode):
    if mode not in _PROGS:
        _PROGS[mode] = build_program(mode)
    return _PROGS[mode]


_COMMON = ["gpk_d", "cw_d", "hpar_d", "gnw_d", "subw_d", "lam_d", "fnw_d", "braw_d", "mpat_d", "cfar_d", "cmat_d"]
FUSED = True


def kernel(**inputs):
    inp = {k: np.asarray(v) for k, v in inputs.items()}
    x = np.ascontiguousarray(inp["x"], dtype=np.float32)
    cores = [(b, hh) for b in range(NB) for hh in range(2)]
    prep = [prep_core_inputs(inp, b, hh) for (b, hh) in cores]
    ids = list(range(8))
    out = np.empty((NB, SEQ, D), np.float32)
    if FUSED:
        maps = []
        for ci, (b, hh) in enumerate(cores):
            m = {k: prep[ci][k] for k in _COMMON}
            for l in range(DEPTH):
                m[f"win{l}"] = prep[ci][f"win{l}"]
                m[f"wout{l}"] = prep[ci][f"wout{l}"]
            m["x_in"] = x[b]
            maps.append(m)
        res = run_bass_kernel_spmd(_prog("fused"), maps, core_ids=ids).results
        for ci, (b, hh) in enumerate(cores):
            out[b, hh * 2048:(hh + 1) * 2048] = np.asarray(res[ci]["out"])[hh * 2048:(hh + 1) * 2048]
        return out
    maps = []
    for ci, (b, hh) in enumerate(cores):
        m = {k: prep[ci][k] for k in _COMMON}
        m["win0"] = prep[ci]["win0"]
        m["x_in"] = x[b]
        maps.append(m)
    r1 = run_bass_kernel_spmd(_prog("p1"), maps, core_ids=ids).results
    mixf0 = [np.stack([np.asarray(r1[2 * b]["mixh"]), np.asarray(r1[2 * b + 1]["mixh"])]) for b in range(NB)]
    maps = []
    for ci, (b, hh) in enumerate(cores):
        m = {k: prep[ci][k] for k in _COMMON}
        m["win1"] = prep[ci]["win1"]
        m["wout0"] = prep[ci]["wout0"]
        m["x_in"] = x[b]
        m["mixf_in"] = mixf0[b]
        maps.append(m)
    r2 = run_bass_kernel_spmd(_prog("p2"), maps, core_ids=ids).results
    mixf1 = [np.stack([np.asarray(r2[2 * b]["mixh"]), np.asarray(r2[2 * b + 1]["mixh"])]) for b in range(NB)]
    maps = []
    for ci, (b, hh) in enumerate(cores):
        m = {k: prep[ci][k] for k in _COMMON}
        m["wout1"] = prep[ci]["wout1"]
        m["x_in"] = np.ascontiguousarray(np.asarray(r2[ci]["x1"])[hh * 2048:(hh + 1) * 2048])
        m["mixf_in"] = np.ascontiguousarray(mixf1[b][:, hh * 2048:(hh + 1) * 2048, :])
        maps.append(m)
    r3 = run_bass_kernel_spmd(_prog("p3"), maps, core_ids=ids).results
    for ci, (b, hh) in enumerate(cores):
        out[b, hh * 2048:(hh + 1) * 2048] = np.asarray(r3[ci]["out"])
    return out
```
